# Optimizing a Trainium2 kernel written in Bass

```python
import math
import jax, jax.numpy as jnp
from jax import lax
import numpy as np

D_MODEL = 1024
BATCH = 16
SEQ = 256
DEPTH = 2
DEC_BATCH = 2
DEC_SEQ = 4096
PAST_LEN = 256

GRID_W = 64
W_CONV = 512
CONV_K = 31
W_HYENA = 512
HYENA_ORDER = 2
SHORT_K = 3
FILTER_BANDS = 16
FILTER_EMB = 1 + 2 * FILTER_BANDS
FILTER_HIDDEN = 64
DECAY_TARGET = 1e-2
FAST_DECAY_PCT = 0.3
SLOW_DECAY_PCT = 1.5
W_POOL = 512
POOL_WINDOWS = (2, 4, 8, 16)
POOL_GROUP = W_POOL // len(POOL_WINDOWS)
HEAD_DIM = 64
N_Q_HEADS = 16
N_KV_HEADS = 4
Q_PER_KV = N_Q_HEADS // N_KV_HEADS
W_ATTN = N_Q_HEADS * HEAD_DIM
W_KV = N_KV_HEADS * HEAD_DIM
ROPE_THETA = 10000.0
Q_BLOCK = 128

N_BRANCH = 4
EPS = 1e-6

SPLITS = (
    2 * W_CONV, W_CONV,
    (HYENA_ORDER + 1) * W_HYENA, W_HYENA,
    W_POOL, W_POOL,
    W_ATTN, W_KV, W_KV, W_ATTN,
    N_BRANCH * D_MODEL,
)
N_IN = sum(SPLITS)

kernel_name = "hybrid_flow_backbone_step"


def rms_norm(x, g):
    xf = x.astype(jnp.float32)
    y = xf * lax.rsqrt(jnp.mean(xf * xf, axis=-1, keepdims=True) + EPS)
    return (y * g.astype(jnp.float32)).astype(x.dtype)


def layer_norm(x, g, b):
    xf = x.astype(jnp.float32)
    mu = jnp.mean(xf, axis=-1, keepdims=True)
    xc = xf - mu
    y = xc * lax.rsqrt(jnp.mean(xc * xc, axis=-1, keepdims=True) + EPS)
    return (y * g.astype(jnp.float32) + b.astype(jnp.float32)).astype(x.dtype)


def depthwise_conv(x, w, b):
    k = w.shape[0]
    y = lax.conv_general_dilated(
        x, w[:, None, :], window_strides=(1,), padding=[(k // 2, k // 2)],
        dimension_numbers=('NWC', 'WIO', 'NWC'), feature_group_count=x.shape[-1])
    return y + b


def conformer_conv(u_glu, dw_w, dw_b, ln_g, ln_b, pw):
    val, gate = jnp.split(u_glu, 2, axis=-1)
    a = val * jax.nn.sigmoid(gate)
    a = depthwise_conv(a, dw_w, dw_b)
    a = jax.nn.silu(layer_norm(a, ln_g, ln_b))
    return a @ pw


def hyena_filters(L, w1, b1, freq, w2, b2, w3, b3):
    f32 = jnp.float32
    t = jnp.linspace(0.0, 1.0, L, dtype=f32)[:, None]
    bands = jnp.linspace(1e-4, FILTER_BANDS - 1, FILTER_BANDS, dtype=f32)[None, :]
    w = (2.0 * math.pi / L) * jnp.arange(L, dtype=f32)[:, None]
    z = jnp.concatenate([t, jnp.cos(bands * w), jnp.sin(bands * w)], axis=-1)
    fr = freq.astype(f32)
    hdn = jnp.sin(fr * (z @ w1.astype(f32) + b1.astype(f32)))
    hdn = jnp.sin(fr * (hdn @ w2.astype(f32) + b2.astype(f32)))
    h = hdn @ w3.astype(f32) + b3.astype(f32)
    h = h.reshape(L, 2, HYENA_ORDER, W_HYENA)
    max_decay = math.log(DECAY_TARGET) / FAST_DECAY_PCT
    min_decay = math.log(DECAY_TARGET) / SLOW_DECAY_PCT
    deltas = jnp.linspace(min_decay, max_decay, W_HYENA, dtype=f32)
    h = h * jnp.exp(-t * jnp.abs(deltas))[:, None, None, :]
    fwd, bwd = h[:, 0], h[:, 1]
    filt = jnp.concatenate(
        [fwd, jnp.zeros((1, HYENA_ORDER, W_HYENA), f32), bwd[1:][::-1]], axis=0)
    return filt * lax.rsqrt(jnp.sum(filt * filt, axis=0, keepdims=True) + EPS)


def hyena(u_proj, short_w, short_b, filt, skip):
    L = u_proj.shape[1]
    n = 2 * L
    uc = depthwise_conv(u_proj, short_w, short_b)
    v, x1, x2 = jnp.split(uc, 3, axis=-1)
    z = v
    for o, gate in enumerate((x1, x2)):
        zf32 = z.astype(jnp.float32)
        zf = jnp.fft.rfft(zf32, n=n, axis=1)
        hf = jnp.fft.rfft(filt[:, o], n=n, axis=0)
        y = jnp.fft.irfft(zf * hf[None], n=n, axis=1)[:, :L]
        z = gate * (y + zf32 * skip[o].astype(jnp.float32)).astype(gate.dtype)
    return z


def multiscale_pool(x, w, scale):
    B, L, _ = x.shape
    t = jnp.arange(L)
    xf = x.astype(jnp.float32)
    outs = []
    for g, win in enumerate(POOL_WINDOWS):
        xg = xf[..., g * POOL_GROUP:(g + 1) * POOL_GROUP]
        cs = jnp.concatenate(
            [jnp.zeros((B, 1, POOL_GROUP), jnp.float32), jnp.cumsum(xg, axis=1)], axis=1)
        lo = jnp.clip(t - win // 2, 0, L)
        hi = jnp.clip(t + win // 2, 0, L)
        cnt = (hi - lo).astype(jnp.float32)
        mean = (cs[:, hi] - cs[:, lo]) / cnt[None, :, None]
        outs.append(mean - xg)
    pooled = jnp.stack(outs, axis=2)
    y = jnp.einsum('blgc,gcd->blgd', pooled, w.astype(jnp.float32)).reshape(B, L, W_POOL)
    return (y * scale.astype(jnp.float32)).astype(x.dtype)


def rope_axis(x, pos):
    f = x.shape[-1] // 2
    inv = ROPE_THETA ** (-jnp.arange(f, dtype=jnp.float32) / f)
    ang = pos[:, None] * inv[None, :]
    cos = jnp.cos(ang)[None, :, None, :]
    sin = jnp.sin(ang)[None, :, None, :]
    x1, x2 = x[..., :f], x[..., f:]
    return jnp.concatenate([x1 * cos - x2 * sin, x2 * cos + x1 * sin], axis=-1)


def rope_2d(x):
    L = x.shape[1]
    rows = L // GRID_W
    row = jnp.repeat(jnp.arange(rows), GRID_W).astype(jnp.float32)
    col = jnp.tile(jnp.arange(GRID_W), rows).astype(jnp.float32)
    xf = x.astype(jnp.float32)
    half = HEAD_DIM // 2
    out = jnp.concatenate([rope_axis(xf[..., :half], row), rope_axis(xf[..., half:], col)], axis=-1)
    return out.astype(x.dtype)


def block_attention(q, k, v):
    B, Lq = q.shape[0], q.shape[1]
    nb = Lq // Q_BLOCK
    qb = q.reshape(B, nb, Q_BLOCK, N_KV_HEADS, Q_PER_KV, HEAD_DIM).transpose(1, 0, 2, 3, 4, 5)
    scale = HEAD_DIM ** -0.5

    def one_block(qi):
        s = jnp.einsum('bqkgd,bskd->bkgqs', qi, k, preferred_element_type=jnp.float32) * scale
        p = jax.nn.softmax(s, axis=-1)
        return jnp.einsum('bkgqs,bskd->bqkgd', p.astype(v.dtype), v)

    o = lax.map(one_block, qb)
    return o.transpose(1, 0, 2, 3, 4, 5).reshape(B, Lq, W_ATTN)


def parallel_mixer(h, p, latent, k_ctx, v_ctx):
    B, L, _ = h.shape
    u = h @ p['w_in']
    points = [int(s) for s in np.cumsum(SPLITS)[:-1]]
    (a_glu, a_gate, b_proj, b_gate, c_in, c_gate,
     q, k, v, d_gate, g_merge) = jnp.split(u, points, axis=-1)

    ya = conformer_conv(a_glu, p['conv_dw_w'], p['conv_dw_b'], p['conv_ln_g'],
                        p['conv_ln_b'], p['conv_pw'])
    ya = (ya * jax.nn.silu(a_gate)) @ p['wo_conv']

    filt = hyena_filters(L, p['hy_w1'], p['hy_b1'], p['hy_freq'], p['hy_w2'], p['hy_b2'],
                         p['hy_w3'], p['hy_b3'])
    yb = hyena(b_proj, p['hy_short_w'], p['hy_short_b'], filt, p['hy_skip'])
    yb = (yb * jax.nn.silu(b_gate)) @ p['wo_hyena']

    yc = multiscale_pool(c_in, p['pool_w'], p['pool_scale'])
    yc = (yc * jax.nn.silu(c_gate)) @ p['wo_pool']

    q = rms_norm(q.reshape(B, L, N_Q_HEADS, HEAD_DIM), p['q_norm'])
    k = rms_norm(k.reshape(B, L, N_KV_HEADS, HEAD_DIM), p['k_norm'])
    v = v.reshape(B, L, N_KV_HEADS, HEAD_DIM)
    if latent:
        q_r, k_r = rope_2d(q), rope_2d(k)
        k_all = jnp.concatenate([k_ctx, k_r], axis=1)
        v_all = jnp.concatenate([v_ctx, v], axis=1)
    else:
        q_r = q
        k_all, v_all = k, v
    att = block_attention(q_r.reshape(B, L, N_KV_HEADS, Q_PER_KV, HEAD_DIM), k_all, v_all)
    yd = (att * jax.nn.silu(d_gate)) @ p['wo_attn']

    gm = jax.nn.sigmoid(g_merge).reshape(B, L, N_BRANCH, D_MODEL)
    merged = gm[:, :, 0] * ya + gm[:, :, 1] * yb + gm[:, :, 2] * yc + gm[:, :, 3] * yd
    return merged @ p['w_out'], k, v


def trunk_layer(x, cond, p, latent, k_ctx, v_ctx):
    mod = jax.nn.silu(cond) @ p['w_ada'] + p['b_ada']
    shift, scale, gate = jnp.split(mod, 3, axis=-1)
    h = rms_norm(x, p['norm_g']) * (1.0 + scale) + shift
    y, k, v = parallel_mixer(h, p, latent, k_ctx, v_ctx)
    return x + gate * y, k, v


def setup_inputs(seed: int = 0) -> dict:
    key = jax.random.key(seed)
    ks = iter(jax.random.split(key, 64))
    f32 = jnp.float32

    def nrm(shape, scale):
        return jax.random.normal(next(ks), shape, f32) * scale

    def gain(shape, s=0.05):
        return 1.0 + nrm(shape, s)

    D = D_MODEL
    return {
        'x_prompt': nrm((BATCH, SEQ, D), 1.0),
        'x_sample': nrm((DEC_BATCH, DEC_SEQ, D), 1.0),
        'cache_k': nrm((DEC_BATCH, DEPTH, PAST_LEN, N_KV_HEADS, HEAD_DIM), 1.0),
        'cache_v': nrm((DEC_BATCH, DEPTH, PAST_LEN, N_KV_HEADS, HEAD_DIM), 1.0),
        'c': nrm((DEC_BATCH, D), 1.0),
        'c_ctx': nrm((D,), 1.0),
        'w_ada': nrm((DEPTH, D, 3 * D), 0.5 * D ** -0.5),
        'b_ada': nrm((DEPTH, 3 * D), 0.01),
        'norm_g': gain((DEPTH, D)),
        'w_in': nrm((DEPTH, D, N_IN), D ** -0.5),
        'conv_dw_w': nrm((DEPTH, CONV_K, W_CONV), CONV_K ** -0.5),
        'conv_dw_b': nrm((DEPTH, W_CONV), 0.01),
        'conv_ln_g': gain((DEPTH, W_CONV)),
        'conv_ln_b': nrm((DEPTH, W_CONV), 0.01),
        'conv_pw': nrm((DEPTH, W_CONV, W_CONV), W_CONV ** -0.5),
        'hy_short_w': nrm((DEPTH, SHORT_K, (HYENA_ORDER + 1) * W_HYENA), SHORT_K ** -0.5),
        'hy_short_b': nrm((DEPTH, (HYENA_ORDER + 1) * W_HYENA), 0.01),
        'hy_w1': nrm((DEPTH, FILTER_EMB, FILTER_HIDDEN), FILTER_EMB ** -0.5),
        'hy_b1': nrm((DEPTH, FILTER_HIDDEN), 0.1),
        'hy_freq': gain((DEPTH, FILTER_HIDDEN), 0.1),
        'hy_w2': nrm((DEPTH, FILTER_HIDDEN, FILTER_HIDDEN), FILTER_HIDDEN ** -0.5),
        'hy_b2': nrm((DEPTH, FILTER_HIDDEN), 0.1),
        'hy_w3': nrm((DEPTH, FILTER_HIDDEN, 2 * HYENA_ORDER * W_HYENA), FILTER_HIDDEN ** -0.5),
        'hy_b3': nrm((DEPTH, 2 * HYENA_ORDER * W_HYENA), 0.01),
        'hy_skip': nrm((DEPTH, HYENA_ORDER, W_HYENA), 0.5),
        'pool_w': nrm((DEPTH, len(POOL_WINDOWS), POOL_GROUP, POOL_GROUP), POOL_GROUP ** -0.5),
        'pool_scale': gain((DEPTH, W_POOL), 0.1),
        'q_norm': gain((DEPTH, HEAD_DIM)),
        'k_norm': gain((DEPTH, HEAD_DIM)),
        'wo_conv': nrm((DEPTH, W_CONV, D), W_CONV ** -0.5),
        'wo_hyena': nrm((DEPTH, W_HYENA, D), W_HYENA ** -0.5),
        'wo_pool': nrm((DEPTH, W_POOL, D), W_POOL ** -0.5),
        'wo_attn': nrm((DEPTH, W_ATTN, D), W_ATTN ** -0.5),
        'w_out': nrm((DEPTH, D, D), D ** -0.5),
    }


def reference(x_prompt, x_sample, cache_k, cache_v, c, c_ctx, w_ada, b_ada, norm_g, w_in,
              conv_dw_w, conv_dw_b, conv_ln_g, conv_ln_b, conv_pw,
              hy_short_w, hy_short_b, hy_w1, hy_b1, hy_freq, hy_w2, hy_b2, hy_w3, hy_b3, hy_skip,
              pool_w, pool_scale, q_norm, k_norm,
              wo_conv, wo_hyena, wo_pool, wo_attn, w_out):
    stacked = {
        'w_ada': w_ada, 'b_ada': b_ada, 'norm_g': norm_g, 'w_in': w_in,
        'conv_dw_w': conv_dw_w, 'conv_dw_b': conv_dw_b, 'conv_ln_g': conv_ln_g,
        'conv_ln_b': conv_ln_b, 'conv_pw': conv_pw,
        'hy_short_w': hy_short_w, 'hy_short_b': hy_short_b, 'hy_w1': hy_w1, 'hy_b1': hy_b1,
        'hy_freq': hy_freq, 'hy_w2': hy_w2, 'hy_b2': hy_b2, 'hy_w3': hy_w3, 'hy_b3': hy_b3,
        'hy_skip': hy_skip, 'pool_w': pool_w, 'pool_scale': pool_scale,
        'q_norm': q_norm, 'k_norm': k_norm,
        'wo_conv': wo_conv, 'wo_hyena': wo_hyena, 'wo_pool': wo_pool, 'wo_attn': wo_attn,
        'w_out': w_out,
    }

    y_prompt = x_prompt
    cond_ctx = c_ctx[None, None, :]
    ks, vs = [], []
    for l in range(DEPTH):
        p = {name: arr[l] for name, arr in stacked.items()}
        y_prompt, k_l, v_l = trunk_layer(y_prompt, cond_ctx, p, False, None, None)
        ks.append(k_l)
        vs.append(v_l)
    new_cache_k = jnp.stack(ks, axis=1)
    new_cache_v = jnp.stack(vs, axis=1)

    y_sample = x_sample
    cond = c[:, None, :]
    for l in range(DEPTH):
        p = {name: arr[l] for name, arr in stacked.items()}
        y_sample, _, _ = trunk_layer(y_sample, cond, p, True, cache_k[:, l], cache_v[:, l])

    return (y_prompt, y_sample, new_cache_k, new_cache_v)
```

```python
import os, math, contextlib
import numpy as np
import ml_dtypes
import concourse.bass as bass
import concourse.mybir as mybir
from concourse.bass_utils import run_bass_kernel_spmd

F32 = mybir.dt.float32; BF16 = mybir.dt.bfloat16; I32 = mybir.dt.int32
AF = mybir.ActivationFunctionType; ALU = mybir.AluOpType
NPBF = ml_dtypes.bfloat16

D = 1024; NIN = 11264; DEPTH = 2; LP = 256; LS = 4096; PAST = 256
EPS = 1e-6
TWO_PI = 2.0 * math.pi

SAME_ENGINE_SYNC = os.environ.get('K_SES', '1') == '1'
STORES_ON_POOL = os.environ.get('K_SOP', '1') == '1'
class Reg:
    __slots__ = ("name", "w", "r", "dent")
    def __init__(self, name=""):
        self.name = name; self.w = None; self.r = []; self.dent = None

class Eng:
    def __init__(self, name, h):
        self.name = name; self.h = h; self.sem = None; self.cnt = 0; self.seen = {}

class FW:
    def __init__(self, nc, es, ndma=70):
        self.nc = nc; self.es = es
        self.pe = Eng("pe", nc.tensor); self.act = Eng("act", nc.scalar)
        self.dve = Eng("dve", nc.vector); self.pool = Eng("pool", nc.gpsimd); self.sp = Eng("sp", nc.sync)
        self.nsem = 0
        for e in (self.pe, self.act, self.dve, self.pool):
            e.sem = self.alloc_sem(e.name)
        self.dpool = [[self.alloc_sem("d%d" % i), 0] for i in range(ndma)]
        self.dfree = list(range(ndma)); self.dused = []
        self.nops = 0
    def alloc_sem(self, name):
        self.nsem += 1
        return self.es.enter_context(self.nc.semaphore("s%d_%s" % (self.nsem, name)))
    def _deps(self, eng, reads, writes):
        best = {}
        def add(t):
            k = id(t[0])
            if k not in best or best[k][1] < t[1]: best[k] = t
        for r in reads:
            if r.w is not None: add(r.w)
        for w in writes:
            if w.w is not None: add(w.w)
            for t in w.r: add(t)
        for k, (sem, val) in best.items():
            if sem is eng.sem and (eng is self.pe or not SAME_ENGINE_SYNC): continue
            if eng.seen.get(k, 0) >= val: continue
            eng.h.wait_ge(sem, val); eng.seen[k] = val
    def _mark(self, tok, reads, writes):
        for w in writes: w.w = tok; w.r = []
        for r in reads:
            if not any(r is w for w in writes): r.r.append(tok)
    def op(self, eng, fn, reads=(), writes=()):
        self._deps(eng, reads, writes)
        ins = fn(eng.h)
        ins.then_inc(eng.sem, 1); eng.cnt += 1; self.nops += 1
        tok = (eng.sem, eng.cnt)
        self._mark(tok, reads, writes)
        return tok
    def dma(self, q, out, in_, reads=(), writes=(), **kw):
        is_store = 'DRam' in type(out.tensor).__name__
        if is_store and q is self.sp and STORES_ON_POOL:
            q = self.pool
        self._deps(q, reads, writes)
        prim = reads[0] if (is_store and len(reads)) else (writes[0] if len(writes) else reads[0])
        if prim.dent is None:
            if not self.dfree: raise RuntimeError("out of dma sems")
            prim.dent = self.dfree.pop(); self.dused.append(prim)
        ent = self.dpool[prim.dent]
        ins = q.h.dma_start(out=out, in_=in_, **kw)
        ins.then_inc(ent[0], 16); ent[1] += 16; self.nops += 1
        tok = (ent[0], ent[1])
        self._mark(tok, reads, writes)
        return tok
    def idma(self, q, out, in_, idx_ap, reads=(), writes=()):
        self._deps(q, reads, writes)
        prim = writes[0]
        if prim.dent is None:
            if not self.dfree: raise RuntimeError("out of dma sems")
            prim.dent = self.dfree.pop(); self.dused.append(prim)
        ent = self.dpool[prim.dent]
        ins = q.h.indirect_dma_start(out=out, out_offset=None, in_=in_, in_offset=bass.IndirectOffsetOnAxis(ap=idx_ap, axis=0))
        ins.then_inc(ent[0], 16); ent[1] += 16; self.nops += 1
        tok = (ent[0], ent[1])
        self._mark(tok, reads, writes)
        return tok
    def all_tokens(self):
        toks = []
        for e in (self.pe, self.act, self.dve, self.pool):
            if e.cnt > 0: toks.append((e.sem, e.cnt))
        for i, (s, c) in enumerate(self.dpool):
            if c > 0: toks.append((s, c))
        return toks
    def barrier(self):
        toks = self.all_tokens()
        for e in (self.pe, self.act, self.dve, self.pool, self.sp):
            for (sem, val) in toks:
                k = id(sem)
                if e.seen.get(k, 0) >= val: continue
                e.h.wait_ge(sem, val); e.seen[k] = val
        for r in self.dused: r.dent = None
        self.dused = []; self.dfree = list(range(len(self.dpool)))

def _bf(a): return np.ascontiguousarray(a.astype(np.float32)).astype(NPBF)

_CONST_CACHE = {}
def host_consts():
    if _CONST_CACHE: return _CONST_CACHE
    C = {}
    C["ident_f"] = np.eye(128, dtype=np.float32)
    C["ident_b"] = _bf(np.eye(128))
    bo = np.zeros((128, 128), np.float32); bo[:64, :64] = 1 / 64.; bo[64:, 64:] = 1 / 64.
    C["bones"] = _bf(bo)
    C["ones512"] = _bf(np.full((128, 128), 1 / 512.))
    R = np.zeros((128, 128), np.float32)
    for d in range(128):
        if (d % 32) < 16: R[d + 16, d] = -1.0
        else: R[d - 16, d] = 1.0
    C["rmat"] = _bf(R)
    t = np.arange(LS); row = (t // 64).astype(np.float32); col = (t % 64).astype(np.float32)
    cosT = np.zeros((128, LS), np.float32); sinT = np.zeros((128, LS), np.float32)
    inv = (10000.0 ** (-np.arange(16, dtype=np.float32) / 16)).astype(np.float32)
    for d in range(128):
        dd = d % 64; half = dd // 32; f = (dd % 32) % 16
        pos = row if half == 0 else col
        ang = (pos * inv[f]).astype(np.float32)
        cosT[d] = np.cos(ang); sinT[d] = np.sin(ang)
    C["ropec"] = cosT; C["ropes"] = sinT
    for nm, L in (("p", LP), ("s", LS)):
        tt = np.arange(L); ic = np.zeros((4, L), np.float32)
        for g, w in enumerate((2, 4, 8, 16)):
            lo = np.clip(tt - w // 2, 0, L); hi = np.clip(tt + w // 2, 0, L)
            ic[g] = 1.0 / (hi - lo)
        C["invcnt_" + nm] = ic
        tl = np.linspace(0.0, 1.0, L, dtype=np.float32)[:, None]
        bands = np.linspace(1e-4, 15, 16, dtype=np.float32)[None, :]
        w_ = ((2.0 * math.pi / L) * np.arange(L, dtype=np.float32))[:, None]
        z = np.concatenate([tl, np.cos(bands * w_), np.sin(bands * w_)], axis=-1).astype(np.float32)
        idx = np.concatenate([np.arange(L), (L - np.arange(L)) % L])
        C["zf_" + nm] = np.ascontiguousarray(z[idx].T)
        trow = tl[:, 0][idx].astype(np.float32).copy(); trow[L] = 1e4
        C["trow_" + nm] = trow[None, :]
    max_decay = math.log(1e-2) / 0.3; min_decay = math.log(1e-2) / 1.5
    deltas = np.abs(np.linspace(min_decay, max_decay, 512, dtype=np.float32))
    C["negdelta"] = np.ascontiguousarray((-deltas).reshape(4, 128).T)
    N = 512
    tt = np.arange(512)[:, None].astype(np.float64); kk = np.arange(512)[None, :].astype(np.float64)
    ang = 2 * np.pi * tt * kk / N
    C["fp_c"] = _bf(np.cos(ang).reshape(4, 128, 512).transpose(1, 0, 2))
    C["fp_sn"] = _bf((-np.sin(ang)).reshape(4, 128, 512).transpose(1, 0, 2))
    k2 = np.arange(512)[:, None].astype(np.float64); t2 = np.arange(256)[None, :].astype(np.float64)
    ang2 = 2 * np.pi * k2 * t2 / N
    C["fi_c"] = _bf((np.cos(ang2) / N).reshape(4, 128, 256).transpose(1, 0, 2))
    C["fi_sn"] = _bf((-np.sin(ang2) / N).reshape(4, 128, 256).transpose(1, 0, 2))
    N = 8192
    j = np.arange(64)[:, None].astype(np.float64); g = np.arange(64)[None, :].astype(np.float64)
    a1 = 2 * np.pi * j * g / 64
    C["wa"] = _bf(np.concatenate([np.concatenate([np.cos(a1), -np.sin(a1), -np.cos(a1)], axis=1), np.zeros((64, 192))], axis=0))
    a = np.arange(128)[:, None, None].astype(np.float64)
    gg = np.arange(64)[None, :, None].astype(np.float64); kb = np.arange(128)[None, None, :].astype(np.float64)
    th = 2 * np.pi * a * (gg + 64 * kb) / N
    C["tb_c"] = _bf(np.cos(th)); C["tb_s"] = _bf(np.sin(th)); C["tb_sn"] = _bf(-np.sin(th))
    kb2 = np.arange(128)[:, None].astype(np.float64); nb = np.arange(128)[None, :].astype(np.float64)
    al = 2 * np.pi * kb2 * nb / 128
    C["tbp_c"] = _bf(np.cos(al)); C["tbp_s"] = _bf(np.sin(al))
    g3 = np.arange(64)[:, None, None].astype(np.float64); nb3 = np.arange(128)[None, :, None].astype(np.float64)
    na3 = np.arange(32)[None, None, :].astype(np.float64)
    ph = 2 * np.pi * g3 * (128 * na3 + nb3) / N
    C["tc_st"] = _bf(np.concatenate([np.cos(ph) / N, -np.sin(ph) / N], axis=0))
    _CONST_CACHE.update(C)
    return C

WEIGHT_SHAPES = {
    'w_ada': (DEPTH, D, 3 * D), 'b_ada': (DEPTH, 3 * D), 'norm_g': (DEPTH, D), 'w_in': (DEPTH, D, NIN),
    'conv_dw_w': (DEPTH, 31, 512), 'conv_dw_b': (DEPTH, 512), 'conv_ln_g': (DEPTH, 512), 'conv_ln_b': (DEPTH, 512),
    'conv_pw': (DEPTH, 512, 512), 'hy_short_w': (DEPTH, 3, 1536), 'hy_short_b': (DEPTH, 1536),
    'hy_w1': (DEPTH, 33, 64), 'hy_b1': (DEPTH, 64), 'hy_freq': (DEPTH, 64), 'hy_w2': (DEPTH, 64, 64), 'hy_b2': (DEPTH, 64),
    'hy_w3': (DEPTH, 64, 2048), 'hy_b3': (DEPTH, 2048), 'hy_skip': (DEPTH, 2, 512),
    'pool_w': (DEPTH, 4, 128, 128), 'pool_scale': (DEPTH, 512), 'q_norm': (DEPTH, 64), 'k_norm': (DEPTH, 64),
    'wo_conv': (DEPTH, 512, D), 'wo_hyena': (DEPTH, 512, D), 'wo_pool': (DEPTH, 512, D), 'wo_attn': (DEPTH, D, D),
    'w_out': (DEPTH, D, D),
}

BLK = dict(a_val=0, a_glu=4, a_gate=8, b_proj=12, b_gate=24, c_in=28, c_gate=32, q=36, k=44, v=46, d_gate=48, gm=56)
def blk_func(j):
    if j < 4: return AF.Identity
    if j < 8: return AF.Sigmoid
    if j < 12: return AF.Silu
    if j < 24: return AF.Identity
    if j < 28: return AF.Silu
    if j < 32: return AF.Identity
    if j < 36: return AF.Silu
    if j < 48: return AF.Identity
    if j < 56: return AF.Silu
    return AF.Sigmoid

class Grp:
    pass

def build_program(cfg):
    nc = bass.Bass("TRN2", target_bir_lowering=False)
    C = host_consts()
    es = contextlib.ExitStack()
    with es:
        fw = FW(nc, es)
        pe, act, dve, pool, sp = fw.pe, fw.act, fw.dve, fw.pool, fw.sp

        _uid = [0]
        def uq(name):
            _uid[0] += 1
            return '%s_%d' % (name, _uid[0])
        def din(name, shape, dt=F32): return nc.dram_tensor(name, list(shape), dt, kind="ExternalInput").ap()
        def dout(name, shape, dt=F32): return nc.dram_tensor(name, list(shape), dt, kind="ExternalOutput").ap()
        def dscr(name, shape, dt): return nc.dram_tensor(name, list(shape), dt).ap()

        W = {k: din(k, s) for k, s in WEIGHT_SHAPES.items()}
        CT = {}
        for k, v in C.items():
            CT[k] = din("c_" + k, v.shape, BF16 if v.dtype == NPBF else F32)
        xp_in = din("xp", (2 * LP, D)); xs_in = din("xs", (LS, D))
        ck_in = din("ck", (DEPTH, PAST, 256)); cv_in = din("cv", (DEPTH, PAST, 256))
        cvec = din("cvec", (2, D))
        idxE_in = din("idxE", (128, 10), I32); idxQ_in = din("idxQ", (128, 1), I32); maskLR_in = din("maskLR", (128, 2))
        ropec_own = din("ropec_own", (128, 1024)); ropes_own = din("ropes_own", (128, 1024)); invcnt_own = din("invcnt_own", (4, 1024))
        yp_out = dout("yp", (2 * LP, D)); ys_out = dout("ys", (1024, D))
        nk_out = dout("nk", (2, DEPTH, LP, 256)); nv_out = dout("nv", (2, DEPTH, LP, 256))
        dbg = {}

        groups = []
        if cfg["prompt"]:
            g = Grp(); g.name = "p"; g.nseq = 2; g.L = LP; g.T = 512; g.x_in = xp_in; g.x_out = yp_out; g.crow = 0; g.latent = False
            groups.append(g)
        if cfg["sample"]:
            g = Grp(); g.name = "s"; g.nseq = 1; g.L = LS; g.T = LS; g.x_in = xs_in; g.x_out = ys_out; g.crow = 1; g.latent = True
            groups.append(g)
        for g in groups:
            T = g.T
            g.x_mid = dscr("xmid_" + g.name, (T, D), F32); g.r_xmid = Reg("xmid")
            g.U = dscr("U_" + g.name, (88, 128, T), BF16); g.rU = [Reg("U%d" % j) for j in range(88)]
            g.UKV = dscr("UKV_" + g.name, (4, 128, T), F32); g.rUKV = [Reg("UKV%d" % j) for j in range(4)]
            g.YIN = dscr("YIN_" + g.name, (20, 128, T), BF16); g.rYIN = [Reg("Y%d" % j) for j in range(20)]
            g.QR = dscr("QR_" + g.name, (8, 128, T), BF16); g.rQR = Reg("QR")
            g.ZB = dscr("ZB_" + g.name, (128, 2 * g.L if g.nseq == 1 else g.T), BF16); g.rZB = Reg("ZB")
            if g.latent:
                g.HS = dscr("HS_" + g.name, (2, 4, 2, 2, 128, 64 * 64), BF16)
            else:
                g.HS = dscr("HS_" + g.name, (2, 4, 2, 128, 4 * 128), BF16)
            g.rHS = Reg("HS")
            if g.latent:
                g.UE = dscr("UE_" + g.name, (88, 128, 1280), BF16); g.rUE = [Reg("UE%d" % j) for j in range(88)]
                g.KR = dscr("KR_" + g.name, (2, 128, g.T), BF16); g.rKR = Reg("KR")
                g.VT = dscr("VT_" + g.name, (128, g.T // 128, 256), BF16); g.rVT = Reg("VT")
                g.YB = [dscr("YB%d_" % cb_ + g.name, (4 * 128, 1024), BF16) for cb_ in range(4)]; g.rYB = Reg("YB")
            g.r_xout = Reg("xout")
        r_nk = Reg("nk"); r_nv = Reg("nv")

        def sbp(name, shape, dt): return es.enter_context(nc.sbuf_tensor(name, list(shape), dt))
        ident_f = sbp("ident_f", [128, 128], F32); ident_b = sbp("ident_b", [128, 128], BF16)
        bones = sbp("bones", [128, 128], BF16); ones512 = sbp("ones512", [128, 128], BF16); rmat = sbp("rmat", [128, 128], BF16)
        r_const = Reg("const")
        for tile_, nm in ((ident_f, "ident_f"), (ident_b, "ident_b"), (bones, "bones"), (ones512, "ones512"), (rmat, "rmat")):
            fw.dma(sp, tile_[:], CT[nm][:, :], writes=[Reg("c" + nm)])
        idxE = sbp("idxE_sb", [128, 10], I32); idxQ = sbp("idxQ_sb", [128, 1], I32); maskLR = sbp("maskLR_sb", [128, 2], F32)
        fw.dma(sp, idxE[:], idxE_in[:, :], writes=[Reg("cidxE")]); fw.dma(sp, idxQ[:], idxQ_in[:, :], writes=[Reg("cidxQ")])
        fw.dma(sp, maskLR[:], maskLR_in[:, :], writes=[Reg("cmask")])
        fw.barrier()
        class PSX_: pass
        PSX = PSX_()
        psi = [0]
        def psum_std(ph, nb=7, with_pst=True):
            PSX.banks = [ph.enter_context(nc.psum_tensor(uq("ps"), [128, 512], F32)) for i in range(nb)]
            PSX.regs = [Reg("ps%d" % i) for i in range(nb)]
            if with_pst:
                PSX.pst = ph.enter_context(nc.psum_tensor(uq("pst"), [128, 1024], BF16)); PSX.rpst = Reg("pst")
        def PS():
            nb = len(PSX.banks)
            i = psi[0] % nb; psi[0] += 1
            return PSX.banks[i], PSX.regs[i]

        def ACT(out, in_, func, reads, writes, **kw):
            return fw.op(act, lambda e: e.activation(out=out, in_=in_, func=func, **kw), reads, writes)
        def MM(out, lhsT, rhs, start, stop, reads, writes):
            return fw.op(pe, lambda e: e.matmul(out, lhsT=lhsT, rhs=rhs, start=start, stop=stop), reads, writes)
        def TR(out, in_, ident, reads, writes):
            return fw.op(pe, lambda e: e.transpose(out, in_, ident), reads, writes)
        def TT(out, a, b, op, reads, writes, eng=None):
            return fw.op(eng or dve, lambda e: e.tensor_tensor(out=out, in0=a, in1=b, op=op), reads, writes)
        def TS(out, a, s1, s2, op0, op1, reads, writes, eng=None):
            if s2 is None:
                return fw.op(eng or dve, lambda e: e.tensor_scalar(out=out, in0=a, scalar1=s1, scalar2=None, op0=op0), reads, writes)
            return fw.op(eng or dve, lambda e: e.tensor_scalar(out=out, in0=a, scalar1=s1, scalar2=s2, op0=op0, op1=op1), reads, writes)
        def STT(out, a, s, b, op0, op1, reads, writes, eng=None):
            return fw.op(eng or dve, lambda e: e.scalar_tensor_tensor(out=out, in0=a, scalar=s, in1=b, op0=op0, op1=op1), reads, writes)
        def CP(out, in_, reads, writes, eng=None):
            return fw.op(eng or dve, lambda e: e.tensor_copy(out=out, in_=in_), reads, writes)
        def MS(out, val, writes, eng=None):
            return fw.op(eng or dve, lambda e: e.memset(out, val), (), writes)
        def RCP(out, in_, reads, writes):
            return fw.op(dve, lambda e: e.reciprocal(out=out, in_=in_), reads, writes)
        def colvec(dst, src_vec_ap, nb, writes):
            return fw.dma(sp, dst, src_vec_ap.rearrange("(b p) -> p b", p=128), writes=writes, allow_slow_non_contiguous=True)

        def dense_in(g, l, x_src, r_xsrc, wbs=tuple(range(22)), ext=False):
            T = g.T; TC = min(T, 2048); nchunk = T // TC
            if ext: TC = 1280; nchunk = 1
            with contextlib.ExitStack() as ph:
                def sb(name, shape, dt): return ph.enter_context(nc.sbuf_tensor(uq(name), list(shape), dt))
                psum_std(ph)
                modb = sb("modb", [128, 3 * D], F32); r_mod = Reg("mod")
                Gt = sb("Gt", [128, D], F32); r_G = Reg("G")
                ngb = sb("ngb", [128, D], F32); r_ng = Reg("ng")
                badab = sb("badab", [128, 3 * D], F32); r_bada = Reg("bada")
                ccol = sb("ccol", [128, 8], F32); r_cc = Reg("cc")
                crep = sb("crep", [128, 8, 128], BF16); r_crep = Reg("crep")
                wsl = [sb("wsl%d" % i, [128, 8, 512], BF16) for i in range(3)]; r_wsl = [Reg("wsl%d" % i) for i in range(3)]
                hT = sb("hT", [128, 8, TC], BF16); r_hT = [Reg("hT%d" % i) for i in range(TC // 128)]
                xt = [sb("xt%d" % i, [128, D], F32) for i in range(2)]; r_xt = [Reg("xt%d" % i) for i in range(2)]
                sqj = sb("sqj", [128, D], F32); r_sqj = Reg("sqj")
                ss = [sb("ss%d" % i, [128, 1], F32) for i in range(2)]; r_ss = [Reg("ss%d" % i) for i in range(2)]
                hb = [sb("hb%d" % i, [128, D], BF16) for i in range(2)]; r_hb = [Reg("hb%d" % i) for i in range(2)]
                stg = [sb("stg%d" % i, [128, TC], BF16) for i in range(3)]; r_stg = [Reg("stg%d" % i) for i in range(3)]
                stgf = [sb("stgf%d" % i, [128, TC], F32) for i in range(2)]; r_stgf = [Reg("stgf%d" % i) for i in range(2)]
                fuse_qk = g.latent
                if fuse_qk:
                    gqd = sb("gqd", [128, 1], F32); gkd = sb("gkd", [128, 1], F32); r_gvd = Reg("gqkd")
                    for h2 in range(2):
                        fw.dma(sp, gqd[h2 * 64:(h2 + 1) * 64, :], W['q_norm'][l].rearrange("(d o) -> d o", o=1), writes=[r_gvd])
                        fw.dma(sp, gkd[h2 * 64:(h2 + 1) * 64, :], W['k_norm'][l].rearrange("(d o) -> d o", o=1), writes=[r_gvd])
                    TS(gqd[:], gqd[:], 0.125, None, ALU.mult, None, [r_gvd], [r_gvd])
                    CW = 512
                    dsq = [sb("dsq%d" % i, [128, CW], BF16) for i in range(4)]; r_dsq = [Reg("dsq%d" % i) for i in range(4)]
                    drs = [sb("drs%d" % i, [128, CW], F32) for i in range(4)]; r_drs = [Reg("drs%d" % i) for i in range(4)]
                    dkf = [sb("dkf%d" % i, [128, CW], F32) for i in range(4)]; r_dkf = [Reg("dkf%d" % i) for i in range(4)]
                    dkb = [sb("dkb%d" % i, [128, CW], BF16) for i in range(4)]; r_dkb = [Reg("dkb%d" % i) for i in range(4)]
                    dt1 = [sb("dt1%d" % i, [128, CW], F32) for i in range(4)]; r_dt1 = [Reg("dt1%d" % i) for i in range(4)]
                    dqo = [sb("dqo%d" % i, [128, CW], BF16) for i in range(4)]; r_dqo = [Reg("dqo%d" % i) for i in range(4)]
                    NRT = 1024 if ext else TC
                    cosd = sb("cosd", [128, NRT], F32); sind = sb("sind", [128, NRT], F32); r_csd = Reg("csd")
                    dcnt = [0]
                    pending = []
                    def advance():
                        for gen in list(pending):
                            try: next(gen)
                            except StopIteration: pending.remove(gen)
                    def qk_chain(*a):
                        gen = qk_chain_gen(*a); next(gen); pending.append(gen)
                    vts = [sb("vts%d" % i, [128, 4, 128], BF16) for i in range(2)]; r_vts = [Reg("vts%d" % i) for i in range(2)]
                    vcnt = [0]
                    def v_chain_gen(src, rsrc, tok0, vb):
                        yield
                        k = vcnt[0] % 2; vcnt[0] += 1
                        for sub in range(4):
                            TR(PSX.pst[:, sub * 128:(sub + 1) * 128], src[:, sub * 128:(sub + 1) * 128], ident_b[:], rsrc, [PSX.rpst])
                        CP(vts[k][:], PSX.pst[:, 0:512].rearrange("p (s f) -> p s f", s=4), [PSX.rpst], [r_vts[k]])
                        fw.dma(sp, g.VT[:, tok0 // 128:tok0 // 128 + 4, vb * 128:(vb + 1) * 128], vts[k][:], reads=[r_vts[k]], writes=[g.rVT])
                    def v_chain(*a):
                        gen = v_chain_gen(*a); next(gen); pending.append(gen)
                    def qk_chain_gen(src, rsrc, gvec, tcol, n, out_dram, r_out):
                        k = dcnt[0] % 4; dcnt[0] += 1
                        TT(dsq[k][:, 0:n], src, src, ALU.mult, rsrc, [r_dsq[k]])
                        STT(dkf[k][:, 0:n], src, gvec, src, ALU.mult, ALU.bypass, rsrc + [r_gvd], [r_dkf[k]])
                        yield
                        ps, rps = PS()
                        MM(ps[:, 0:n], bones[:], dsq[k][:, 0:n], True, True, [r_dsq[k]], [rps])
                        ACT(drs[k][:, 0:n], ps[:, 0:n], AF.Ln, [rps], [r_drs[k]], bias=EPS)
                        ACT(drs[k][:, 0:n], drs[k][:, 0:n], AF.Exp, [r_drs[k]], [r_drs[k]], scale=-0.5)
                        TT(dkf[k][:, 0:n], dkf[k][:, 0:n], drs[k][:, 0:n], ALU.mult, [r_dkf[k], r_drs[k]], [r_dkf[k]])
                        CP(dkb[k][:, 0:n], dkf[k][:, 0:n], [r_dkf[k]], [r_dkb[k]])
                        yield
                        ps, rps = PS()
                        MM(ps[:, 0:n], rmat[:], dkb[k][:, 0:n], True, True, [r_dkb[k]], [rps])
                        TT(dt1[k][:, 0:n], ps[:, 0:n], sind[:, tcol:tcol + n], ALU.mult, [rps, r_csd], [r_dt1[k]])
                        TT(dkf[k][:, 0:n], dkf[k][:, 0:n], cosd[:, tcol:tcol + n], ALU.mult, [r_dkf[k], r_csd], [r_dkf[k]])
                        TT(dqo[k][:, 0:n], dkf[k][:, 0:n], dt1[k][:, 0:n], ALU.add, [r_dkf[k], r_dt1[k]], [r_dqo[k]])
                        fw.dma(sp, out_dram, dqo[k][:, 0:n], reads=[r_dqo[k]], writes=[r_out])
                fw.dma(sp, ccol[:], cvec[g.crow, :].rearrange("(b p) -> p b", p=128), writes=[r_cc], allow_slow_non_contiguous=True)
                fw.dma(sp, ngb[:], W['norm_g'][l:l + 1, :].partition_broadcast(128), writes=[r_ng])
                fw.dma(sp, badab[:], W['b_ada'][l:l + 1, :].partition_broadcast(128), writes=[r_bada])
                ACT(ccol[:], ccol[:], AF.Silu, [r_cc], [r_cc])
                CP(crep[:], ccol[:].unsqueeze(2).broadcast_to([128, 8, 128]), [r_cc], [r_crep])
                wi = 0
                wada = W['w_ada'][l].rearrange("(kt p) n -> p kt n", p=128)
                for cb in range(6):
                    s_ = wi % 3; wi += 1
                    fw.dma(pool, wsl[s_][:], wada[:, :, cb * 512:(cb + 1) * 512], writes=[r_wsl[s_]])
                    ps, rps = PS()
                    for kt in range(8):
                        MM(ps[:], crep[:, kt, :], wsl[s_][:, kt, :], kt == 0, kt == 7, [r_crep, r_wsl[s_]], [rps])
                    TT(modb[:, cb * 512:(cb + 1) * 512], ps[:], badab[:, cb * 512:(cb + 1) * 512], ALU.add, [rps, r_bada], [r_mod])
                STT(Gt[:], modb[:, D:2 * D], 1.0, ngb[:], ALU.add, ALU.mult, [r_mod, r_ng], [r_G])
                win = W['w_in'][l].rearrange("(kt p) n -> p kt n", p=128)
                si = 0; sfi = 0
                for ci in range(nchunk):
                    if fuse_qk:
                        if ext:
                            fw.dma(sp, cosd[:], ropec_own[:, :], writes=[r_csd]); fw.dma(sp, sind[:], ropes_own[:, :], writes=[r_csd])
                        else:
                            fw.dma(sp, cosd[:], CT["ropec"][:, ci * TC:(ci + 1) * TC], writes=[r_csd])
                            fw.dma(sp, sind[:], CT["ropes"][:, ci * TC:(ci + 1) * TC], writes=[r_csd])
                    for tl in range(TC // 128):
                        tok0 = ci * TC + tl * 128; b_ = tl % 2
                        if ext:
                            fw.idma(pool, xt[b_][:], x_src[:, :], idxE[:, tl:tl + 1], reads=[r_xsrc], writes=[r_xt[b_]])
                        else:
                            fw.dma(sp, xt[b_][:], x_src[tok0:tok0 + 128, :], reads=[r_xsrc], writes=[r_xt[b_]])
                        ACT(sqj[:], xt[b_][:], AF.Square, [r_xt[b_]], [r_sqj, r_ss[b_]], accum_out=ss[b_][:])
                        ACT(ss[b_][:], ss[b_][:], AF.Sqrt, [r_ss[b_]], [r_ss[b_]], scale=1.0 / D, bias=EPS)
                        RCP(ss[b_][:], ss[b_][:], [r_ss[b_]], [r_ss[b_]])
                        STT(xt[b_][:], xt[b_][:], ss[b_][:, 0:1], Gt[:], ALU.mult, ALU.mult, [r_xt[b_], r_ss[b_], r_G], [r_xt[b_]])
                        TT(hb[b_][:], xt[b_][:], modb[:, 0:D], ALU.add, [r_xt[b_], r_mod], [r_hb[b_]])
                        for kt in range(8):
                            TR(PSX.pst[:, kt * 128:(kt + 1) * 128], hb[b_][:, kt * 128:(kt + 1) * 128], ident_b[:], [r_hb[b_]], [PSX.rpst])
                        CP(hT[:, :, tl * 128:(tl + 1) * 128], PSX.pst[:].rearrange("p (k t) -> p k t", k=8), [PSX.rpst], [r_hT[tl]],
                           eng=act if tl % 2 else dve) if False else ACT(hT[:, :, tl * 128:(tl + 1) * 128], PSX.pst[:].rearrange("p (k t) -> p k t", k=8), AF.Copy, [PSX.rpst], [r_hT[tl]])
                    wb_order = list(wbs)
                    if fuse_qk:
                        hot = [w for w in (9, 10, 11) if w in wb_order]; cold = [w for w in wb_order if w not in hot]
                        wb_order = []
                        step = max(1, len(cold) // (len(hot) + 1)) if hot else 1
                        ci_ = 0
                        for hw in hot:
                            wb_order.append(hw); wb_order += cold[ci_:ci_ + step]; ci_ += step
                        wb_order += cold[ci_:]
                    for wb in wb_order:
                        s_ = wi % 3; wi += 1
                        fw.dma(pool, wsl[s_][:], win[:, :, wb * 512:(wb + 1) * 512], writes=[r_wsl[s_]])
                        for sub in range(4):
                            j = wb * 4 + sub
                            iskv = 44 <= j < 48
                            if iskv and not (fuse_qk and j >= 46):
                                so = stgf[sfi % 2]; rso = r_stgf[sfi % 2]; sfi += 1
                            else:
                                so = stg[si % 3]; rso = r_stg[si % 3]; si += 1
                            for c0 in range(0, TC, 512):
                                cw = min(512, TC - c0)
                                ps, rps = PS()
                                for kt in range(8):
                                    MM(ps[:, 0:cw], wsl[s_][:, kt, sub * 128:(sub + 1) * 128], hT[:, kt, c0:c0 + cw],
                                       kt == 0, kt == 7, [r_wsl[s_]] + r_hT[c0 // 128:(c0 + cw) // 128], [rps])
                                ACT(so[:, c0:c0 + cw], ps[:, 0:cw], blk_func(j), [rps], [rso])
                                if fuse_qk and c0 == 512: advance()
                            if fuse_qk: advance()
                            if fuse_qk and 36 <= j < 46:
                                isq = j < 44
                                if ext:
                                    for t_ in range(0, 1024, 512):
                                        qk_chain(so[:, 128 + t_:128 + t_ + 512], [rso], gqd[:, 0:1], t_, 512, g.QR[j - 36][:, t_:t_ + 512], g.rQR)
                                else:
                                    for t_ in range(0, TC, 512):
                                        if isq:
                                            qk_chain(so[:, t_:t_ + 512], [rso], gqd[:, 0:1], t_, 512, g.QR[j - 36][:, ci * TC + t_:ci * TC + t_ + 512], g.rQR)
                                        else:
                                            qk_chain(so[:, t_:t_ + 512], [rso], gkd[:, 0:1], t_, 512, g.KR[j - 44][:, ci * TC + t_:ci * TC + t_ + 512], g.rKR)
                            if fuse_qk and j in (46, 47) and not ext:
                                for t_ in range(0, TC, 512):
                                    v_chain(so[:, t_:t_ + 512], [rso], ci * TC + t_, j - 46)
                            if ext:
                                fw.dma(sp, g.UE[j][:, :], so[:, 0:TC], reads=[rso], writes=[g.rUE[j]])
                            elif iskv and fuse_qk and j >= 46:
                                pass
                            elif iskv:
                                fw.dma(sp, g.UKV[j - 44][:, ci * TC:(ci + 1) * TC], so[:], reads=[rso], writes=[g.rUKV[j - 44]])
                            else:
                                fw.dma(sp, g.U[j][:, ci * TC:(ci + 1) * TC], so[:], reads=[rso], writes=[g.rU[j]])
                if fuse_qk:
                    while pending: advance()
                fw.barrier()
            return

        def seq_tiles(g, n):
            out = []
            for s in range(g.nseq):
                for t0 in range(0, g.L, n):
                    out.append((s, t0, min(n, g.L - t0)))
            return out

        def own_tiles(own):
            return [(0, 0, 512), (0, 512, 512)]
        def conformer(g, l, own=False):
            NT = 512 if g.L >= 512 else g.L
            Us = g.UE if own else g.U; rUs = g.rUE if own else g.rU
            with contextlib.ExitStack() as ph:
                def sb(name, shape, dt): return ph.enter_context(nc.sbuf_tensor(uq(name), list(shape), dt))
                psum_std(ph)
                dwc = sb("dwc", [128, 4, 31], F32); r_dwc = Reg("dwc")
                dgm = sb("dgm", [128, 4, 31, 128], BF16); r_dgm = Reg("dgm")
                dwb = sb("dwb", [128, 4], F32); lng = sb("lng", [128, 4], F32); lnb = sb("lnb", [128, 4], F32); r_vec = Reg("cvec")
                pw = sb("pw", [128, 4, 512], BF16); r_pw = Reg("pw")
                vl = [sb("vl%d" % i, [128, 4, NT + 30], BF16) for i in range(2)]; r_vl = [Reg("vl%d" % i) for i in range(2)]
                gl = [sb("gl%d" % i, [128, 4, NT + 30], BF16) for i in range(2)]; r_gl = [Reg("gl%d" % i) for i in range(2)]
                sag = [sb("sag%d" % i, [128, 4, NT], BF16) for i in range(2)]; r_sag = [Reg("sag%d" % i) for i in range(2)]
                a2b_ = [sb("a2b%d" % i, [128, 4, NT], BF16) for i in range(2)]; r_a2b_ = [Reg("a2b%d" % i) for i in range(2)]
                a2s_ = [sb("a2s%d" % i, [128, 4, NT], BF16) for i in range(2)]; r_a2s_ = [Reg("a2s%d" % i) for i in range(2)]
                mean_ = [sb("mean%d" % i, [128, NT], F32) for i in range(2)]; r_mean_ = [Reg("mean%d" % i) for i in range(2)]
                var_ = [sb("var%d" % i, [128, NT], F32) for i in range(2)]; r_var_ = [Reg("var%d" % i) for i in range(2)]
                nmr_ = [sb("nmr%d" % i, [128, NT], F32) for i in range(2)]; r_nmr_ = [Reg("nmr%d" % i) for i in range(2)]
                tmp_ = [sb("tmpc%d" % i, [128, NT], F32) for i in range(2)]; r_tmp_ = [Reg("tmpc%d" % i) for i in range(2)]
                lno_ = [sb("lno%d" % i, [128, 4, NT], BF16) for i in range(2)]; r_lno_ = [Reg("lno%d" % i) for i in range(2)]
                yo = [sb("yoc%d" % i, [128, 4, NT], BF16) for i in range(2)]; r_yo = [Reg("yoc%d" % i) for i in range(2)]
                for b in range(4):
                    fw.dma(sp, dwc[:, b, :], W['conv_dw_w'][l][:, b * 128:(b + 1) * 128].rearrange("k c -> c k"), writes=[r_dwc],
                           allow_slow_non_contiguous=True)
                colvec(dwb[:], W['conv_dw_b'][l], 4, [r_vec]); colvec(lng[:], W['conv_ln_g'][l], 4, [r_vec]); colvec(lnb[:], W['conv_ln_b'][l], 4, [r_vec])
                fw.dma(pool, pw[:], W['conv_pw'][l].rearrange("(kt p) n -> p kt n", p=128), writes=[r_pw])
                for b in range(4):
                    for k in range(31):
                        TS(dgm[:, b, k, :], ident_b[:], dwc[:, b, k:k + 1], None, ALU.mult, None, [r_dwc], [r_dgm])
                for it, (s, t0, n) in enumerate(own_tiles(own) if own else seq_tiles(g, NT)):
                    b_ = it % 2; g0 = s * g.L + t0
                    a2b = a2b_[b_]; r_a2b = r_a2b_[b_]; a2s = a2s_[b_]; r_a2s = r_a2s_[b_]; mean = mean_[b_]; r_mean = r_mean_[b_]
                    var = var_[b_]; r_var = r_var_[b_]; nmr = nmr_[b_]; r_nmr = r_nmr_[b_]; tmp = tmp_[b_]; r_tmp = r_tmp_[b_]; lno = lno_[b_]; r_lno = r_lno_[b_]
                    vmin, vmax, cb0 = (-128, 1152, 128) if own else (0, g.L, s * g.L)
                    lo = max(t0 - 15, vmin); hi = min(t0 + n + 15, vmax)
                    off = lo - (t0 - 15); w = hi - lo
                    if off > 0 or w < n + 30:
                        MS(vl[b_][:], 0.0, [r_vl[b_]]); MS(gl[b_][:], 0.0, [r_gl[b_]])
                    fw.dma(sp, vl[b_][:, :, off:off + w], Us[0:4, :, cb0 + lo:cb0 + hi].rearrange("j p w -> p j w"), reads=rUs[0:4], writes=[r_vl[b_]])
                    fw.dma(sp, gl[b_][:, :, off:off + w], Us[4:8, :, cb0 + lo:cb0 + hi].rearrange("j p w -> p j w"), reads=rUs[4:8], writes=[r_gl[b_]])
                    fw.dma(sp, sag[b_][:, :, 0:n], Us[8:12, :, cb0 + t0:cb0 + t0 + n].rearrange("j p w -> p j w"), reads=rUs[8:12], writes=[r_sag[b_]])
                    TT(vl[b_][:], vl[b_][:], gl[b_][:], ALU.mult, [r_vl[b_], r_gl[b_]], [r_vl[b_]])
                    if own:
                        g0 = t0
                        if t0 == 0:
                            TS(vl[b_][:, :, 0:15], vl[b_][:, :, 0:15], maskLR[:, 0:1], None, ALU.mult, None, [r_vl[b_]], [r_vl[b_]])
                        if t0 + n == 1024:
                            TS(vl[b_][:, :, n + 15:n + 30], vl[b_][:, :, n + 15:n + 30], maskLR[:, 1:2], None, ALU.mult, None, [r_vl[b_]], [r_vl[b_]])
                    pss = []
                    for b in range(4):
                        ps, rps = PS(); pss.append((ps, rps))
                        for k in range(31):
                            MM(ps[:, 0:n], dgm[:, b, k, :], vl[b_][:, b, k:k + n], k == 0, k == 30, [r_dgm, r_vl[b_]], [rps])
                        ACT(a2b[:, b, 0:n], ps[:, 0:n], AF.Identity, [rps], [r_a2b], bias=dwb[:, b:b + 1])
                        ACT(a2s[:, b, 0:n], ps[:, 0:n], AF.Square, [rps], [r_a2s], bias=dwb[:, b:b + 1])
                    pm, rpm = PS(); pq, rpq = PS()
                    for b in range(4):
                        MM(pm[:, 0:n], ones512[:], a2b[:, b, 0:n], b == 0, b == 3, [r_a2b], [rpm])
                    for b in range(4):
                        MM(pq[:, 0:n], ones512[:], a2s[:, b, 0:n], b == 0, b == 3, [r_a2s], [rpq])
                    ACT(mean[:, 0:n], pm[:, 0:n], AF.Copy, [rpm], [r_mean])
                    TT(var[:, 0:n], mean[:, 0:n], mean[:, 0:n], ALU.mult, [r_mean], [r_var])
                    TT(var[:, 0:n], pq[:, 0:n], var[:, 0:n], ALU.subtract, [rpq, r_var], [r_var])
                    TS(var[:, 0:n], var[:, 0:n], 0.0, None, ALU.max, None, [r_var], [r_var])
                    ACT(var[:, 0:n], var[:, 0:n], AF.Sqrt, [r_var], [r_var], bias=EPS)
                    RCP(var[:, 0:n], var[:, 0:n], [r_var], [r_var])
                    STT(nmr[:, 0:n], mean[:, 0:n], -1.0, var[:, 0:n], ALU.mult, ALU.mult, [r_mean, r_var], [r_nmr])
                    for b in range(4):
                        TT(tmp[:, 0:n], a2b[:, b, 0:n], var[:, 0:n], ALU.mult, [r_a2b, r_var], [r_tmp])
                        TT(tmp[:, 0:n], tmp[:, 0:n], nmr[:, 0:n], ALU.add, [r_tmp, r_nmr], [r_tmp])
                        ACT(lno[:, b, 0:n], tmp[:, 0:n], AF.Silu, [r_tmp, r_vec], [r_lno], scale=lng[:, b:b + 1], bias=lnb[:, b:b + 1])
                    for ob in range(4):
                        ps, rps = PS()
                        for kb in range(4):
                            MM(ps[:, 0:n], pw[:, kb, ob * 128:(ob + 1) * 128], lno[:, kb, 0:n], kb == 0, kb == 3, [r_pw, r_lno], [rps])
                        TT(yo[b_][:, ob, 0:n], ps[:, 0:n], sag[b_][:, ob, 0:n], ALU.mult, [rps, r_sag[b_]], [r_yo[b_]])
                    fw.dma(sp, g.YIN[0:4, :, g0:g0 + n].rearrange("j p w -> p j w"), yo[b_][:, :, 0:n], reads=[r_yo[b_]], writes=g.rYIN[0:4])
                fw.barrier()

        def pooling(g, l, own=False):
            NT = 512 if g.L >= 512 else g.L
            Us = g.UE if own else g.U; rUs = g.rUE if own else g.rU
            with contextlib.ExitStack() as ph:
                def sb(name, shape, dt): return ph.enter_context(nc.sbuf_tensor(uq(name), list(shape), dt))
                psum_std(ph)
                pwt = sb("pwt", [128, 4, 128], BF16); r_pwt = Reg("pwt")
                psc = sb("psc", [128, 4], F32); r_psc = Reg("psc")
                xin = [sb("xin%d" % i, [128, 4, NT + 32], BF16) for i in range(2)]; r_xin = [Reg("xin%d" % i) for i in range(2)]
                scg = [sb("scg%d" % i, [128, 4, NT], BF16) for i in range(2)]; r_scg = [Reg("scg%d" % i) for i in range(2)]
                icn = [sb("icn%d" % i, [128, 4, NT], F32) for i in range(2)]; r_icn = [Reg("icn%d" % i) for i in range(2)]
                was = [sb("wa_%d" % i, [128, NT + 32], F32) for i in range(4)]; wbs_ = [sb("wb_%d" % i, [128, NT + 32], F32) for i in range(4)]
                r_was = [Reg("wa%d" % i) for i in range(4)]; r_wbs = [Reg("wb%d" % i) for i in range(4)]
                plds = [sb("pld%d" % i, [128, NT], BF16) for i in range(4)]; r_plds = [Reg("pld%d" % i) for i in range(4)]
                yo = [sb("yop%d" % i, [128, 4, NT], BF16) for i in range(2)]; r_yo = [Reg("yop%d" % i) for i in range(2)]
                fw.dma(pool, pwt[:], W['pool_w'][l].rearrange("g c d -> c g d"), writes=[r_pwt])
                colvec(psc[:], W['pool_scale'][l], 4, [r_psc])
                ict = invcnt_own if own else CT["invcnt_" + g.name]
                for it, (s, t0, n) in enumerate(own_tiles(own) if own else seq_tiles(g, NT)):
                    b_ = it % 2; g0 = s * g.L + t0
                    vmin, vmax, cb0 = (-128, 1152, 128) if own else (0, g.L, s * g.L)
                    lo = max(t0 - 16, vmin); hi = min(t0 + n + 16, vmax); off = lo - (t0 - 16); w = hi - lo
                    if off > 0 or w < n + 32:
                        MS(xin[b_][:], 0.0, [r_xin[b_]])
                    fw.dma(sp, xin[b_][:, :, off:off + w], Us[28:32, :, cb0 + lo:cb0 + hi].rearrange("j p w -> p j w"), reads=rUs[28:32], writes=[r_xin[b_]])
                    fw.dma(sp, scg[b_][:, :, 0:n], Us[32:36, :, cb0 + t0:cb0 + t0 + n].rearrange("j p w -> p j w"), reads=rUs[32:36], writes=[r_scg[b_]])
                    if own:
                        g0 = t0
                        if t0 == 0:
                            TS(xin[b_][:, :, 0:16], xin[b_][:, :, 0:16], maskLR[:, 0:1], None, ALU.mult, None, [r_xin[b_]], [r_xin[b_]])
                        if t0 + n == 1024:
                            TS(xin[b_][:, :, n + 16:n + 32], xin[b_][:, :, n + 16:n + 32], maskLR[:, 1:2], None, ALU.mult, None, [r_xin[b_]], [r_xin[b_]])
                    for gi in range(4):
                        fw.dma(sp, icn[b_][:, gi, 0:n], ict[gi:gi + 1, t0:t0 + n].partition_broadcast(128), writes=[r_icn[b_]])
                    m = n + 32
                    for gi in range(4):
                        wa = was[gi]; wb_ = wbs_[gi]; r_wa = r_was[gi]; r_wb = r_wbs[gi]; pld = plds[gi]; r_pld = r_plds[gi]
                        x_ = xin[b_][:, gi, :]
                        TT(wa[:, 1:m], x_[:, 0:m - 1], x_[:, 1:m], ALU.add, [r_xin[b_]], [r_wa])
                        cur, rcur, oth, roth = wa, r_wa, wb_, r_wb
                        lo_i = 1; hi_i = m
                        sh = 1
                        for lev in range(gi):
                            nlo = lo_i + sh; nhi = hi_i - sh
                            TT(oth[:, nlo:nhi], cur[:, nlo - sh:nhi - sh], cur[:, nlo + sh:nhi + sh], ALU.add, [rcur], [roth])
                            cur, rcur, oth, roth = oth, roth, cur, rcur
                            lo_i, hi_i = nlo, nhi; sh *= 2
                        TT(oth[:, 16:16 + n], cur[:, 16:16 + n], icn[b_][:, gi, 0:n], ALU.mult, [rcur, r_icn[b_]], [roth])
                        TT(pld[:, 0:n], oth[:, 16:16 + n], x_[:, 16:16 + n], ALU.subtract, [roth, r_xin[b_]], [r_pld])
                        ps, rps = PS()
                        MM(ps[:, 0:n], pwt[:, gi, :], pld[:, 0:n], True, True, [r_pwt, r_pld], [rps])
                        STT(yo[b_][:, gi, 0:n], ps[:, 0:n], psc[:, gi:gi + 1], scg[b_][:, gi, 0:n], ALU.mult, ALU.mult, [rps, r_psc, r_scg[b_]], [r_yo[b_]])
                    fw.dma(sp, g.YIN[8:12, :, g0:g0 + n].rearrange("j p w -> p j w"), yo[b_][:, :, 0:n], reads=[r_yo[b_]], writes=g.rYIN[8:12])
                fw.barrier()

        def attention(g, l, own=False):
            L = g.L; NT = 512 if L >= 512 else L
            Us = g.UE if own else g.U; rUs = g.rUE if own else g.rU
            nkeys = L + (PAST if g.latent else 0); nst = nkeys // 128
            with contextlib.ExitStack() as ph:
                def sb(name, shape, dt): return ph.enter_context(nc.sbuf_tensor(uq(name), list(shape), dt))
                sc = [ph.enter_context(nc.psum_tensor(uq("sc"), [128, 1024], F32)) for i in range(3)]; r_sc = [Reg("sc%d" % i) for i in range(3)]
                PSX.banks = [sc[i][:, k * 512:(k + 1) * 512] for i in range(3) for k in range(2)]
                PSX.regs = [r_sc[i] for i in range(3) for k in range(2)]
                K2 = sb("K2", [128, 4, g.nseq, nkeys], BF16); r_K2 = Reg("K2")
                Ve = sb("Ve", [128, g.nseq, nst, 4, 128], BF16); Vo = sb("Vo", [128, g.nseq, nst, 4, 128], BF16); r_V = Reg("V")
                gq = sb("gq", [128, 1], F32); gk = sb("gk", [128, 1], F32); r_gv = Reg("gqk")
                NSET = 2
                kraws = [sb("kraw%d" % i, [128, NT], F32) for i in range(NSET)]; r_kraws = [Reg("kraw%d" % i) for i in range(NSET)]
                sqs = [sb("sqa%d" % i, [128, NT], BF16) for i in range(NSET)]; r_sqs = [Reg("sqa%d" % i) for i in range(NSET)]
                rstds = [sb("rstd%d" % i, [128, NT], F32) for i in range(NSET)]; r_rstds = [Reg("rstd%d" % i) for i in range(NSET)]
                knfs = [sb("knf%d" % i, [128, NT], F32) for i in range(NSET)]; r_knfs = [Reg("knf%d" % i) for i in range(NSET)]
                knbs = [sb("knb%d" % i, [128, NT], BF16) for i in range(NSET)]; r_knbs = [Reg("knb%d" % i) for i in range(NSET)]
                t1s = [sb("t1a%d" % i, [128, NT], F32) for i in range(NSET)]; r_t1s = [Reg("t1a%d" % i) for i in range(NSET)]
                qsts = [sb("qst%d" % i, [128, NT], BF16) for i in range(NSET)]; r_qsts = [Reg("qst%d" % i) for i in range(NSET)]
                cosl = sb("cosl", [128, NT], F32); sinl = sb("sinl", [128, NT], F32); r_cs = Reg("cs")
                cosq = sb("cosq", [128, NT], F32); sinq = sb("sinq", [128, NT], F32); r_csq = Reg("csq")
                setc = [0]
                otm = sb("otm", [128, 256], F32); r_otm = Reg("otm")
                vraw = sb("vraw", [128, 2, NT], F32); r_vraw = Reg("vraw")
                MS(Ve[:], 1.0, [r_V]); MS(Vo[:], 1.0, [r_V])
                for h2 in range(2):
                    fw.dma(sp, gq[h2 * 64:(h2 + 1) * 64, :], W['q_norm'][l].rearrange("(d o) -> d o", o=1), writes=[r_gv])
                    fw.dma(sp, gk[h2 * 64:(h2 + 1) * 64, :], W['k_norm'][l].rearrange("(d o) -> d o", o=1), writes=[r_gv])
                TS(gq[:], gq[:], 0.125, None, ALU.mult, None, [r_gv], [r_gv])
                koff = PAST if g.latent else 0
                if g.latent:
                    ckd = sb("ckd", [128, 2, 4, 128], F32); r_ckd = Reg("ckd")
                    for st in range(2):
                        for dup in range(2):
                            fw.dma(sp, ckd[:, st, :, dup * 64:(dup + 1) * 64],
                                   ck_in[l, st * 128:(st + 1) * 128, :].rearrange("s (g d) -> s g d", g=4), writes=[r_ckd])
                    for st in range(2):
                        for hg in range(4):
                            ps, rps = PS()
                            TR(ps[:, 0:128], ckd[:, st, hg, :], ident_f[:], [r_ckd], [rps])
                            CP(K2[:, hg, 0, st * 128:(st + 1) * 128], ps[:, 0:128], [rps], [r_K2])
                        fw.dma(pool, Ve[:, 0, st, :, 0:64], cv_in[l, st * 128:(st + 1) * 128, :].rearrange("s (g d) -> s g d", g=4), writes=[r_V])
                        fw.dma(pool, Vo[:, 0, st, :, 64:128], cv_in[l, st * 128:(st + 1) * 128, :].rearrange("s (g d) -> s g d", g=4), writes=[r_V])
                def qk_norm(k, src, rsrc, gvec, n):
                    TT(sqs[k][:, 0:n], src, src, ALU.mult, rsrc, [r_sqs[k]])
                    ps, rps = PS()
                    MM(ps[:, 0:n], bones[:], sqs[k][:, 0:n], True, True, [r_sqs[k]], [rps])
                    ACT(rstds[k][:, 0:n], ps[:, 0:n], AF.Ln, [rps], [r_rstds[k]], bias=EPS)
                    ACT(rstds[k][:, 0:n], rstds[k][:, 0:n], AF.Exp, [r_rstds[k]], [r_rstds[k]], scale=-0.5)
                    STT(knfs[k][:, 0:n], src, gvec, rstds[k][:, 0:n], ALU.mult, ALU.mult, rsrc + [r_gv, r_rstds[k]], [r_knfs[k]])
                def rope_load(n, tok0, ownt=False, q=False):
                    c_, s_, r_ = (cosq, sinq, r_csq) if q else (cosl, sinl, r_cs)
                    fw.dma(sp, c_[:, 0:n], (ropec_own if ownt else CT["ropec"])[:, tok0:tok0 + n], writes=[r_])
                    fw.dma(sp, s_[:, 0:n], (ropes_own if ownt else CT["ropes"])[:, tok0:tok0 + n], writes=[r_])
                def rope(k, n, outs, q=False):
                    c_, s_, r_ = (cosq, sinq, r_csq) if q else (cosl, sinl, r_cs)
                    CP(knbs[k][:, 0:n], knfs[k][:, 0:n], [r_knfs[k]], [r_knbs[k]])
                    ps, rps = PS()
                    MM(ps[:, 0:n], rmat[:], knbs[k][:, 0:n], True, True, [r_knbs[k]], [rps])
                    TT(t1s[k][:, 0:n], ps[:, 0:n], s_[:, 0:n], ALU.mult, [rps, r_], [r_t1s[k]])
                    TT(knfs[k][:, 0:n], knfs[k][:, 0:n], c_[:, 0:n], ALU.mult, [r_knfs[k], r_], [r_knfs[k]])
                    for (psl, out_ap, rout) in outs:
                        TT(out_ap, knfs[k][psl, 0:n], t1s[k][psl, 0:n], ALU.add, [r_knfs[k], r_t1s[k]], rout)
                if g.latent:
                    for dup in range(2):
                        fw.dma(sp, K2[dup * 64:(dup + 1) * 64, :, 0, koff:koff + L], g.KR.rearrange("b (h d) t -> d (b h) t", h=2), reads=[g.rKR], writes=[r_K2])
                    src_v = g.VT[:, :, :].rearrange("p st (g d) -> p (st g) d", g=4)
                    nst0 = koff // 128
                    fw.dma(sp, Ve[:, 0, nst0:nst0 + L // 128, :, 0:64].rearrange("p st g d -> p (st g) d"), src_v, reads=[g.rVT], writes=[r_V])
                    fw.dma(sp, Vo[:, 0, nst0:nst0 + L // 128, :, 64:128].rearrange("p st g d -> p (st g) d"), src_v, reads=[g.rVT], writes=[r_V])
                for (s, t0, n) in ([] if g.latent else seq_tiles(g, NT)):
                    g0 = s * L + t0
                    for hg in range(4):
                        if g.latent:
                            for dup in range(2):
                                fw.dma(sp, K2[dup * 64:(dup + 1) * 64, hg, s, koff + t0:koff + t0 + n],
                                       g.KR[hg // 2][(hg % 2) * 64:(hg % 2) * 64 + 64, g0:g0 + n], reads=[g.rKR], writes=[r_K2])
                            continue
                        k = setc[0] % NSET; setc[0] += 1
                        kraw = kraws[k]; r_kraw = r_kraws[k]; knf = knfs[k]; r_knf = r_knfs[k]
                        for dup in range(2):
                            fw.dma(sp, kraw[dup * 64:(dup + 1) * 64, 0:n], g.UKV[hg // 2][(hg % 2) * 64:(hg % 2) * 64 + 64, g0:g0 + n],
                                   reads=[g.rUKV[hg // 2]], writes=[r_kraw])
                        qk_norm(k, kraw[:, 0:n], [r_kraw], gk[:, 0:1], n)
                        if g.latent:
                            if hg == 0: rope_load(n, t0)
                            rope(k, n, [(slice(0, 128), K2[:, hg, s, koff + t0:koff + t0 + n], [r_K2])])
                        else:
                            CP(K2[:, hg, s, t0:t0 + n], knf[:, 0:n], [r_knf], [r_K2])
                            for sub in range(n // 128):
                                ps, rps = PS()
                                TR(ps[:, 0:64], knf[0:64, sub * 128:(sub + 1) * 128], ident_f[0:64, 0:64], [r_knf], [rps])
                                CP(otm[:, hg * 64:(hg + 1) * 64], ps[:, 0:64], [rps], [r_otm]) if False else None
                                ACT(otm[:, 0:64], ps[:, 0:64], AF.Copy, [rps], [r_otm])
                                fw.dma(sp, nk_out[s, l, t0 + sub * 128:t0 + (sub + 1) * 128, hg * 64:(hg + 1) * 64], otm[:, 0:64], reads=[r_otm], writes=[r_nk])
                    if g.latent:
                        st0 = (koff + t0) // 128; nsub = n // 128
                        src_v = g.VT[:, g0 // 128:g0 // 128 + nsub, :].rearrange("p st (g d) -> p (st g) d", g=4)
                        fw.dma(sp, Ve[:, s, st0:st0 + nsub, :, 0:64].rearrange("p st g d -> p (st g) d"), src_v, reads=[g.rVT], writes=[r_V])
                        fw.dma(sp, Vo[:, s, st0:st0 + nsub, :, 64:128].rearrange("p st g d -> p (st g) d"), src_v, reads=[g.rVT], writes=[r_V])
                        continue
                    fw.dma(sp, vraw[:, :, 0:n], g.UKV[2:4, :, g0:g0 + n].rearrange("j p w -> p j w"), reads=g.rUKV[2:4], writes=[r_vraw])
                    for sub in range(n // 128):
                        st = (koff + t0) // 128 + sub
                        for vb in range(2):
                            ps, rps = PS()
                            TR(ps[:, 0:128], vraw[:, vb, sub * 128:(sub + 1) * 128], ident_f[:], [r_vraw], [rps])
                            CP(Ve[:, s, st, 2 * vb:2 * vb + 2, 0:64], ps[:, 0:128].rearrange("p (g d) -> p g d", g=2), [rps], [r_V])
                            ACT(Vo[:, s, st, 2 * vb:2 * vb + 2, 64:128], ps[:, 0:128].rearrange("p (g d) -> p g d", g=2), AF.Copy, [rps], [r_V])
                            if not g.latent:
                                ACT(otm[:, 0:128], ps[:, 0:128], AF.Copy, [rps], [r_otm])
                                fw.dma(sp, nv_out[s, l, t0 + sub * 128:t0 + (sub + 1) * 128, vb * 128:(vb + 1) * 128], otm[:, 0:128], reads=[r_otm], writes=[r_nv])
                qraw = [sb("qraw%d" % i, [128, 8, NT], BF16) for i in range(2)]; r_qraw = [Reg("qraw%d" % i) for i in range(2)]
                sdg = [sb("sdg0", [128, 8, NT], BF16)] * 2; r_sdg = [Reg("sdg0")] * 2
                Qz = sb("Qz", [128, 16, NT], BF16); r_Qz = Reg("Qz")
                MS(Qz[:], 0.0, [r_Qz])
                pT2 = [sb("pT2%d" % i, [128, 2, NT], BF16) for i in range(3)]; r_pT2 = [Reg("pT2%d" % i) for i in range(3)]
                pob = [ph.enter_context(nc.psum_tensor(uq("po"), [128, 512], F32)) for i in range(2)]; r_pob = [Reg("po%d" % i) for i in range(2)]
                dn = sb("dn", [128, NT], F32); r_dn = Reg("dn")
                att = sb("att", [128, NT], F32); r_att = Reg("att")
                yd = [sb("yd0", [128, 8, NT], BF16)] * 2; r_yd = [Reg("yd0")] * 2
                qtiles = own_tiles(own) if own else seq_tiles(g, NT)
                def qcols(it):
                    s_, t0_, n_ = qtiles[it]
                    return (t0_ if own else s_ * L + t0_)
                for it, (s_, t0_, n_) in enumerate([] if g.latent else qtiles):
                    cq0 = 128 + t0_ if own else s_ * L + t0_
                    qb_ = it % 2
                    fw.dma(sp, qraw[qb_][:, :, 0:n_], Us[36:44, :, cq0:cq0 + n_].rearrange("j p w -> p j w"), reads=rUs[36:44], writes=[r_qraw[qb_]])
                    if g.latent: rope_load(n_, t0_, own, q=True)
                    for qb in range(8):
                        k = setc[0] % NSET; setc[0] += 1
                        qk_norm(k, qraw[qb_][:, qb, 0:n_], [r_qraw[qb_]], gq[:, 0:1], n_)
                        if g.latent:
                            rope(k, n_, [(slice(0, 128), qsts[k][:, 0:n_], [r_qsts[k]])], q=True)
                        else:
                            CP(qsts[k][:, 0:n_], knfs[k][:, 0:n_], [r_knfs[k]], [r_qsts[k]])
                        fw.dma(sp, g.QR[qb][:, qcols(it):qcols(it) + n_], qsts[k][:, 0:n_], reads=[r_qsts[k]], writes=[g.rQR])
                def load_tile(it):
                    s_, t0_, n_ = qtiles[it]
                    c0 = qcols(it); cq0 = 128 + t0_ if own else s_ * L + t0_
                    for hh in range(2):
                        fw.dma(sp, Qz[hh * 64:(hh + 1) * 64, hh:16:2, 0:n_], g.QR[:, hh * 64:(hh + 1) * 64, c0:c0 + n_].rearrange("q p w -> p q w"),
                               reads=[g.rQR], writes=[r_Qz])
                    fw.dma(sp, sdg[0][:, :, 0:n_], Us[48:56, :, cq0:cq0 + n_].rearrange("j p w -> p j w"), reads=rUs[48:56], writes=[r_sdg[0]])
                for it, (s, t0, n) in enumerate(qtiles):
                    b_ = it % 2; g0 = s * L + t0
                    if own: g0 = t0
                    load_tile(it)
                    r_Qt = [r_Qz] * 8
                    npair = nst // 2
                    items = [(h, pr_) for h in range(16) for pr_ in range(npair)]
                    def emit_qk(idx):
                        h, pr_ = items[idx]
                        qb = h // 2; base = (h % 2) * 64; hg = h // 4
                        sc_ = sc[idx % 3]; rsc_ = r_sc[idx % 3]
                        for k2 in range(2):
                            st = pr_ * 2 + k2
                            MM(sc_[:, k2 * 512:k2 * 512 + n], K2[:, hg, s, st * 128:(st + 1) * 128], Qz[:, h, 0:n],
                               True, True, [r_K2, r_Qt[qb]], [rsc_])
                    def emit_pv(idx):
                        h, pr_ = items[idx]
                        qb = h // 2; base = (h % 2) * 64; hg = h // 4
                        Vt = Ve if base == 0 else Vo
                        sc_ = sc[idx % 3]; rsc_ = r_sc[idx % 3]
                        pi = idx % 3
                        po = pob[h % 2]; rpo = r_pob[h % 2]
                        ACT(pT2[pi][:, :, 0:n], sc_[:].rearrange("p (k w) -> p k w", k=2)[:, :, 0:n], AF.Exp, [rsc_], [r_pT2[pi]])
                        for k2 in range(2):
                            st = pr_ * 2 + k2
                            MM(po[:, 0:n], Vt[:, s, st, hg, :], pT2[pi][:, k2, 0:n], st == 0, st == nst - 1, [r_V, r_pT2[pi]], [rpo])
                        if pr_ == npair - 1:
                            nb_ = slice(base, base + 64); db_ = slice(64 - base, 128 - base)
                            CP(dn[nb_, 0:n], po[db_, 0:n], [rpo], [r_dn])
                            RCP(dn[nb_, 0:n], dn[nb_, 0:n], [r_dn], [r_dn])
                            TT(att[nb_, 0:n], po[nb_, 0:n], dn[nb_, 0:n], ALU.mult, [rpo, r_dn], [r_att])
                            TT(yd[b_][nb_, qb, 0:n], att[nb_, 0:n], sdg[b_][nb_, qb, 0:n], ALU.mult, [r_att, r_sdg[b_]], [r_yd[b_]])
                    for idx in range(len(items) + 2):
                        if idx < len(items): emit_qk(idx)
                        if idx >= 2: emit_pv(idx - 2)
                    fw.dma(sp, g.YIN[12:20, :, g0:g0 + n].rearrange("j p w -> p j w"), yd[b_][:, :, 0:n], reads=[r_yd[b_]], writes=g.rYIN[12:20])
                fw.barrier()

        def sin_reduce(ph_sb, x, rx, n, tag):
            ki, kf, mk, rk = ph_sb
            TS(ki[0:64, 0:n], x, 1.0 / TWO_PI, None, ALU.mult, None, rx, [rk])
            CP(kf[0:64, 0:n], ki[0:64, 0:n], [rk], [rk])
            STT(x, kf[0:64, 0:n], -TWO_PI, x, ALU.mult, ALU.add, [rk] + rx, rx)
            TS(mk[0:64, 0:n], x, math.pi, -TWO_PI, ALU.is_gt, ALU.mult, rx, [rk])
            TT(x, x, mk[0:64, 0:n], ALU.add, rx + [rk], rx)
            TS(mk[0:64, 0:n], x, -math.pi, TWO_PI, ALU.is_lt, ALU.mult, rx, [rk])
            TT(x, x, mk[0:64, 0:n], ALU.add, rx + [rk], rx)

        def hyena_filters(g, l):
            L = g.L; N2 = 2 * L; nm = g.name
            with contextlib.ExitStack() as ph:
                def sb(name, shape, dt): return ph.enter_context(nc.sbuf_tensor(uq(name), list(shape), dt))
                psum_std(ph, nb=4, with_pst=False) if g.latent else psum_std(ph)
                w1 = sb("w1", [33, 64], F32); w2 = sb("w2", [64, 64], F32); r_w = Reg("hw")
                b1 = sb("b1", [64, 1], F32); b2 = sb("b2", [64, 1], F32); fr = sb("fr", [64, 1], F32)
                w3 = sb("w3", [64, 2048], BF16); b3 = sb("b3", [128, 16], F32); ndl = sb("ndl", [128, 4], F32)
                h2 = sb("h2", [64, N2], BF16); r_h2 = Reg("h2")
                mlp_scope = contextlib.ExitStack()
                def sbm(name, shape, dt): return mlp_scope.enter_context(nc.sbuf_tensor(uq(name), list(shape), dt))
                trow = sb("trow", [128, N2], F32); r_trow = Reg("trow")
                decs = [sb("dec%d" % i, [128, 512], F32) for i in range(2)]; r_decs = [Reg("dec%d" % i) for i in range(2)]
                fts = [sb("ft%d" % i, [128, 512], F32) for i in range(2)]; r_fts = [Reg("ft%d" % i) for i in range(2)]
                sqjf = sb("sqjf", [128, 512], F32); r_sqjf = Reg("sqjf")
                fb = sb("fb", [128, N2], BF16); r_fb = Reg("fb")
                ssq = sb("ssq", [128, 16], F32); r_ssq = Reg("ssq"); rs = sb("rs", [128, 1], F32)
                zf = sbm("zf", [33, 512], F32); r_zf = Reg("zf")
                h1 = sbm("h1", [64, 512], F32); r_h1 = Reg("h1")
                ki = sbm("ki", [64, 512], I32); kf = sbm("kf", [64, 512], F32); mk = sbm("mk", [64, 512], F32); rk = Reg("rk")
                fw.dma(sp, w1[:], W['hy_w1'][l], writes=[r_w]); fw.dma(sp, w2[:], W['hy_w2'][l], writes=[r_w])
                for (t_, nm_) in ((b1, 'hy_b1'), (b2, 'hy_b2'), (fr, 'hy_freq')):
                    fw.dma(sp, t_[:], W[nm_][l].rearrange("(d o) -> d o", o=1), writes=[r_w])
                fw.dma(pool, w3[:], W['hy_w3'][l], writes=[r_w])
                colvec(b3[:], W['hy_b3'][l], 16, [r_w])
                fw.dma(sp, ndl[:], CT["negdelta"][:, :], writes=[r_w])
                fw.dma(sp, trow[:], CT["trow_" + nm][0:1, :].partition_broadcast(128), writes=[r_trow])
                for ch in range(N2 // 512):
                    fw.dma(sp, zf[:], CT["zf_" + nm][:, ch * 512:(ch + 1) * 512], writes=[r_zf])
                    ps, rps = PS()
                    MM(ps[0:64, :], w1[:], zf[:], True, True, [r_w, r_zf], [rps])
                    TS(h1[:], ps[0:64, :], b1[:, 0:1], fr[:, 0:1], ALU.add, ALU.mult, [rps, r_w], [r_h1])
                    sin_reduce((ki, kf, mk, rk), h1[:], [r_h1], 512, "a")
                    ACT(h1[:], h1[:], AF.Sin, [r_h1], [r_h1])
                    ps, rps = PS()
                    MM(ps[0:64, :], w2[:], h1[:], True, True, [r_w, r_h1], [rps])
                    TS(h1[:], ps[0:64, :], b2[:, 0:1], fr[:, 0:1], ALU.add, ALU.mult, [rps, r_w], [r_h1])
                    sin_reduce((ki, kf, mk, rk), h1[:], [r_h1], 512, "b")
                    ACT(h2[:, ch * 512:(ch + 1) * 512], h1[:], AF.Sin, [r_h1], [r_h2])
                fw.barrier()
                mlp_scope.close()
                if g.latent:
                    Dms = [sb("Dm%d" % i, [128, 64, 128], BF16) for i in range(2)]; r_Ds = [[Reg("D%d" % i)] for i in range(2)]
                    hcnt = [0]
                    Abuf = sb("Abuf", [128, 64, 3, 64], BF16); r_A = Reg("A")
                    Xb = sb("Xb", [128, 3, 64, 64], BF16); r_X = [Reg("X%d" % i) for i in range(4)]
                    wa_t = sb("wa_t", [128, 192], BF16); r_tab = Reg("tab")
                    tbc = sb("tbc", [128, 64, 128], BF16); tbs = sb("tbs", [128, 64, 128], BF16)
                    pa = [ph.enter_context(nc.psum_tensor(uq("pa"), [128, 1024], F32)) for i in range(2)]; r_pa = [Reg("pa%d" % i) for i in range(2)]
                    fw.dma(sp, wa_t[:], CT["wa"][:, :], writes=[r_tab])
                    fw.dma(sp, tbc[:], CT["tb_c"][:, :, :], writes=[r_tab]); fw.dma(sp, tbs[:], CT["tb_s"][:, :, :], writes=[r_tab])
                else:
                    Zt = sb("Zt", [128, 4, 128], BF16); r_Zt = Reg("Zt")
                    fpc = sb("fpc", [128, 4, 512], BF16); fps = sb("fps", [128, 4, 512], BF16); r_tab = Reg("tab")
                    Xp = sb("Xp", [128, 2, 4, 128], BF16); r_X = Reg("X")
                    fw.dma(sp, fpc[:], CT["fp_c"][:, :, :], writes=[r_tab]); fw.dma(sp, fps[:], CT["fp_sn"][:, :, :], writes=[r_tab])
                for o in range(2):
                    for cb in range(4):
                        ssi = 0
                        for half in range(2):
                            col0 = half * 1024 + o * 512 + cb * 128
                            bcol = col0 // 128
                            for ch in range(L // 512 if L >= 512 else 1):
                                n = min(512, L); p0 = half * L + ch * 512
                                kq = ssi % 2
                                dec = decs[kq]; r_dec = r_decs[kq]; ft = fts[kq]; r_ft = r_fts[kq]
                                ps, rps = PS()
                                MM(ps[:, 0:n], w3[:, col0:col0 + 128], h2[:, p0:p0 + n], True, True, [r_w, r_h2], [rps])
                                ACT(dec[:, 0:n], trow[:, p0:p0 + n], AF.Exp, [r_trow, r_w], [r_dec], scale=ndl[:, cb:cb + 1])
                                STT(ft[:, 0:n], ps[:, 0:n], b3[:, bcol:bcol + 1], dec[:, 0:n], ALU.add, ALU.mult, [rps, r_w, r_dec], [r_ft])
                                CP(fb[:, p0:p0 + n], ft[:, 0:n], [r_ft], [r_fb])
                                ACT(sqjf[:, 0:n], ft[:, 0:n], AF.Square, [r_ft], [r_sqjf, r_ssq], accum_out=ssq[:, ssi:ssi + 1]); ssi += 1
                        fw.op(dve, lambda e: e.reduce_sum(out=rs[:], in_=ssq[:, 0:ssi], axis=mybir.AxisListType.X), [r_ssq], [r_ssq])
                        ACT(rs[:], rs[:], AF.Sqrt, [r_ssq], [r_ssq], bias=EPS)
                        RCP(rs[:], rs[:], [r_ssq], [r_ssq])
                        TS(fb[:], fb[:], rs[:, 0:1], None, ALU.mult, None, [r_fb, r_ssq], [r_fb])
                        if g.latent:
                            fw.dma(sp, g.ZB[:, :], fb[:], reads=[r_fb], writes=[g.rZB])
                            for half in range(2):
                                Dm = Dms[(hcnt[0] + half) % 2]; r_D = r_Ds[(hcnt[0] + half) % 2]
                                fw.dma(sp, Dm[0:64], g.ZB[half * 64:(half + 1) * 64, :].rearrange("c (j a) -> j c a", a=128), reads=[g.rZB], writes=r_D)
                            for half in range(2):
                                Dm = Dms[(hcnt[0] + half) % 2]; r_D = r_Ds[(hcnt[0] + half) % 2]
                                fft_fwd_sample(Dm, r_D, 64, Abuf, r_A, Xb, r_X, wa_t, r_tab, tbc, tbs, pa, r_pa)
                                for ri in range(2):
                                    fw.dma(sp, g.HS[o, cb, half, ri], Xb[:, 1 + ri].rearrange("p g c -> p (g c)"), reads=r_X, writes=[g.rHS])
                        else:
                            for tt in range(4):
                                TR(PSX.pst[:, tt * 128:(tt + 1) * 128], fb[:, tt * 128:(tt + 1) * 128], ident_b[:], [r_fb], [PSX.rpst])
                            ACT(Zt[:], PSX.pst[:, 0:512].rearrange("p (t c) -> p t c", t=4), AF.Copy, [PSX.rpst], [r_Zt])
                            for ri, tab in ((0, fpc), (1, fps)):
                                ps, rps = PS()
                                for kt in range(4):
                                    for tt in range(4):
                                        MM(ps[:, kt * 128:(kt + 1) * 128], tab[:, tt, kt * 128:(kt + 1) * 128], Zt[:, tt, :], tt == 0, tt == 3, [r_tab, r_Zt], [rps])
                                ACT(Xp[:, ri], ps[:].rearrange("p (k c) -> p k c", k=4), AF.Copy, [rps], [r_X])
                                fw.dma(sp, g.HS[o, cb, ri], Xp[:, ri].rearrange("p k c -> p (k c)"), reads=[r_X], writes=[g.rHS])
                fw.barrier()

        def fft_fwd_sample(Dm, r_D, J, Abuf, r_A, Xb, r_X, wa_t, r_tab, tbc, tbs, pa, r_pa):
            for i4, c0 in enumerate(range(0, 64, 4)):
                pa_ = pa[i4 % 2]; rpa_ = r_pa[i4 % 2]
                for ci in range(4):
                    o_ = (ci // 2) * 512 + (ci % 2) * 192
                    MM(pa_[:, o_:o_ + 192], Dm[0:J, c0 + ci, :], wa_t[0:J, :], True, True, r_D + [r_tab], [rpa_])
                src_ = pa_[:].rearrange("p (b x) -> p b x", b=2)[:, :, 0:384]
                dst_ = Abuf[:, c0:c0 + 4].rearrange("p (b c) r g -> p b (c r g)", b=2)
                if i4 % 2:
                    CP(dst_, src_, [rpa_], [r_A])
                else:
                    ACT(dst_, src_, AF.Copy, [rpa_], [r_A])
            for g4 in range(16):
                pq, rpq = PS()
                for gi in range(4):
                    gg = g4 * 4 + gi
                    MM(pq[:, gi * 128:(gi + 1) * 128], tbc[:, gg, :], Abuf[:, :, 0:2, gg].rearrange("p c r -> p r c"), True, False, [r_tab, r_A], [rpq])
                    MM(pq[:, gi * 128:(gi + 1) * 128], tbs[:, gg, :], Abuf[:, :, 1:3, gg].rearrange("p c r -> p r c"), False, True, [r_tab, r_A], [rpq])
                pv4 = pq[:].rearrange("p (g r c) -> p g r c", g=4, r=2)
                ACT(Xb[:, 1, g4 * 4:(g4 + 1) * 4, :], pv4[:, :, 0, :], AF.Copy, [rpq], [r_X[g4 // 4]] + (r_D if g4 == 0 else []))
                CP(Xb[:, 2, g4 * 4:(g4 + 1) * 4, :], pv4[:, :, 1, :], [rpq], [r_X[g4 // 4]] + (r_D if g4 == 0 else []))

        def hyena(g, l, own=False):
            L = g.L; T = g.T
            with contextlib.ExitStack() as ph:
                def sb(name, shape, dt): return ph.enter_context(nc.sbuf_tensor(uq(name), list(shape), dt))
                psum_std(ph, nb=4, with_pst=False) if g.latent else psum_std(ph)
                shw = sb("shw", [128, 12, 3], F32); shb = sb("shb", [128, 12], F32); skp = sb("skp", [128, 2, 4], F32); r_hv = Reg("hv")
                for k in range(3):
                    fw.dma(sp, shw[:, :, k], W['hy_short_w'][l][k, :].rearrange("(b p) -> p b", p=128), writes=[r_hv], allow_slow_non_contiguous=True)
                colvec(shb[:], W['hy_short_b'][l], 12, [r_hv])
                for o in range(2):
                    fw.dma(sp, skp[:, o, :], W['hy_skip'][l][o, :].rearrange("(b p) -> p b", p=128), writes=[r_hv], allow_slow_non_contiguous=True)
                raw = sb("raw", [128, g.nseq, L + 2], BF16); r_raw = Reg("raw")
                bufA = sb("bufA", [128, T], F32); bufB = sb("bufB", [128, T], F32); r_bA = Reg("bufA"); r_bB = Reg("bufB")
                xg = sb("xg", [128, T], F32); r_xg = Reg("xg")
                zb = sb("zbh", [128, T], BF16); r_zb = Reg("zbh")
                sbg = raw[:, 0, 0:T] if g.latent else sb("sbg", [128, T], BF16); r_sbg = r_raw if g.latent else Reg("sbg")
                yo = zb; r_yo = r_zb
                if g.latent:
                    DA = sb("DA", [128, 20480], BF16)
                    r_Dlo = Reg("Dlo"); r_Dhi = Reg("Dhi"); r_D = [r_Dlo, r_Dhi]
                    Abuf = DA[:, 8192:20480].rearrange("p (c r g) -> p c r g", r=3, g=64); r_A = Reg("A")
                    Pb = DA[:, 0:8192].rearrange("p (c n) -> p c n", n=128); r_Plo = Reg("Plo"); r_Phi = Reg("Phi"); r_P = [r_Plo, r_Phi]
                    Xb = sb("Xb", [128, 3, 64, 64], BF16); r_X = [Reg("X%d" % i) for i in range(4)]
                    Dm = Xb[:].rearrange("p r g c -> p (r g c)")[:, 0:8192].rearrange("p (c a) -> p c a", a=128)
                    Hb = [sb("Hb%d" % i, [128, 2, 64 * 64], BF16) for i in range(1)]; r_H = Reg("H")
                    QC = 1024
                    tq = [sb("tq%d" % i, [128, QC], BF16) for i in range(4)]; r_tq = [Reg("tq%d" % i) for i in range(4)]
                    wa_t = sb("wa_t", [128, 192], BF16); r_tab = Reg("tab")
                    tbc = sb("tbc", [128, 64, 128], BF16); tbs = sb("tbs", [128, 64, 128], BF16)
                    pa = [ph.enter_context(nc.psum_tensor(uq("pa"), [128, 1024], F32)) for i in range(2)]; r_pa = [Reg("pa%d" % i) for i in range(2)]
                    fw.dma(sp, tbc[:], CT["tb_c"][:, :, :], writes=[r_tab]); fw.dma(sp, tbs[:], CT["tb_s"][:, :, :], writes=[r_tab])
                    tpc = sb("tpc", [128, 128], BF16); tps = sb("tps", [128, 128], BF16)
                    tcst = sb("tcst", [128, 128, 32], BF16)
                    fw.dma(sp, wa_t[:], CT["wa"][:, :], writes=[r_tab]); fw.dma(sp, tpc[:], CT["tbp_c"][:, :], writes=[r_tab])
                    fw.dma(sp, tps[:], CT["tbp_s"][:, :], writes=[r_tab]); fw.dma(sp, tcst[:], CT["tc_st"][:, :, :], writes=[r_tab])
                else:
                    Zt = sb("Zt", [128, 2, 2, 128], BF16); r_Zt = Reg("Zt")
                    fpc = sb("fpc", [128, 4, 512], BF16); fps = sb("fps", [128, 4, 512], BF16); r_tab = Reg("tab")
                    fic = sb("fic", [128, 4, 256], BF16); fis = sb("fis", [128, 4, 256], BF16)
                    Xp = sb("Xp", [128, 2, 4, 2, 128], BF16); r_X = Reg("X")
                    Hp = sb("Hp", [128, 2, 4, 128], BF16); r_H = Reg("H")
                    tq = [sb("tq%d" % i, [128, 4, 2, 128], BF16) for i in range(4)]; r_tq = [Reg("tq%d" % i) for i in range(4)]
                    for t_, nm_ in ((fpc, "fp_c"), (fps, "fp_sn"), (fic, "fi_c"), (fis, "fi_sn")):
                        fw.dma(sp, t_[:], CT[nm_][:, :, :], writes=[r_tab])

                def short_conv(blk, dst, rdst):
                    MS(raw[:, :, 0:1], 0.0, [r_raw]); MS(raw[:, :, L + 1:L + 2], 0.0, [r_raw])
                    fw.dma(sp, raw[:, :, 1:L + 1], g.U[12 + blk].rearrange("p (s t) -> p s t", s=g.nseq), reads=[g.rU[12 + blk]], writes=[r_raw])
                    d3 = dst.rearrange("p (s t) -> p s t", s=g.nseq)
                    TS(d3, raw[:, :, 0:L], shw[:, blk, 0:1], shb[:, blk:blk + 1], ALU.mult, ALU.add, [r_raw, r_hv], rdst)
                    STT(d3, raw[:, :, 1:L + 1], shw[:, blk, 1:2], d3, ALU.mult, ALU.add, [r_raw, r_hv] + rdst, rdst)
                    STT(d3, raw[:, :, 2:L + 2], shw[:, blk, 2:3], d3, ALU.mult, ALU.add, [r_raw, r_hv] + rdst, rdst)

                def longconv_sample(src, rsrc, dst, rdst, o, cb):
                    CP(zb[:], src, rsrc, [r_zb])
                    fw.dma(sp, g.ZB[:, 0:T], zb[:], reads=[r_zb], writes=[g.rZB])
                    for half in range(2):
                        pass
                        fw.dma(sp, Dm[0:32], g.ZB[half * 64:(half + 1) * 64, 0:T].rearrange("c (j a) -> j c a", a=128), reads=[g.rZB], writes=[r_Dlo, r_Dhi, r_A] + r_X)
                        fw.dma(sp, Hb[0][:], g.HS[o, cb, half].rearrange("r p x -> p r x"), reads=[g.rHS], writes=[r_H])
                        fft_fwd_sample(Dm, r_D, 32, Abuf, r_A, Xb, r_X, wa_t, r_tab, tbc, tbs, pa, r_pa)
                        for q in range(4096 // QC):
                            sl = slice(q * QC, (q + 1) * QC)
                            xr = Xb[:, 1].rearrange("p g c -> p (g c)")[:, sl]; xi = Xb[:, 2].rearrange("p g c -> p (g c)")[:, sl]
                            xn = Xb[:, 0].rearrange("p g c -> p (g c)")[:, sl]
                            hr = Hb[0][:, 0, sl]; hi = Hb[0][:, 1, sl]
                            rxq = r_X[q * QC // 1024]
                            TT(tq[0][:], xr, hr, ALU.mult, [rxq, r_H], [r_tq[0]])
                            TT(tq[1][:], xi, hi, ALU.mult, [rxq, r_H], [r_tq[1]])
                            TT(tq[2][:], xr, hi, ALU.mult, [rxq, r_H], [r_tq[2]])
                            TT(tq[3][:], xi, hr, ALU.mult, [rxq, r_H], [r_tq[3]])
                            TT(xr, tq[0][:], tq[1][:], ALU.subtract, [r_tq[0], r_tq[1]], [rxq])
                            TT(xi, tq[2][:], tq[3][:], ALU.add, [r_tq[2], r_tq[3]], [rxq])
                            STT(xn, tq[2][:], -1.0, tq[3][:], ALU.mult, ALU.subtract, [r_tq[2], r_tq[3]], [rxq])
                        for i4, c0 in enumerate(range(0, 64, 4)):
                            ps, rps = PS()
                            for ci in range(4):
                                MM(ps[:, ci * 128:(ci + 1) * 128], Xb[:, 1:3, :, c0 + ci], tpc[:], True, False, r_X + [r_tab], [rps])
                                MM(ps[:, ci * 128:(ci + 1) * 128], Xb[:, 0:2, :, c0 + ci], tps[:], False, True, r_X + [r_tab], [rps])
                            if i4 % 2:
                                CP(Pb[:, c0:c0 + 4, :], ps[:].rearrange("p (c n) -> p c n", c=4), [rps], r_P)
                            else:
                                ACT(Pb[:, c0:c0 + 4, :], ps[:].rearrange("p (c n) -> p c n", c=4), AF.Copy, [rps], r_P)
                        for n16 in range(8):
                            ps, rps = PS()
                            for bi in range(16):
                                nb = n16 * 16 + bi
                                pv_ = ps[0:64, :].rearrange("p (j b) -> p j b", b=16)[:, :, bi]
                                MM(pv_, Pb[:, :, nb], tcst[:, nb, :], True, True, r_P + [r_tab], [rps])
                            dv = dst[half * 64:(half + 1) * 64, :].rearrange("p (j a) -> p j a", a=128)[:, :, n16 * 16:(n16 + 1) * 16]
                            if n16 % 2:
                                CP(dv, ps[0:64, :].rearrange("p (j b) -> p j b", b=16), [rps], rdst)
                            else:
                                ACT(dv, ps[0:64, :].rearrange("p (j b) -> p j b", b=16), AF.Copy, [rps], rdst)

                def longconv_prompt(src, rsrc, dst, rdst, o, cb):
                    CP(zb[:], src, rsrc, [r_zb])
                    fw.dma(sp, Hp[:], g.HS[o, cb].rearrange("r p (k c) -> p r k c", k=4), reads=[g.rHS], writes=[r_H])
                    for s in range(2):
                        for tt in range(2):
                            TR(PSX.pst[:, (s * 2 + tt) * 128:(s * 2 + tt + 1) * 128], zb[:, s * L + tt * 128:s * L + (tt + 1) * 128], ident_b[:], [r_zb], [PSX.rpst])
                    ACT(Zt[:].rearrange("p t s c -> p s t c"), PSX.pst[:, 0:512].rearrange("p (s t c) -> p s t c", s=2, t=2), AF.Copy, [PSX.rpst], [r_Zt])
                    for ri, tab in ((0, fpc), (1, fps)):
                        for k2 in range(2):
                            ps, rps = PS()
                            for kk in range(2):
                                kt = k2 * 2 + kk
                                for tt in range(2):
                                    MM(ps[:, kk * 256:(kk + 1) * 256], tab[:, tt, kt * 128:(kt + 1) * 128], Zt[:, tt].rearrange("p s c -> p (s c)"),
                                       tt == 0, tt == 1, [r_tab, r_Zt], [rps])
                            ACT(Xp[:, ri, k2 * 2:k2 * 2 + 2].rearrange("p k s c -> p (k s c)"), ps[:], AF.Copy, [rps], [r_X])
                    for s in range(2):
                        xr = Xp[:, 0, :, s, :]; xi = Xp[:, 1, :, s, :]; hr = Hp[:, 0]; hi = Hp[:, 1]
                        TT(tq[0][:, :, s, :], xr, hr, ALU.mult, [r_X, r_H], [r_tq[0]])
                        TT(tq[1][:, :, s, :], xi, hi, ALU.mult, [r_X, r_H], [r_tq[1]])
                        TT(tq[2][:, :, s, :], xr, hi, ALU.mult, [r_X, r_H], [r_tq[2]])
                        TT(tq[3][:, :, s, :], xi, hr, ALU.mult, [r_X, r_H], [r_tq[3]])
                    TT(Xp[:, 0], tq[0][:], tq[1][:], ALU.subtract, [r_tq[0], r_tq[1]], [r_X])
                    TT(Xp[:, 1], tq[2][:], tq[3][:], ALU.add, [r_tq[2], r_tq[3]], [r_X])
                    for s in range(2):
                        ps, rps = PS()
                        for kt in range(4):
                            MM(ps[:, 0:256], Xp[:, 0, kt, s, :], fic[:, kt, :], kt == 0, False, [r_X, r_tab], [rps])
                            MM(ps[:, 0:256], Xp[:, 1, kt, s, :], fis[:, kt, :], False, kt == 3, [r_X, r_tab], [rps])
                        ACT(dst[:, s * L:(s + 1) * L], ps[:, 0:256], AF.Copy, [rps], rdst)

                longconv = longconv_sample if g.latent else longconv_prompt
                for cb in range(4):
                    short_conv(cb, bufA[:], [r_bA])
                    short_conv(4 + cb, xg[:], [r_xg])
                    longconv(bufA[:], [r_bA], bufB[:], [r_bB], 0, cb)
                    STT(bufB[:], bufA[:], skp[:, 0, cb:cb + 1], bufB[:], ALU.mult, ALU.add, [r_bA, r_bB, r_hv], [r_bB])
                    TT(bufB[:], bufB[:], xg[:], ALU.mult, [r_bB, r_xg], [r_bB])
                    short_conv(8 + cb, xg[:], [r_xg])
                    longconv(bufB[:], [r_bB], bufA[:], [r_bA], 1, cb)
                    STT(bufA[:], bufB[:], skp[:, 1, cb:cb + 1], bufA[:], ALU.mult, ALU.add, [r_bA, r_bB, r_hv], [r_bA])
                    TT(bufA[:], bufA[:], xg[:], ALU.mult, [r_bA, r_xg], [r_bA])
                    fw.dma(sp, sbg[:], g.U[24 + cb], reads=[g.rU[24 + cb]], writes=[r_sbg])
                    TT(yo[:], bufA[:], sbg[:], ALU.mult, [r_bA, r_sbg], [r_yo])
                    if own:
                        fw.dma(sp, g.YB[cb].rearrange("(q p) w -> p q w", p=128), yo[:].rearrange("p (q w) -> p q w", q=4), reads=[r_yo], writes=[g.rYB])
                    else:
                        fw.dma(sp, g.YIN[4 + cb], yo[:], reads=[r_yo], writes=[g.rYIN[4 + cb]])
                fw.barrier()

        def tail(g, l, x_src, r_xsrc, x_dst, r_xdst, own=False):
            T = 1024 if own else g.T
            with contextlib.ExitStack() as ph:
                def sb(name, shape, dt): return ph.enter_context(nc.sbuf_tensor(uq(name), list(shape), dt))
                psum_std(ph)
                wo = sb("wo", [128, 20, D], BF16); r_wo = Reg("wo")
                wout = sb("wout", [128, 8, D], BF16); r_wout = Reg("wout")
                gate = sb("gate", [128, D], F32); r_gate = Reg("gate")
                badab = sb("badab", [128, D], F32); r_bada = Reg("bada")
                ccol = sb("ccol", [128, 8], F32); r_cc = Reg("cc")
                crep = sb("crep", [128, 8, 128], BF16); r_crep = Reg("crep")
                wsl = [sb("wsl%d" % i, [128, 8, 512], BF16) for i in range(2)]; r_wsl = [Reg("wsl%d" % i) for i in range(2)]
                yin = [sb("yin%d" % i, [128, 20, 512], BF16) for i in range(2)]; r_yin = [Reg("yin%d" % i) for i in range(2)]
                gmf = [sb("gmf%d" % i, [128, 4, 512], BF16) for i in range(3)]; r_gmf = [Reg("gmf%d" % i) for i in range(3)]
                mrg = sb("mrg", [128, 8, 512], F32); r_mrg = [Reg("mrg%d" % i) for i in range(8)]
                mrb = sb("mrb", [128, 8, 512], BF16); r_mrb = [Reg("mrb%d" % i) for i in range(8)]
                tmp = sb("tmpt", [128, 512], F32); r_tmp = Reg("tmpt")
                xt = [sb("xtt%d" % i, [128, D], F32) for i in range(2)]; r_xt = [Reg("xtt%d" % i) for i in range(2)]
                xo = [sb("xo%d" % i, [128, D], F32) for i in range(2)]; r_xo = [Reg("xo%d" % i) for i in range(2)]
                for bi, nm_ in ((0, 'wo_conv'), (4, 'wo_hyena'), (8, 'wo_pool')):
                    fw.dma(pool, wo[:, bi:bi + 4, :], W[nm_][l].rearrange("(kt p) n -> p kt n", p=128), writes=[r_wo])
                fw.dma(pool, wo[:, 12:20, :], W['wo_attn'][l].rearrange("(kt p) n -> p kt n", p=128), writes=[r_wo])
                fw.dma(pool, wout[:], W['w_out'][l].rearrange("(kt p) n -> p kt n", p=128), writes=[r_wout])
                fw.dma(sp, ccol[:], cvec[g.crow, :].rearrange("(b p) -> p b", p=128), writes=[r_cc], allow_slow_non_contiguous=True)
                fw.dma(sp, badab[:], W['b_ada'][l:l + 1, 2 * D:3 * D].partition_broadcast(128), writes=[r_bada])
                ACT(ccol[:], ccol[:], AF.Silu, [r_cc], [r_cc])
                CP(crep[:], ccol[:].unsqueeze(2).broadcast_to([128, 8, 128]), [r_cc], [r_crep])
                wada = W['w_ada'][l].rearrange("(kt p) n -> p kt n", p=128)
                for cb in range(2):
                    fw.dma(pool, wsl[cb][:], wada[:, :, 2 * D + cb * 512:2 * D + (cb + 1) * 512], writes=[r_wsl[cb]])
                    ps, rps = PS()
                    for kt in range(8):
                        MM(ps[:], crep[:, kt, :], wsl[cb][:, kt, :], kt == 0, kt == 7, [r_crep, r_wsl[cb]], [rps])
                    TT(gate[:, cb * 512:(cb + 1) * 512], ps[:], badab[:, cb * 512:(cb + 1) * 512], ALU.add, [rps, r_bada], [r_gate])
                kts = (4, 4, 4, 8); kb0 = (0, 4, 8, 12)
                if own:
                    ybo = sb("ybo", [128, 4, 1024], BF16); r_ybo = Reg("ybo")
                    for cb in range(4):
                        fw.idma(pool, ybo[:, cb, :], g.YB[cb][:, :], idxQ[:, 0:1], reads=[g.rYB], writes=[r_ybo])
                for tt in range(T // 512):
                    b_ = tt % 2; t0 = tt * 512
                    fw.dma(sp, yin[b_][:], g.YIN[:, :, t0:t0 + 512].rearrange("j p w -> p j w"), reads=g.rYIN, writes=[r_yin[b_]])
                    if own:
                        CP(yin[b_][:, 4:8, :], ybo[:, :, t0:t0 + 512], [r_ybo], [r_yin[b_]])
                    for f in range(8):
                        gi_ = (tt * 8 + f) % 3
                        if own:
                            fw.dma(sp, gmf[gi_][:], g.UE[56 + f:88:8, :, 128 + t0:128 + t0 + 512].rearrange("j p w -> p j w"), reads=g.rUE[56 + f:88:8], writes=[r_gmf[gi_]])
                        else:
                            fw.dma(sp, gmf[gi_][:], g.U[56 + f:88:8, :, t0:t0 + 512].rearrange("j p w -> p j w"), reads=g.rU[56 + f:88:8], writes=[r_gmf[gi_]])
                        for br in range(4):
                            ps, rps = PS()
                            for kt in range(kts[br]):
                                MM(ps[:], wo[:, kb0[br] + kt, f * 128:(f + 1) * 128], yin[b_][:, kb0[br] + kt, :], kt == 0, kt == kts[br] - 1,
                                   [r_wo, r_yin[b_]], [rps])
                            if br == 0:
                                TT(mrg[:, f, :], ps[:], gmf[gi_][:, br, :], ALU.mult, [rps, r_gmf[gi_]], [r_mrg[f]])
                            else:
                                TT(tmp[:], ps[:], gmf[gi_][:, br, :], ALU.mult, [rps, r_gmf[gi_]], [r_tmp])
                                TT(mrg[:, f, :], mrg[:, f, :], tmp[:], ALU.add, [r_mrg[f], r_tmp], [r_mrg[f]])
                        ACT(mrb[:, f, :], mrg[:, f, :], AF.Copy, [r_mrg[f]], [r_mrb[f]])
                    for sub in range(4):
                        xb_ = (tt * 4 + sub) % 2; tok0 = t0 + sub * 128
                        if own:
                            fw.idma(pool, xt[xb_][:], x_src[:, :], idxE[:, 1 + tt * 4 + sub:2 + tt * 4 + sub], reads=[r_xsrc], writes=[r_xt[xb_]])
                        else:
                            fw.dma(sp, xt[xb_][:], x_src[tok0:tok0 + 128, :], reads=[r_xsrc], writes=[r_xt[xb_]])
                        for nchk in range(2):
                            ps, rps = PS()
                            for kt in range(8):
                                MM(ps[:], mrb[:, kt, sub * 128:(sub + 1) * 128], wout[:, kt, nchk * 512:(nchk + 1) * 512], kt == 0, kt == 7,
                                   [r_mrb[kt], r_wout], [rps])
                            TT(tmp[:], ps[:], gate[:, nchk * 512:(nchk + 1) * 512], ALU.mult, [rps, r_gate], [r_tmp])
                            TT(xo[xb_][:, nchk * 512:(nchk + 1) * 512], tmp[:], xt[xb_][:, nchk * 512:(nchk + 1) * 512], ALU.add, [r_tmp, r_xt[xb_]], [r_xo[xb_]])
                        fw.dma(sp, x_dst[tok0:tok0 + 128, :], xo[xb_][:], reads=[r_xo[xb_]], writes=[r_xdst])
                fw.barrier()

        r_in = Reg("xin")
        for g in groups:
            for l in range(cfg["layers"]):
                last = (l == cfg["layers"] - 1)
                x_src = g.x_in if l == 0 else g.x_mid
                r_src = r_in if l == 0 else g.r_xmid
                x_dst = g.x_out if last else g.x_mid
                r_dst = g.r_xout if last else g.r_xmid
                own = g.latent and last
                if own:
                    dense_in(g, l, x_src, r_src, wbs=(3, 4, 5, 6, 11))
                    dense_in(g, l, x_src, r_src, wbs=tuple(w for w in range(22) if w not in (3, 4, 5, 6, 11)), ext=True)
                else:
                    dense_in(g, l, x_src, r_src)
                conformer(g, l, own)
                pooling(g, l, own)
                attention(g, l, own)
                hyena_filters(g, l)
                hyena(g, l, own)
                tail(g, l, x_src, r_src, x_dst, r_dst, own)
        fw.barrier()
        nops = fw.nops
    return nc, nops

_PROG = {}
def _cfg():
    return dict(prompt=os.environ.get("K_PROMPT", "1") == "1", sample=os.environ.get("K_SAMPLE", "1") == "1",
                layers=int(os.environ.get("K_LAYERS", "2")))

def kernel(**inputs):
    cfg = _cfg()
    key = tuple(sorted(cfg.items()))
    if key not in _PROG:
        _PROG[key] = build_program(cfg)
    nc, nops = _PROG[key]
    C = host_consts()
    f32 = lambda a: np.ascontiguousarray(np.asarray(a, dtype=np.float32))
    xp = f32(inputs['x_prompt']); xs = f32(inputs['x_sample'])
    ck = f32(inputs['cache_k']); cv = f32(inputs['cache_v']); c = f32(inputs['c']); cctx = f32(inputs['c_ctx'])
    base = {k: f32(inputs[k]) for k in WEIGHT_SHAPES}
    for k, v in C.items(): base["c_" + k] = v
    in_maps = []
    for core in range(8):
        b = core // 4
        m = dict(base)
        m["xp"] = xp[2 * core:2 * core + 2].reshape(2 * LP, D)
        m["xs"] = xs[b]
        m["ck"] = ck[b].reshape(DEPTH, PAST, 256); m["cv"] = cv[b].reshape(DEPTH, PAST, 256)
        m["cvec"] = np.stack([cctx, c[b]], axis=0)
        q = core % 4; off = q * 1024
        ie = (off - 128 + np.arange(1280)).reshape(10, 128).T
        m["idxE"] = np.ascontiguousarray(np.clip(ie, 0, LS - 1).astype(np.int32))
        m["idxQ"] = (q * 128 + np.arange(128)).astype(np.int32).reshape(128, 1)
        m["maskLR"] = np.tile(np.array([[0.0 if q == 0 else 1.0, 0.0 if q == 3 else 1.0]], np.float32), (128, 1))
        m["ropec_own"] = np.ascontiguousarray(C["ropec"][:, off:off + 1024]); m["ropes_own"] = np.ascontiguousarray(C["ropes"][:, off:off + 1024])
        m["invcnt_own"] = np.ascontiguousarray(C["invcnt_s"][:, off:off + 1024])
        in_maps.append(m)
    res = run_bass_kernel_spmd(nc, in_maps, core_ids=list(range(8)))
    R = res.results
    y_prompt = np.concatenate([R[i]["yp"].reshape(2, LP, D) for i in range(8)], axis=0).astype(np.float32)
    y_sample = np.stack([np.concatenate([R[4 * b + q]["ys"] for q in range(4)], axis=0) for b in range(2)], axis=0).astype(np.float32)
    nk = np.concatenate([R[i]["nk"].reshape(2, DEPTH, LP, 4, 64) for i in range(8)], axis=0).astype(np.float32)
    nv = np.concatenate([R[i]["nv"].reshape(2, DEPTH, LP, 4, 64) for i in range(8)], axis=0).astype(np.float32)
    return (y_prompt, y_sample, nk, nv)
```

```python
import os, math, contextlib
import numpy as np
import ml_dtypes
import concourse.bass as bass
import concourse.mybir as mybir
from concourse.bass_utils import run_bass_kernel_spmd

F32 = mybir.dt.float32; BF16 = mybir.dt.bfloat16; I32 = mybir.dt.int32
AF = mybir.ActivationFunctionType; ALU = mybir.AluOpType
NPBF = ml_dtypes.bfloat16

D = 1024; NIN = 11264; DEPTH = 2; LP = 256; LS = 4096; PAST = 256
EPS = 1e-6
TWO_PI = 2.0 * math.pi

SAME_ENGINE_SYNC = os.environ.get('K_SES', '1') == '1'
class Reg:
    __slots__ = ("name", "w", "r", "dent")
    def __init__(self, name=""):
        self.name = name; self.w = None; self.r = []; self.dent = None

class Eng:
    def __init__(self, name, h):
        self.name = name; self.h = h; self.sem = None; self.cnt = 0; self.seen = {}

class FW:
    def __init__(self, nc, es, ndma=70):
        self.nc = nc; self.es = es
        self.pe = Eng("pe", nc.tensor); self.act = Eng("act", nc.scalar)
        self.dve = Eng("dve", nc.vector); self.pool = Eng("pool", nc.gpsimd); self.sp = Eng("sp", nc.sync)
        self.nsem = 0
        for e in (self.pe, self.act, self.dve, self.pool):
            e.sem = self.alloc_sem(e.name)
        self.dpool = [[self.alloc_sem("d%d" % i), 0] for i in range(ndma)]
        self.dfree = list(range(ndma)); self.dused = []
        self.nops = 0
    def alloc_sem(self, name):
        self.nsem += 1
        return self.es.enter_context(self.nc.semaphore("s%d_%s" % (self.nsem, name)))
    def _deps(self, eng, reads, writes):
        best = {}
        def add(t):
            k = id(t[0])
            if k not in best or best[k][1] < t[1]: best[k] = t
        for r in reads:
            if r.w is not None: add(r.w)
        for w in writes:
            if w.w is not None: add(w.w)
            for t in w.r: add(t)
        for k, (sem, val) in best.items():
            if sem is eng.sem and (eng is self.pe or not SAME_ENGINE_SYNC): continue
            if eng.seen.get(k, 0) >= val: continue
            eng.h.wait_ge(sem, val); eng.seen[k] = val
    def _mark(self, tok, reads, writes):
        for w in writes: w.w = tok; w.r = []
        for r in reads:
            if not any(r is w for w in writes): r.r.append(tok)
    def op(self, eng, fn, reads=(), writes=()):
        self._deps(eng, reads, writes)
        ins = fn(eng.h)
        ins.then_inc(eng.sem, 1); eng.cnt += 1; self.nops += 1
        tok = (eng.sem, eng.cnt)
        self._mark(tok, reads, writes)
        return tok
    def dma(self, q, out, in_, reads=(), writes=(), **kw):
        self._deps(q, reads, writes)
        is_store = 'DRam' in type(out.tensor).__name__
        prim = reads[0] if (is_store and len(reads)) else (writes[0] if len(writes) else reads[0])
        if prim.dent is None:
            if not self.dfree: raise RuntimeError("out of dma sems")
            prim.dent = self.dfree.pop(); self.dused.append(prim)
        ent = self.dpool[prim.dent]
        ins = q.h.dma_start(out=out, in_=in_, **kw)
        ins.then_inc(ent[0], 16); ent[1] += 16; self.nops += 1
        tok = (ent[0], ent[1])
        self._mark(tok, reads, writes)
        return tok
    def idma(self, q, out, in_, idx_ap, reads=(), writes=()):
        self._deps(q, reads, writes)
        prim = writes[0]
        if prim.dent is None:
            if not self.dfree: raise RuntimeError("out of dma sems")
            prim.dent = self.dfree.pop(); self.dused.append(prim)
        ent = self.dpool[prim.dent]
        ins = q.h.indirect_dma_start(out=out, out_offset=None, in_=in_, in_offset=bass.IndirectOffsetOnAxis(ap=idx_ap, axis=0))
        ins.then_inc(ent[0], 16); ent[1] += 16; self.nops += 1
        tok = (ent[0], ent[1])
        self._mark(tok, reads, writes)
        return tok
    def all_tokens(self):
        toks = []
        for e in (self.pe, self.act, self.dve, self.pool):
            if e.cnt > 0: toks.append((e.sem, e.cnt))
        for i, (s, c) in enumerate(self.dpool):
            if c > 0: toks.append((s, c))
        return toks
    def barrier(self):
        toks = self.all_tokens()
        for e in (self.pe, self.act, self.dve, self.pool, self.sp):
            for (sem, val) in toks:
                k = id(sem)
                if e.seen.get(k, 0) >= val: continue
                e.h.wait_ge(sem, val); e.seen[k] = val
        for r in self.dused: r.dent = None
        self.dused = []; self.dfree = list(range(len(self.dpool)))

def _bf(a): return np.ascontiguousarray(a.astype(np.float32)).astype(NPBF)

_CONST_CACHE = {}
def host_consts():
    if _CONST_CACHE: return _CONST_CACHE
    C = {}
    C["ident_f"] = np.eye(128, dtype=np.float32)
    C["ident_b"] = _bf(np.eye(128))
    bo = np.zeros((128, 128), np.float32); bo[:64, :64] = 1 / 64.; bo[64:, 64:] = 1 / 64.
    C["bones"] = _bf(bo)
    C["ones512"] = _bf(np.full((128, 128), 1 / 512.))
    R = np.zeros((128, 128), np.float32)
    for d in range(128):
        if (d % 32) < 16: R[d + 16, d] = -1.0
        else: R[d - 16, d] = 1.0
    C["rmat"] = _bf(R)
    t = np.arange(LS); row = (t // 64).astype(np.float32); col = (t % 64).astype(np.float32)
    cosT = np.zeros((128, LS), np.float32); sinT = np.zeros((128, LS), np.float32)
    inv = (10000.0 ** (-np.arange(16, dtype=np.float32) / 16)).astype(np.float32)
    for d in range(128):
        dd = d % 64; half = dd // 32; f = (dd % 32) % 16
        pos = row if half == 0 else col
        ang = (pos * inv[f]).astype(np.float32)
        cosT[d] = np.cos(ang); sinT[d] = np.sin(ang)
    C["ropec"] = cosT; C["ropes"] = sinT
    for nm, L in (("p", LP), ("s", LS)):
        tt = np.arange(L); ic = np.zeros((4, L), np.float32)
        for g, w in enumerate((2, 4, 8, 16)):
            lo = np.clip(tt - w // 2, 0, L); hi = np.clip(tt + w // 2, 0, L)
            ic[g] = 1.0 / (hi - lo)
        C["invcnt_" + nm] = ic
        tl = np.linspace(0.0, 1.0, L, dtype=np.float32)[:, None]
        bands = np.linspace(1e-4, 15, 16, dtype=np.float32)[None, :]
        w_ = ((2.0 * math.pi / L) * np.arange(L, dtype=np.float32))[:, None]
        z = np.concatenate([tl, np.cos(bands * w_), np.sin(bands * w_)], axis=-1).astype(np.float32)
        idx = np.concatenate([np.arange(L), (L - np.arange(L)) % L])
        C["zf_" + nm] = np.ascontiguousarray(z[idx].T)
        trow = tl[:, 0][idx].astype(np.float32).copy(); trow[L] = 1e4
        C["trow_" + nm] = trow[None, :]
    max_decay = math.log(1e-2) / 0.3; min_decay = math.log(1e-2) / 1.5
    deltas = np.abs(np.linspace(min_decay, max_decay, 512, dtype=np.float32))
    C["negdelta"] = np.ascontiguousarray((-deltas).reshape(4, 128).T)
    N = 512
    tt = np.arange(512)[:, None].astype(np.float64); kk = np.arange(512)[None, :].astype(np.float64)
    ang = 2 * np.pi * tt * kk / N
    C["fp_c"] = _bf(np.cos(ang).reshape(4, 128, 512).transpose(1, 0, 2))
    C["fp_sn"] = _bf((-np.sin(ang)).reshape(4, 128, 512).transpose(1, 0, 2))
    k2 = np.arange(512)[:, None].astype(np.float64); t2 = np.arange(256)[None, :].astype(np.float64)
    ang2 = 2 * np.pi * k2 * t2 / N
    C["fi_c"] = _bf((np.cos(ang2) / N).reshape(4, 128, 256).transpose(1, 0, 2))
    C["fi_sn"] = _bf((-np.sin(ang2) / N).reshape(4, 128, 256).transpose(1, 0, 2))
    N = 8192
    j = np.arange(64)[:, None].astype(np.float64); g = np.arange(64)[None, :].astype(np.float64)
    a1 = 2 * np.pi * j * g / 64
    C["wa"] = _bf(np.concatenate([np.concatenate([np.cos(a1), -np.sin(a1), -np.cos(a1)], axis=1), np.zeros((64, 192))], axis=0))
    a = np.arange(128)[:, None, None].astype(np.float64)
    gg = np.arange(64)[None, :, None].astype(np.float64); kb = np.arange(128)[None, None, :].astype(np.float64)
    th = 2 * np.pi * a * (gg + 64 * kb) / N
    C["tb_c"] = _bf(np.cos(th)); C["tb_s"] = _bf(np.sin(th)); C["tb_sn"] = _bf(-np.sin(th))
    kb2 = np.arange(128)[:, None].astype(np.float64); nb = np.arange(128)[None, :].astype(np.float64)
    al = 2 * np.pi * kb2 * nb / 128
    C["tbp_c"] = _bf(np.cos(al)); C["tbp_s"] = _bf(np.sin(al))
    g3 = np.arange(64)[:, None, None].astype(np.float64); nb3 = np.arange(128)[None, :, None].astype(np.float64)
    na3 = np.arange(32)[None, None, :].astype(np.float64)
    ph = 2 * np.pi * g3 * (128 * na3 + nb3) / N
    C["tc_st"] = _bf(np.concatenate([np.cos(ph) / N, -np.sin(ph) / N], axis=0))
    _CONST_CACHE.update(C)
    return C

WEIGHT_SHAPES = {
    'w_ada': (DEPTH, D, 3 * D), 'b_ada': (DEPTH, 3 * D), 'norm_g': (DEPTH, D), 'w_in': (DEPTH, D, NIN),
    'conv_dw_w': (DEPTH, 31, 512), 'conv_dw_b': (DEPTH, 512), 'conv_ln_g': (DEPTH, 512), 'conv_ln_b': (DEPTH, 512),
    'conv_pw': (DEPTH, 512, 512), 'hy_short_w': (DEPTH, 3, 1536), 'hy_short_b': (DEPTH, 1536),
    'hy_w1': (DEPTH, 33, 64), 'hy_b1': (DEPTH, 64), 'hy_freq': (DEPTH, 64), 'hy_w2': (DEPTH, 64, 64), 'hy_b2': (DEPTH, 64),
    'hy_w3': (DEPTH, 64, 2048), 'hy_b3': (DEPTH, 2048), 'hy_skip': (DEPTH, 2, 512),
    'pool_w': (DEPTH, 4, 128, 128), 'pool_scale': (DEPTH, 512), 'q_norm': (DEPTH, 64), 'k_norm': (DEPTH, 64),
    'wo_conv': (DEPTH, 512, D), 'wo_hyena': (DEPTH, 512, D), 'wo_pool': (DEPTH, 512, D), 'wo_attn': (DEPTH, D, D),
    'w_out': (DEPTH, D, D),
}

BLK = dict(a_val=0, a_glu=4, a_gate=8, b_proj=12, b_gate=24, c_in=28, c_gate=32, q=36, k=44, v=46, d_gate=48, gm=56)
def blk_func(j):
    if j < 4: return AF.Identity
    if j < 8: return AF.Sigmoid
    if j < 12: return AF.Silu
    if j < 24: return AF.Identity
    if j < 28: return AF.Silu
    if j < 32: return AF.Identity
    if j < 36: return AF.Silu
    if j < 48: return AF.Identity
    if j < 56: return AF.Silu
    return AF.Sigmoid

class Grp:
    pass

def build_program(cfg):
    nc = bass.Bass("TRN2", target_bir_lowering=False)
    C = host_consts()
    es = contextlib.ExitStack()
    with es:
        fw = FW(nc, es)
        pe, act, dve, pool, sp = fw.pe, fw.act, fw.dve, fw.pool, fw.sp

        _uid = [0]
        def uq(name):
            _uid[0] += 1
            return '%s_%d' % (name, _uid[0])
        def din(name, shape, dt=F32): return nc.dram_tensor(name, list(shape), dt, kind="ExternalInput").ap()
        def dout(name, shape, dt=F32): return nc.dram_tensor(name, list(shape), dt, kind="ExternalOutput").ap()
        def dscr(name, shape, dt): return nc.dram_tensor(name, list(shape), dt).ap()

        W = {k: din(k, s) for k, s in WEIGHT_SHAPES.items()}
        CT = {}
        for k, v in C.items():
            CT[k] = din("c_" + k, v.shape, BF16 if v.dtype == NPBF else F32)
        xp_in = din("xp", (2 * LP, D)); xs_in = din("xs", (LS, D))
        ck_in = din("ck", (DEPTH, PAST, 256)); cv_in = din("cv", (DEPTH, PAST, 256))
        cvec = din("cvec", (2, D))
        idxE_in = din("idxE", (128, 10), I32); idxQ_in = din("idxQ", (128, 1), I32); maskLR_in = din("maskLR", (128, 2))
        ropec_own = din("ropec_own", (128, 1024)); ropes_own = din("ropes_own", (128, 1024)); invcnt_own = din("invcnt_own", (4, 1024))
        yp_out = dout("yp", (2 * LP, D)); ys_out = dout("ys", (1024, D))
        nk_out = dout("nk", (2, DEPTH, LP, 256)); nv_out = dout("nv", (2, DEPTH, LP, 256))
        dbg = {}

        groups = []
        if cfg["prompt"]:
            g = Grp(); g.name = "p"; g.nseq = 2; g.L = LP; g.T = 512; g.x_in = xp_in; g.x_out = yp_out; g.crow = 0; g.latent = False
            groups.append(g)
        if cfg["sample"]:
            g = Grp(); g.name = "s"; g.nseq = 1; g.L = LS; g.T = LS; g.x_in = xs_in; g.x_out = ys_out; g.crow = 1; g.latent = True
            groups.append(g)
        for g in groups:
            T = g.T
            g.x_mid = dscr("xmid_" + g.name, (T, D), F32); g.r_xmid = Reg("xmid")
            g.U = dscr("U_" + g.name, (88, 128, T), BF16); g.rU = [Reg("U%d" % j) for j in range(88)]
            g.UKV = dscr("UKV_" + g.name, (4, 128, T), F32); g.rUKV = [Reg("UKV%d" % j) for j in range(4)]
            g.YIN = dscr("YIN_" + g.name, (20, 128, T), BF16); g.rYIN = [Reg("Y%d" % j) for j in range(20)]
            g.QR = dscr("QR_" + g.name, (8, 128, T), BF16); g.rQR = Reg("QR")
            g.ZB = dscr("ZB_" + g.name, (128, 2 * g.L if g.nseq == 1 else g.T), BF16); g.rZB = Reg("ZB")
            if g.latent:
                g.HS = dscr("HS_" + g.name, (2, 4, 2, 2, 128, 64 * 64), BF16)
            else:
                g.HS = dscr("HS_" + g.name, (2, 4, 2, 128, 4 * 128), BF16)
            g.rHS = Reg("HS")
            if g.latent:
                g.UE = dscr("UE_" + g.name, (88, 128, 1280), BF16); g.rUE = [Reg("UE%d" % j) for j in range(88)]
                g.KR = dscr("KR_" + g.name, (2, 128, g.T), BF16); g.rKR = Reg("KR")
                g.VT = dscr("VT_" + g.name, (128, g.T // 128, 256), BF16); g.rVT = Reg("VT")
                g.YB = [dscr("YB%d_" % cb_ + g.name, (4 * 128, 1024), BF16) for cb_ in range(4)]; g.rYB = Reg("YB")
            g.r_xout = Reg("xout")
        r_nk = Reg("nk"); r_nv = Reg("nv")

        def sbp(name, shape, dt): return es.enter_context(nc.sbuf_tensor(name, list(shape), dt))
        ident_f = sbp("ident_f", [128, 128], F32); ident_b = sbp("ident_b", [128, 128], BF16)
        bones = sbp("bones", [128, 128], BF16); ones512 = sbp("ones512", [128, 128], BF16); rmat = sbp("rmat", [128, 128], BF16)
        r_const = Reg("const")
        for tile_, nm in ((ident_f, "ident_f"), (ident_b, "ident_b"), (bones, "bones"), (ones512, "ones512"), (rmat, "rmat")):
            fw.dma(sp, tile_[:], CT[nm][:, :], writes=[Reg("c" + nm)])
        idxE = sbp("idxE_sb", [128, 10], I32); idxQ = sbp("idxQ_sb", [128, 1], I32); maskLR = sbp("maskLR_sb", [128, 2], F32)
        fw.dma(sp, idxE[:], idxE_in[:, :], writes=[Reg("cidxE")]); fw.dma(sp, idxQ[:], idxQ_in[:, :], writes=[Reg("cidxQ")])
        fw.dma(sp, maskLR[:], maskLR_in[:, :], writes=[Reg("cmask")])
        fw.barrier()
        class PSX_: pass
        PSX = PSX_()
        psi = [0]
        def psum_std(ph, nb=7, with_pst=True):
            PSX.banks = [ph.enter_context(nc.psum_tensor(uq("ps"), [128, 512], F32)) for i in range(nb)]
            PSX.regs = [Reg("ps%d" % i) for i in range(nb)]
            if with_pst:
                PSX.pst = ph.enter_context(nc.psum_tensor(uq("pst"), [128, 1024], BF16)); PSX.rpst = Reg("pst")
        def PS():
            nb = len(PSX.banks)
            i = psi[0] % nb; psi[0] += 1
            return PSX.banks[i], PSX.regs[i]

        def ACT(out, in_, func, reads, writes, **kw):
            return fw.op(act, lambda e: e.activation(out=out, in_=in_, func=func, **kw), reads, writes)
        def MM(out, lhsT, rhs, start, stop, reads, writes):
            return fw.op(pe, lambda e: e.matmul(out, lhsT=lhsT, rhs=rhs, start=start, stop=stop), reads, writes)
        def TR(out, in_, ident, reads, writes):
            return fw.op(pe, lambda e: e.transpose(out, in_, ident), reads, writes)
        def TT(out, a, b, op, reads, writes, eng=None):
            return fw.op(eng or dve, lambda e: e.tensor_tensor(out=out, in0=a, in1=b, op=op), reads, writes)
        def TS(out, a, s1, s2, op0, op1, reads, writes, eng=None):
            if s2 is None:
                return fw.op(eng or dve, lambda e: e.tensor_scalar(out=out, in0=a, scalar1=s1, scalar2=None, op0=op0), reads, writes)
            return fw.op(eng or dve, lambda e: e.tensor_scalar(out=out, in0=a, scalar1=s1, scalar2=s2, op0=op0, op1=op1), reads, writes)
        def STT(out, a, s, b, op0, op1, reads, writes, eng=None):
            return fw.op(eng or dve, lambda e: e.scalar_tensor_tensor(out=out, in0=a, scalar=s, in1=b, op0=op0, op1=op1), reads, writes)
        def CP(out, in_, reads, writes, eng=None):
            return fw.op(eng or dve, lambda e: e.tensor_copy(out=out, in_=in_), reads, writes)
        def MS(out, val, writes, eng=None):
            return fw.op(eng or dve, lambda e: e.memset(out, val), (), writes)
        def RCP(out, in_, reads, writes):
            return fw.op(dve, lambda e: e.reciprocal(out=out, in_=in_), reads, writes)
        def colvec(dst, src_vec_ap, nb, writes):
            return fw.dma(sp, dst, src_vec_ap.rearrange("(b p) -> p b", p=128), writes=writes, allow_slow_non_contiguous=True)

        def dense_in(g, l, x_src, r_xsrc, wbs=tuple(range(22)), ext=False):
            T = g.T; TC = min(T, 2048); nchunk = T // TC
            if ext: TC = 1280; nchunk = 1
            with contextlib.ExitStack() as ph:
                def sb(name, shape, dt): return ph.enter_context(nc.sbuf_tensor(uq(name), list(shape), dt))
                psum_std(ph)
                modb = sb("modb", [128, 3 * D], F32); r_mod = Reg("mod")
                Gt = sb("Gt", [128, D], F32); r_G = Reg("G")
                ngb = sb("ngb", [128, D], F32); r_ng = Reg("ng")
                badab = sb("badab", [128, 3 * D], F32); r_bada = Reg("bada")
                ccol = sb("ccol", [128, 8], F32); r_cc = Reg("cc")
                crep = sb("crep", [128, 8, 128], BF16); r_crep = Reg("crep")
                wsl = [sb("wsl%d" % i, [128, 8, 512], BF16) for i in range(3)]; r_wsl = [Reg("wsl%d" % i) for i in range(3)]
                hT = sb("hT", [128, 8, TC], BF16); r_hT = [Reg("hT%d" % i) for i in range(TC // 128)]
                xt = [sb("xt%d" % i, [128, D], F32) for i in range(2)]; r_xt = [Reg("xt%d" % i) for i in range(2)]
                sqj = sb("sqj", [128, D], F32); r_sqj = Reg("sqj")
                ss = [sb("ss%d" % i, [128, 1], F32) for i in range(2)]; r_ss = [Reg("ss%d" % i) for i in range(2)]
                hb = [sb("hb%d" % i, [128, D], BF16) for i in range(2)]; r_hb = [Reg("hb%d" % i) for i in range(2)]
                stg = [sb("stg%d" % i, [128, TC], BF16) for i in range(3)]; r_stg = [Reg("stg%d" % i) for i in range(3)]
                stgf = [sb("stgf%d" % i, [128, TC], F32) for i in range(2)]; r_stgf = [Reg("stgf%d" % i) for i in range(2)]
                fuse_qk = g.latent
                if fuse_qk:
                    gqd = sb("gqd", [128, 1], F32); gkd = sb("gkd", [128, 1], F32); r_gvd = Reg("gqkd")
                    for h2 in range(2):
                        fw.dma(sp, gqd[h2 * 64:(h2 + 1) * 64, :], W['q_norm'][l].rearrange("(d o) -> d o", o=1), writes=[r_gvd])
                        fw.dma(sp, gkd[h2 * 64:(h2 + 1) * 64, :], W['k_norm'][l].rearrange("(d o) -> d o", o=1), writes=[r_gvd])
                    TS(gqd[:], gqd[:], 0.125, None, ALU.mult, None, [r_gvd], [r_gvd])
                    CW = 512
                    dsq = [sb("dsq%d" % i, [128, CW], BF16) for i in range(4)]; r_dsq = [Reg("dsq%d" % i) for i in range(4)]
                    drs = [sb("drs%d" % i, [128, CW], F32) for i in range(4)]; r_drs = [Reg("drs%d" % i) for i in range(4)]
                    dkf = [sb("dkf%d" % i, [128, CW], F32) for i in range(4)]; r_dkf = [Reg("dkf%d" % i) for i in range(4)]
                    dkb = [sb("dkb%d" % i, [128, CW], BF16) for i in range(4)]; r_dkb = [Reg("dkb%d" % i) for i in range(4)]
                    dt1 = [sb("dt1%d" % i, [128, CW], F32) for i in range(4)]; r_dt1 = [Reg("dt1%d" % i) for i in range(4)]
                    dqo = [sb("dqo%d" % i, [128, CW], BF16) for i in range(4)]; r_dqo = [Reg("dqo%d" % i) for i in range(4)]
                    NRT = 1024 if ext else TC
                    cosd = sb("cosd", [128, NRT], F32); sind = sb("sind", [128, NRT], F32); r_csd = Reg("csd")
                    dcnt = [0]
                    pending = []
                    def advance():
                        for gen in list(pending):
                            try: next(gen)
                            except StopIteration: pending.remove(gen)
                    def qk_chain(*a):
                        gen = qk_chain_gen(*a); next(gen); pending.append(gen)
                    vts = [sb("vts%d" % i, [128, 4, 128], BF16) for i in range(2)]; r_vts = [Reg("vts%d" % i) for i in range(2)]
                    vcnt = [0]
                    def v_chain_gen(src, rsrc, tok0, vb):
                        yield
                        k = vcnt[0] % 2; vcnt[0] += 1
                        for sub in range(4):
                            TR(PSX.pst[:, sub * 128:(sub + 1) * 128], src[:, sub * 128:(sub + 1) * 128], ident_b[:], rsrc, [PSX.rpst])
                        CP(vts[k][:], PSX.pst[:, 0:512].rearrange("p (s f) -> p s f", s=4), [PSX.rpst], [r_vts[k]])
                        fw.dma(sp, g.VT[:, tok0 // 128:tok0 // 128 + 4, vb * 128:(vb + 1) * 128], vts[k][:], reads=[r_vts[k]], writes=[g.rVT])
                    def v_chain(*a):
                        gen = v_chain_gen(*a); next(gen); pending.append(gen)
                    def qk_chain_gen(src, rsrc, gvec, tcol, n, out_dram, r_out):
                        k = dcnt[0] % 4; dcnt[0] += 1
                        TT(dsq[k][:, 0:n], src, src, ALU.mult, rsrc, [r_dsq[k]])
                        STT(dkf[k][:, 0:n], src, gvec, src, ALU.mult, ALU.bypass, rsrc + [r_gvd], [r_dkf[k]])
                        yield
                        ps, rps = PS()
                        MM(ps[:, 0:n], bones[:], dsq[k][:, 0:n], True, True, [r_dsq[k]], [rps])
                        ACT(drs[k][:, 0:n], ps[:, 0:n], AF.Ln, [rps], [r_drs[k]], bias=EPS)
                        ACT(drs[k][:, 0:n], drs[k][:, 0:n], AF.Exp, [r_drs[k]], [r_drs[k]], scale=-0.5)
                        TT(dkf[k][:, 0:n], dkf[k][:, 0:n], drs[k][:, 0:n], ALU.mult, [r_dkf[k], r_drs[k]], [r_dkf[k]])
                        CP(dkb[k][:, 0:n], dkf[k][:, 0:n], [r_dkf[k]], [r_dkb[k]])
                        yield
                        ps, rps = PS()
                        MM(ps[:, 0:n], rmat[:], dkb[k][:, 0:n], True, True, [r_dkb[k]], [rps])
                        TT(dt1[k][:, 0:n], ps[:, 0:n], sind[:, tcol:tcol + n], ALU.mult, [rps, r_csd], [r_dt1[k]])
                        TT(dkf[k][:, 0:n], dkf[k][:, 0:n], cosd[:, tcol:tcol + n], ALU.mult, [r_dkf[k], r_csd], [r_dkf[k]])
                        TT(dqo[k][:, 0:n], dkf[k][:, 0:n], dt1[k][:, 0:n], ALU.add, [r_dkf[k], r_dt1[k]], [r_dqo[k]])
                        fw.dma(sp, out_dram, dqo[k][:, 0:n], reads=[r_dqo[k]], writes=[r_out])
                fw.dma(sp, ccol[:], cvec[g.crow, :].rearrange("(b p) -> p b", p=128), writes=[r_cc], allow_slow_non_contiguous=True)
                fw.dma(sp, ngb[:], W['norm_g'][l:l + 1, :].partition_broadcast(128), writes=[r_ng])
                fw.dma(sp, badab[:], W['b_ada'][l:l + 1, :].partition_broadcast(128), writes=[r_bada])
                ACT(ccol[:], ccol[:], AF.Silu, [r_cc], [r_cc])
                CP(crep[:], ccol[:].unsqueeze(2).broadcast_to([128, 8, 128]), [r_cc], [r_crep])
                wi = 0
                wada = W['w_ada'][l].rearrange("(kt p) n -> p kt n", p=128)
                for cb in range(6):
                    s_ = wi % 3; wi += 1
                    fw.dma(pool, wsl[s_][:], wada[:, :, cb * 512:(cb + 1) * 512], writes=[r_wsl[s_]])
                    ps, rps = PS()
                    for kt in range(8):
                        MM(ps[:], crep[:, kt, :], wsl[s_][:, kt, :], kt == 0, kt == 7, [r_crep, r_wsl[s_]], [rps])
                    TT(modb[:, cb * 512:(cb + 1) * 512], ps[:], badab[:, cb * 512:(cb + 1) * 512], ALU.add, [rps, r_bada], [r_mod])
                STT(Gt[:], modb[:, D:2 * D], 1.0, ngb[:], ALU.add, ALU.mult, [r_mod, r_ng], [r_G])
                win = W['w_in'][l].rearrange("(kt p) n -> p kt n", p=128)
                si = 0; sfi = 0
                for ci in range(nchunk):
                    if fuse_qk:
                        if ext:
                            fw.dma(sp, cosd[:], ropec_own[:, :], writes=[r_csd]); fw.dma(sp, sind[:], ropes_own[:, :], writes=[r_csd])
                        else:
                            fw.dma(sp, cosd[:], CT["ropec"][:, ci * TC:(ci + 1) * TC], writes=[r_csd])
                            fw.dma(sp, sind[:], CT["ropes"][:, ci * TC:(ci + 1) * TC], writes=[r_csd])
                    for tl in range(TC // 128):
                        tok0 = ci * TC + tl * 128; b_ = tl % 2
                        if ext:
                            fw.idma(pool, xt[b_][:], x_src[:, :], idxE[:, tl:tl + 1], reads=[r_xsrc], writes=[r_xt[b_]])
                        else:
                            fw.dma(sp, xt[b_][:], x_src[tok0:tok0 + 128, :], reads=[r_xsrc], writes=[r_xt[b_]])
                        ACT(sqj[:], xt[b_][:], AF.Square, [r_xt[b_]], [r_sqj, r_ss[b_]], accum_out=ss[b_][:])
                        ACT(ss[b_][:], ss[b_][:], AF.Sqrt, [r_ss[b_]], [r_ss[b_]], scale=1.0 / D, bias=EPS)
                        RCP(ss[b_][:], ss[b_][:], [r_ss[b_]], [r_ss[b_]])
                        STT(xt[b_][:], xt[b_][:], ss[b_][:, 0:1], Gt[:], ALU.mult, ALU.mult, [r_xt[b_], r_ss[b_], r_G], [r_xt[b_]])
                        TT(hb[b_][:], xt[b_][:], modb[:, 0:D], ALU.add, [r_xt[b_], r_mod], [r_hb[b_]])
                        for kt in range(8):
                            TR(PSX.pst[:, kt * 128:(kt + 1) * 128], hb[b_][:, kt * 128:(kt + 1) * 128], ident_b[:], [r_hb[b_]], [PSX.rpst])
                        CP(hT[:, :, tl * 128:(tl + 1) * 128], PSX.pst[:].rearrange("p (k t) -> p k t", k=8), [PSX.rpst], [r_hT[tl]],
                           eng=act if tl % 2 else dve) if False else ACT(hT[:, :, tl * 128:(tl + 1) * 128], PSX.pst[:].rearrange("p (k t) -> p k t", k=8), AF.Copy, [PSX.rpst], [r_hT[tl]])
                    wb_order = list(wbs)
                    if fuse_qk:
                        hot = [w for w in (9, 10, 11) if w in wb_order]; cold = [w for w in wb_order if w not in hot]
                        wb_order = []
                        step = max(1, len(cold) // (len(hot) + 1)) if hot else 1
                        ci_ = 0
                        for hw in hot:
                            wb_order.append(hw); wb_order += cold[ci_:ci_ + step]; ci_ += step
                        wb_order += cold[ci_:]
                    for wb in wb_order:
                        s_ = wi % 3; wi += 1
                        fw.dma(pool, wsl[s_][:], win[:, :, wb * 512:(wb + 1) * 512], writes=[r_wsl[s_]])
                        for sub in range(4):
                            j = wb * 4 + sub
                            iskv = 44 <= j < 48
                            if iskv and not (fuse_qk and j >= 46):
                                so = stgf[sfi % 2]; rso = r_stgf[sfi % 2]; sfi += 1
                            else:
                                so = stg[si % 3]; rso = r_stg[si % 3]; si += 1
                            for c0 in range(0, TC, 512):
                                cw = min(512, TC - c0)
                                ps, rps = PS()
                                for kt in range(8):
                                    MM(ps[:, 0:cw], wsl[s_][:, kt, sub * 128:(sub + 1) * 128], hT[:, kt, c0:c0 + cw],
                                       kt == 0, kt == 7, [r_wsl[s_]] + r_hT[c0 // 128:(c0 + cw) // 128], [rps])
                                ACT(so[:, c0:c0 + cw], ps[:, 0:cw], blk_func(j), [rps], [rso])
                                if fuse_qk and c0 == 512: advance()
                            if fuse_qk: advance()
                            if fuse_qk and 36 <= j < 46:
                                isq = j < 44
                                if ext:
                                    for t_ in range(0, 1024, 512):
                                        qk_chain(so[:, 128 + t_:128 + t_ + 512], [rso], gqd[:, 0:1], t_, 512, g.QR[j - 36][:, t_:t_ + 512], g.rQR)
                                else:
                                    for t_ in range(0, TC, 512):
                                        if isq:
                                            qk_chain(so[:, t_:t_ + 512], [rso], gqd[:, 0:1], t_, 512, g.QR[j - 36][:, ci * TC + t_:ci * TC + t_ + 512], g.rQR)
                                        else:
                                            qk_chain(so[:, t_:t_ + 512], [rso], gkd[:, 0:1], t_, 512, g.KR[j - 44][:, ci * TC + t_:ci * TC + t_ + 512], g.rKR)
                            if fuse_qk and j in (46, 47) and not ext:
                                for t_ in range(0, TC, 512):
                                    v_chain(so[:, t_:t_ + 512], [rso], ci * TC + t_, j - 46)
                            if ext:
                                fw.dma(sp, g.UE[j][:, :], so[:, 0:TC], reads=[rso], writes=[g.rUE[j]])
                            elif iskv and fuse_qk and j >= 46:
                                pass
                            elif iskv:
                                fw.dma(sp, g.UKV[j - 44][:, ci * TC:(ci + 1) * TC], so[:], reads=[rso], writes=[g.rUKV[j - 44]])
                            else:
                                fw.dma(sp, g.U[j][:, ci * TC:(ci + 1) * TC], so[:], reads=[rso], writes=[g.rU[j]])
                if fuse_qk:
                    while pending: advance()
                fw.barrier()
            return

        def seq_tiles(g, n):
            out = []
            for s in range(g.nseq):
                for t0 in range(0, g.L, n):
                    out.append((s, t0, min(n, g.L - t0)))
            return out

        def own_tiles(own):
            return [(0, 0, 512), (0, 512, 512)]
        def conformer(g, l, own=False):
            NT = 512 if g.L >= 512 else g.L
            Us = g.UE if own else g.U; rUs = g.rUE if own else g.rU
            with contextlib.ExitStack() as ph:
                def sb(name, shape, dt): return ph.enter_context(nc.sbuf_tensor(uq(name), list(shape), dt))
                psum_std(ph)
                dwc = sb("dwc", [128, 4, 31], F32); r_dwc = Reg("dwc")
                dgm = sb("dgm", [128, 4, 31, 128], BF16); r_dgm = Reg("dgm")
                dwb = sb("dwb", [128, 4], F32); lng = sb("lng", [128, 4], F32); lnb = sb("lnb", [128, 4], F32); r_vec = Reg("cvec")
                pw = sb("pw", [128, 4, 512], BF16); r_pw = Reg("pw")
                vl = [sb("vl%d" % i, [128, 4, NT + 30], BF16) for i in range(2)]; r_vl = [Reg("vl%d" % i) for i in range(2)]
                gl = [sb("gl%d" % i, [128, 4, NT + 30], BF16) for i in range(2)]; r_gl = [Reg("gl%d" % i) for i in range(2)]
                sag = [sb("sag%d" % i, [128, 4, NT], BF16) for i in range(2)]; r_sag = [Reg("sag%d" % i) for i in range(2)]
                a2b_ = [sb("a2b%d" % i, [128, 4, NT], BF16) for i in range(2)]; r_a2b_ = [Reg("a2b%d" % i) for i in range(2)]
                a2s_ = [sb("a2s%d" % i, [128, 4, NT], BF16) for i in range(2)]; r_a2s_ = [Reg("a2s%d" % i) for i in range(2)]
                mean_ = [sb("mean%d" % i, [128, NT], F32) for i in range(2)]; r_mean_ = [Reg("mean%d" % i) for i in range(2)]
                var_ = [sb("var%d" % i, [128, NT], F32) for i in range(2)]; r_var_ = [Reg("var%d" % i) for i in range(2)]
                nmr_ = [sb("nmr%d" % i, [128, NT], F32) for i in range(2)]; r_nmr_ = [Reg("nmr%d" % i) for i in range(2)]
                tmp_ = [sb("tmpc%d" % i, [128, NT], F32) for i in range(2)]; r_tmp_ = [Reg("tmpc%d" % i) for i in range(2)]
                lno_ = [sb("lno%d" % i, [128, 4, NT], BF16) for i in range(2)]; r_lno_ = [Reg("lno%d" % i) for i in range(2)]
                yo = [sb("yoc%d" % i, [128, 4, NT], BF16) for i in range(2)]; r_yo = [Reg("yoc%d" % i) for i in range(2)]
                for b in range(4):
                    fw.dma(sp, dwc[:, b, :], W['conv_dw_w'][l][:, b * 128:(b + 1) * 128].rearrange("k c -> c k"), writes=[r_dwc],
                           allow_slow_non_contiguous=True)
                colvec(dwb[:], W['conv_dw_b'][l], 4, [r_vec]); colvec(lng[:], W['conv_ln_g'][l], 4, [r_vec]); colvec(lnb[:], W['conv_ln_b'][l], 4, [r_vec])
                fw.dma(pool, pw[:], W['conv_pw'][l].rearrange("(kt p) n -> p kt n", p=128), writes=[r_pw])
                for b in range(4):
                    for k in range(31):
                        TS(dgm[:, b, k, :], ident_b[:], dwc[:, b, k:k + 1], None, ALU.mult, None, [r_dwc], [r_dgm])
                for it, (s, t0, n) in enumerate(own_tiles(own) if own else seq_tiles(g, NT)):
                    b_ = it % 2; g0 = s * g.L + t0
                    a2b = a2b_[b_]; r_a2b = r_a2b_[b_]; a2s = a2s_[b_]; r_a2s = r_a2s_[b_]; mean = mean_[b_]; r_mean = r_mean_[b_]
                    var = var_[b_]; r_var = r_var_[b_]; nmr = nmr_[b_]; r_nmr = r_nmr_[b_]; tmp = tmp_[b_]; r_tmp = r_tmp_[b_]; lno = lno_[b_]; r_lno = r_lno_[b_]
                    vmin, vmax, cb0 = (-128, 1152, 128) if own else (0, g.L, s * g.L)
                    lo = max(t0 - 15, vmin); hi = min(t0 + n + 15, vmax)
                    off = lo - (t0 - 15); w = hi - lo
                    if off > 0 or w < n + 30:
                        MS(vl[b_][:], 0.0, [r_vl[b_]]); MS(gl[b_][:], 0.0, [r_gl[b_]])
                    fw.dma(sp, vl[b_][:, :, off:off + w], Us[0:4, :, cb0 + lo:cb0 + hi].rearrange("j p w -> p j w"), reads=rUs[0:4], writes=[r_vl[b_]])
                    fw.dma(sp, gl[b_][:, :, off:off + w], Us[4:8, :, cb0 + lo:cb0 + hi].rearrange("j p w -> p j w"), reads=rUs[4:8], writes=[r_gl[b_]])
                    fw.dma(sp, sag[b_][:, :, 0:n], Us[8:12, :, cb0 + t0:cb0 + t0 + n].rearrange("j p w -> p j w"), reads=rUs[8:12], writes=[r_sag[b_]])
                    TT(vl[b_][:], vl[b_][:], gl[b_][:], ALU.mult, [r_vl[b_], r_gl[b_]], [r_vl[b_]])
                    if own:
                        g0 = t0
                        if t0 == 0:
                            TS(vl[b_][:, :, 0:15], vl[b_][:, :, 0:15], maskLR[:, 0:1], None, ALU.mult, None, [r_vl[b_]], [r_vl[b_]])
                        if t0 + n == 1024:
                            TS(vl[b_][:, :, n + 15:n + 30], vl[b_][:, :, n + 15:n + 30], maskLR[:, 1:2], None, ALU.mult, None, [r_vl[b_]], [r_vl[b_]])
                    pss = []
                    for b in range(4):
                        ps, rps = PS(); pss.append((ps, rps))
                        for k in range(31):
                            MM(ps[:, 0:n], dgm[:, b, k, :], vl[b_][:, b, k:k + n], k == 0, k == 30, [r_dgm, r_vl[b_]], [rps])
                        ACT(a2b[:, b, 0:n], ps[:, 0:n], AF.Identity, [rps], [r_a2b], bias=dwb[:, b:b + 1])
                        ACT(a2s[:, b, 0:n], ps[:, 0:n], AF.Square, [rps], [r_a2s], bias=dwb[:, b:b + 1])
                    pm, rpm = PS(); pq, rpq = PS()
                    for b in range(4):
                        MM(pm[:, 0:n], ones512[:], a2b[:, b, 0:n], b == 0, b == 3, [r_a2b], [rpm])
                    for b in range(4):
                        MM(pq[:, 0:n], ones512[:], a2s[:, b, 0:n], b == 0, b == 3, [r_a2s], [rpq])
                    ACT(mean[:, 0:n], pm[:, 0:n], AF.Copy, [rpm], [r_mean])
                    TT(var[:, 0:n], mean[:, 0:n], mean[:, 0:n], ALU.mult, [r_mean], [r_var])
                    TT(var[:, 0:n], pq[:, 0:n], var[:, 0:n], ALU.subtract, [rpq, r_var], [r_var])
                    TS(var[:, 0:n], var[:, 0:n], 0.0, None, ALU.max, None, [r_var], [r_var])
                    ACT(var[:, 0:n], var[:, 0:n], AF.Sqrt, [r_var], [r_var], bias=EPS)
                    RCP(var[:, 0:n], var[:, 0:n], [r_var], [r_var])
                    STT(nmr[:, 0:n], mean[:, 0:n], -1.0, var[:, 0:n], ALU.mult, ALU.mult, [r_mean, r_var], [r_nmr])
                    for b in range(4):
                        TT(tmp[:, 0:n], a2b[:, b, 0:n], var[:, 0:n], ALU.mult, [r_a2b, r_var], [r_tmp])
                        TT(tmp[:, 0:n], tmp[:, 0:n], nmr[:, 0:n], ALU.add, [r_tmp, r_nmr], [r_tmp])
                        ACT(lno[:, b, 0:n], tmp[:, 0:n], AF.Silu, [r_tmp, r_vec], [r_lno], scale=lng[:, b:b + 1], bias=lnb[:, b:b + 1])
                    for ob in range(4):
                        ps, rps = PS()
                        for kb in range(4):
                            MM(ps[:, 0:n], pw[:, kb, ob * 128:(ob + 1) * 128], lno[:, kb, 0:n], kb == 0, kb == 3, [r_pw, r_lno], [rps])
                        TT(yo[b_][:, ob, 0:n], ps[:, 0:n], sag[b_][:, ob, 0:n], ALU.mult, [rps, r_sag[b_]], [r_yo[b_]])
                    fw.dma(sp, g.YIN[0:4, :, g0:g0 + n].rearrange("j p w -> p j w"), yo[b_][:, :, 0:n], reads=[r_yo[b_]], writes=g.rYIN[0:4])
                fw.barrier()

        def pooling(g, l, own=False):
            NT = 512 if g.L >= 512 else g.L
            Us = g.UE if own else g.U; rUs = g.rUE if own else g.rU
            with contextlib.ExitStack() as ph:
                def sb(name, shape, dt): return ph.enter_context(nc.sbuf_tensor(uq(name), list(shape), dt))
                psum_std(ph)
                pwt = sb("pwt", [128, 4, 128], BF16); r_pwt = Reg("pwt")
                psc = sb("psc", [128, 4], F32); r_psc = Reg("psc")
                xin = [sb("xin%d" % i, [128, 4, NT + 32], BF16) for i in range(2)]; r_xin = [Reg("xin%d" % i) for i in range(2)]
                scg = [sb("scg%d" % i, [128, 4, NT], BF16) for i in range(2)]; r_scg = [Reg("scg%d" % i) for i in range(2)]
                icn = [sb("icn%d" % i, [128, 4, NT], F32) for i in range(2)]; r_icn = [Reg("icn%d" % i) for i in range(2)]
                was = [sb("wa_%d" % i, [128, NT + 32], F32) for i in range(4)]; wbs_ = [sb("wb_%d" % i, [128, NT + 32], F32) for i in range(4)]
                r_was = [Reg("wa%d" % i) for i in range(4)]; r_wbs = [Reg("wb%d" % i) for i in range(4)]
                plds = [sb("pld%d" % i, [128, NT], BF16) for i in range(4)]; r_plds = [Reg("pld%d" % i) for i in range(4)]
                yo = [sb("yop%d" % i, [128, 4, NT], BF16) for i in range(2)]; r_yo = [Reg("yop%d" % i) for i in range(2)]
                fw.dma(pool, pwt[:], W['pool_w'][l].rearrange("g c d -> c g d"), writes=[r_pwt])
                colvec(psc[:], W['pool_scale'][l], 4, [r_psc])
                ict = invcnt_own if own else CT["invcnt_" + g.name]
                for it, (s, t0, n) in enumerate(own_tiles(own) if own else seq_tiles(g, NT)):
                    b_ = it % 2; g0 = s * g.L + t0
                    vmin, vmax, cb0 = (-128, 1152, 128) if own else (0, g.L, s * g.L)
                    lo = max(t0 - 16, vmin); hi = min(t0 + n + 16, vmax); off = lo - (t0 - 16); w = hi - lo
                    if off > 0 or w < n + 32:
                        MS(xin[b_][:], 0.0, [r_xin[b_]])
                    fw.dma(sp, xin[b_][:, :, off:off + w], Us[28:32, :, cb0 + lo:cb0 + hi].rearrange("j p w -> p j w"), reads=rUs[28:32], writes=[r_xin[b_]])
                    fw.dma(sp, scg[b_][:, :, 0:n], Us[32:36, :, cb0 + t0:cb0 + t0 + n].rearrange("j p w -> p j w"), reads=rUs[32:36], writes=[r_scg[b_]])
                    if own:
                        g0 = t0
                        if t0 == 0:
                            TS(xin[b_][:, :, 0:16], xin[b_][:, :, 0:16], maskLR[:, 0:1], None, ALU.mult, None, [r_xin[b_]], [r_xin[b_]])
                        if t0 + n == 1024:
                            TS(xin[b_][:, :, n + 16:n + 32], xin[b_][:, :, n + 16:n + 32], maskLR[:, 1:2], None, ALU.mult, None, [r_xin[b_]], [r_xin[b_]])
                    for gi in range(4):
                        fw.dma(sp, icn[b_][:, gi, 0:n], ict[gi:gi + 1, t0:t0 + n].partition_broadcast(128), writes=[r_icn[b_]])
                    m = n + 32
                    for gi in range(4):
                        wa = was[gi]; wb_ = wbs_[gi]; r_wa = r_was[gi]; r_wb = r_wbs[gi]; pld = plds[gi]; r_pld = r_plds[gi]
                        x_ = xin[b_][:, gi, :]
                        TT(wa[:, 1:m], x_[:, 0:m - 1], x_[:, 1:m], ALU.add, [r_xin[b_]], [r_wa])
                        cur, rcur, oth, roth = wa, r_wa, wb_, r_wb
                        lo_i = 1; hi_i = m
                        sh = 1
                        for lev in range(gi):
                            nlo = lo_i + sh; nhi = hi_i - sh
                            TT(oth[:, nlo:nhi], cur[:, nlo - sh:nhi - sh], cur[:, nlo + sh:nhi + sh], ALU.add, [rcur], [roth])
                            cur, rcur, oth, roth = oth, roth, cur, rcur
                            lo_i, hi_i = nlo, nhi; sh *= 2
                        TT(oth[:, 16:16 + n], cur[:, 16:16 + n], icn[b_][:, gi, 0:n], ALU.mult, [rcur, r_icn[b_]], [roth])
                        TT(pld[:, 0:n], oth[:, 16:16 + n], x_[:, 16:16 + n], ALU.subtract, [roth, r_xin[b_]], [r_pld])
                        ps, rps = PS()
                        MM(ps[:, 0:n], pwt[:, gi, :], pld[:, 0:n], True, True, [r_pwt, r_pld], [rps])
                        STT(yo[b_][:, gi, 0:n], ps[:, 0:n], psc[:, gi:gi + 1], scg[b_][:, gi, 0:n], ALU.mult, ALU.mult, [rps, r_psc, r_scg[b_]], [r_yo[b_]])
                    fw.dma(sp, g.YIN[8:12, :, g0:g0 + n].rearrange("j p w -> p j w"), yo[b_][:, :, 0:n], reads=[r_yo[b_]], writes=g.rYIN[8:12])
                fw.barrier()

        def attention(g, l, own=False):
            L = g.L; NT = 512 if L >= 512 else L
            Us = g.UE if own else g.U; rUs = g.rUE if own else g.rU
            nkeys = L + (PAST if g.latent else 0); nst = nkeys // 128
            with contextlib.ExitStack() as ph:
                def sb(name, shape, dt): return ph.enter_context(nc.sbuf_tensor(uq(name), list(shape), dt))
                sc = [ph.enter_context(nc.psum_tensor(uq("sc"), [128, 1024], F32)) for i in range(3)]; r_sc = [Reg("sc%d" % i) for i in range(3)]
                PSX.banks = [sc[i][:, k * 512:(k + 1) * 512] for i in range(3) for k in range(2)]
                PSX.regs = [r_sc[i] for i in range(3) for k in range(2)]
                K2 = sb("K2", [128, 4, g.nseq, nkeys], BF16); r_K2 = Reg("K2")
                Ve = sb("Ve", [128, g.nseq, nst, 4, 128], BF16); Vo = sb("Vo", [128, g.nseq, nst, 4, 128], BF16); r_V = Reg("V")
                gq = sb("gq", [128, 1], F32); gk = sb("gk", [128, 1], F32); r_gv = Reg("gqk")
                NSET = 2
                kraws = [sb("kraw%d" % i, [128, NT], F32) for i in range(NSET)]; r_kraws = [Reg("kraw%d" % i) for i in range(NSET)]
                sqs = [sb("sqa%d" % i, [128, NT], BF16) for i in range(NSET)]; r_sqs = [Reg("sqa%d" % i) for i in range(NSET)]
                rstds = [sb("rstd%d" % i, [128, NT], F32) for i in range(NSET)]; r_rstds = [Reg("rstd%d" % i) for i in range(NSET)]
                knfs = [sb("knf%d" % i, [128, NT], F32) for i in range(NSET)]; r_knfs = [Reg("knf%d" % i) for i in range(NSET)]
                knbs = [sb("knb%d" % i, [128, NT], BF16) for i in range(NSET)]; r_knbs = [Reg("knb%d" % i) for i in range(NSET)]
                t1s = [sb("t1a%d" % i, [128, NT], F32) for i in range(NSET)]; r_t1s = [Reg("t1a%d" % i) for i in range(NSET)]
                qsts = [sb("qst%d" % i, [128, NT], BF16) for i in range(NSET)]; r_qsts = [Reg("qst%d" % i) for i in range(NSET)]
                cosl = sb("cosl", [128, NT], F32); sinl = sb("sinl", [128, NT], F32); r_cs = Reg("cs")
                cosq = sb("cosq", [128, NT], F32); sinq = sb("sinq", [128, NT], F32); r_csq = Reg("csq")
                setc = [0]
                otm = sb("otm", [128, 256], F32); r_otm = Reg("otm")
                vraw = sb("vraw", [128, 2, NT], F32); r_vraw = Reg("vraw")
                MS(Ve[:], 1.0, [r_V]); MS(Vo[:], 1.0, [r_V])
                for h2 in range(2):
                    fw.dma(sp, gq[h2 * 64:(h2 + 1) * 64, :], W['q_norm'][l].rearrange("(d o) -> d o", o=1), writes=[r_gv])
                    fw.dma(sp, gk[h2 * 64:(h2 + 1) * 64, :], W['k_norm'][l].rearrange("(d o) -> d o", o=1), writes=[r_gv])
                TS(gq[:], gq[:], 0.125, None, ALU.mult, None, [r_gv], [r_gv])
                koff = PAST if g.latent else 0
                if g.latent:
                    ckd = sb("ckd", [128, 2, 4, 128], F32); r_ckd = Reg("ckd")
                    for st in range(2):
                        for dup in range(2):
                            fw.dma(sp, ckd[:, st, :, dup * 64:(dup + 1) * 64],
                                   ck_in[l, st * 128:(st + 1) * 128, :].rearrange("s (g d) -> s g d", g=4), writes=[r_ckd])
                    for st in range(2):
                        for hg in range(4):
                            ps, rps = PS()
                            TR(ps[:, 0:128], ckd[:, st, hg, :], ident_f[:], [r_ckd], [rps])
                            CP(K2[:, hg, 0, st * 128:(st + 1) * 128], ps[:, 0:128], [rps], [r_K2])
                        fw.dma(pool, Ve[:, 0, st, :, 0:64], cv_in[l, st * 128:(st + 1) * 128, :].rearrange("s (g d) -> s g d", g=4), writes=[r_V])
                        fw.dma(pool, Vo[:, 0, st, :, 64:128], cv_in[l, st * 128:(st + 1) * 128, :].rearrange("s (g d) -> s g d", g=4), writes=[r_V])
                def qk_norm(k, src, rsrc, gvec, n):
                    TT(sqs[k][:, 0:n], src, src, ALU.mult, rsrc, [r_sqs[k]])
                    ps, rps = PS()
                    MM(ps[:, 0:n], bones[:], sqs[k][:, 0:n], True, True, [r_sqs[k]], [rps])
                    ACT(rstds[k][:, 0:n], ps[:, 0:n], AF.Ln, [rps], [r_rstds[k]], bias=EPS)
                    ACT(rstds[k][:, 0:n], rstds[k][:, 0:n], AF.Exp, [r_rstds[k]], [r_rstds[k]], scale=-0.5)
                    STT(knfs[k][:, 0:n], src, gvec, rstds[k][:, 0:n], ALU.mult, ALU.mult, rsrc + [r_gv, r_rstds[k]], [r_knfs[k]])
                def rope_load(n, tok0, ownt=False, q=False):
                    c_, s_, r_ = (cosq, sinq, r_csq) if q else (cosl, sinl, r_cs)
                    fw.dma(sp, c_[:, 0:n], (ropec_own if ownt else CT["ropec"])[:, tok0:tok0 + n], writes=[r_])
                    fw.dma(sp, s_[:, 0:n], (ropes_own if ownt else CT["ropes"])[:, tok0:tok0 + n], writes=[r_])
                def rope(k, n, outs, q=False):
                    c_, s_, r_ = (cosq, sinq, r_csq) if q else (cosl, sinl, r_cs)
                    CP(knbs[k][:, 0:n], knfs[k][:, 0:n], [r_knfs[k]], [r_knbs[k]])
                    ps, rps = PS()
                    MM(ps[:, 0:n], rmat[:], knbs[k][:, 0:n], True, True, [r_knbs[k]], [rps])
                    TT(t1s[k][:, 0:n], ps[:, 0:n], s_[:, 0:n], ALU.mult, [rps, r_], [r_t1s[k]])
                    TT(knfs[k][:, 0:n], knfs[k][:, 0:n], c_[:, 0:n], ALU.mult, [r_knfs[k], r_], [r_knfs[k]])
                    for (psl, out_ap, rout) in outs:
                        TT(out_ap, knfs[k][psl, 0:n], t1s[k][psl, 0:n], ALU.add, [r_knfs[k], r_t1s[k]], rout)
                if g.latent:
                    for dup in range(2):
                        fw.dma(sp, K2[dup * 64:(dup + 1) * 64, :, 0, koff:koff + L], g.KR.rearrange("b (h d) t -> d (b h) t", h=2), reads=[g.rKR], writes=[r_K2])
                    src_v = g.VT[:, :, :].rearrange("p st (g d) -> p (st g) d", g=4)
                    nst0 = koff // 128
                    fw.dma(sp, Ve[:, 0, nst0:nst0 + L // 128, :, 0:64].rearrange("p st g d -> p (st g) d"), src_v, reads=[g.rVT], writes=[r_V])
                    fw.dma(sp, Vo[:, 0, nst0:nst0 + L // 128, :, 64:128].rearrange("p st g d -> p (st g) d"), src_v, reads=[g.rVT], writes=[r_V])
                for (s, t0, n) in ([] if g.latent else seq_tiles(g, NT)):
                    g0 = s * L + t0
                    for hg in range(4):
                        if g.latent:
                            for dup in range(2):
                                fw.dma(sp, K2[dup * 64:(dup + 1) * 64, hg, s, koff + t0:koff + t0 + n],
                                       g.KR[hg // 2][(hg % 2) * 64:(hg % 2) * 64 + 64, g0:g0 + n], reads=[g.rKR], writes=[r_K2])
                            continue
                        k = setc[0] % NSET; setc[0] += 1
                        kraw = kraws[k]; r_kraw = r_kraws[k]; knf = knfs[k]; r_knf = r_knfs[k]
                        for dup in range(2):
                            fw.dma(sp, kraw[dup * 64:(dup + 1) * 64, 0:n], g.UKV[hg // 2][(hg % 2) * 64:(hg % 2) * 64 + 64, g0:g0 + n],
                                   reads=[g.rUKV[hg // 2]], writes=[r_kraw])
                        qk_norm(k, kraw[:, 0:n], [r_kraw], gk[:, 0:1], n)
                        if g.latent:
                            if hg == 0: rope_load(n, t0)
                            rope(k, n, [(slice(0, 128), K2[:, hg, s, koff + t0:koff + t0 + n], [r_K2])])
                        else:
                            CP(K2[:, hg, s, t0:t0 + n], knf[:, 0:n], [r_knf], [r_K2])
                            for sub in range(n // 128):
                                ps, rps = PS()
                                TR(ps[:, 0:64], knf[0:64, sub * 128:(sub + 1) * 128], ident_f[0:64, 0:64], [r_knf], [rps])
                                CP(otm[:, hg * 64:(hg + 1) * 64], ps[:, 0:64], [rps], [r_otm]) if False else None
                                ACT(otm[:, 0:64], ps[:, 0:64], AF.Copy, [rps], [r_otm])
                                fw.dma(sp, nk_out[s, l, t0 + sub * 128:t0 + (sub + 1) * 128, hg * 64:(hg + 1) * 64], otm[:, 0:64], reads=[r_otm], writes=[r_nk])
                    if g.latent:
                        st0 = (koff + t0) // 128; nsub = n // 128
                        src_v = g.VT[:, g0 // 128:g0 // 128 + nsub, :].rearrange("p st (g d) -> p (st g) d", g=4)
                        fw.dma(sp, Ve[:, s, st0:st0 + nsub, :, 0:64].rearrange("p st g d -> p (st g) d"), src_v, reads=[g.rVT], writes=[r_V])
                        fw.dma(sp, Vo[:, s, st0:st0 + nsub, :, 64:128].rearrange("p st g d -> p (st g) d"), src_v, reads=[g.rVT], writes=[r_V])
                        continue
                    fw.dma(sp, vraw[:, :, 0:n], g.UKV[2:4, :, g0:g0 + n].rearrange("j p w -> p j w"), reads=g.rUKV[2:4], writes=[r_vraw])
                    for sub in range(n // 128):
                        st = (koff + t0) // 128 + sub
                        for vb in range(2):
                            ps, rps = PS()
                            TR(ps[:, 0:128], vraw[:, vb, sub * 128:(sub + 1) * 128], ident_f[:], [r_vraw], [rps])
                            CP(Ve[:, s, st, 2 * vb:2 * vb + 2, 0:64], ps[:, 0:128].rearrange("p (g d) -> p g d", g=2), [rps], [r_V])
                            ACT(Vo[:, s, st, 2 * vb:2 * vb + 2, 64:128], ps[:, 0:128].rearrange("p (g d) -> p g d", g=2), AF.Copy, [rps], [r_V])
                            if not g.latent:
                                ACT(otm[:, 0:128], ps[:, 0:128], AF.Copy, [rps], [r_otm])
                                fw.dma(sp, nv_out[s, l, t0 + sub * 128:t0 + (sub + 1) * 128, vb * 128:(vb + 1) * 128], otm[:, 0:128], reads=[r_otm], writes=[r_nv])
                qraw = [sb("qraw%d" % i, [128, 8, NT], BF16) for i in range(2)]; r_qraw = [Reg("qraw%d" % i) for i in range(2)]
                sdg = [sb("sdg0", [128, 8, NT], BF16)] * 2; r_sdg = [Reg("sdg0")] * 2
                Qz = sb("Qz", [128, 16, NT], BF16); r_Qz = Reg("Qz")
                MS(Qz[:], 0.0, [r_Qz])
                pT2 = [sb("pT2%d" % i, [128, 2, NT], BF16) for i in range(3)]; r_pT2 = [Reg("pT2%d" % i) for i in range(3)]
                pob = [ph.enter_context(nc.psum_tensor(uq("po"), [128, 512], F32)) for i in range(2)]; r_pob = [Reg("po%d" % i) for i in range(2)]
                dn = sb("dn", [128, NT], F32); r_dn = Reg("dn")
                att = sb("att", [128, NT], F32); r_att = Reg("att")
                yd = [sb("yd0", [128, 8, NT], BF16)] * 2; r_yd = [Reg("yd0")] * 2
                qtiles = own_tiles(own) if own else seq_tiles(g, NT)
                def qcols(it):
                    s_, t0_, n_ = qtiles[it]
                    return (t0_ if own else s_ * L + t0_)
                for it, (s_, t0_, n_) in enumerate([] if g.latent else qtiles):
                    cq0 = 128 + t0_ if own else s_ * L + t0_
                    qb_ = it % 2
                    fw.dma(sp, qraw[qb_][:, :, 0:n_], Us[36:44, :, cq0:cq0 + n_].rearrange("j p w -> p j w"), reads=rUs[36:44], writes=[r_qraw[qb_]])
                    if g.latent: rope_load(n_, t0_, own, q=True)
                    for qb in range(8):
                        k = setc[0] % NSET; setc[0] += 1
                        qk_norm(k, qraw[qb_][:, qb, 0:n_], [r_qraw[qb_]], gq[:, 0:1], n_)
                        if g.latent:
                            rope(k, n_, [(slice(0, 128), qsts[k][:, 0:n_], [r_qsts[k]])], q=True)
                        else:
                            CP(qsts[k][:, 0:n_], knfs[k][:, 0:n_], [r_knfs[k]], [r_qsts[k]])
                        fw.dma(sp, g.QR[qb][:, qcols(it):qcols(it) + n_], qsts[k][:, 0:n_], reads=[r_qsts[k]], writes=[g.rQR])
                def load_tile(it):
                    s_, t0_, n_ = qtiles[it]
                    c0 = qcols(it); cq0 = 128 + t0_ if own else s_ * L + t0_
                    for hh in range(2):
                        fw.dma(sp, Qz[hh * 64:(hh + 1) * 64, hh:16:2, 0:n_], g.QR[:, hh * 64:(hh + 1) * 64, c0:c0 + n_].rearrange("q p w -> p q w"),
                               reads=[g.rQR], writes=[r_Qz])
                    fw.dma(sp, sdg[0][:, :, 0:n_], Us[48:56, :, cq0:cq0 + n_].rearrange("j p w -> p j w"), reads=rUs[48:56], writes=[r_sdg[0]])
                for it, (s, t0, n) in enumerate(qtiles):
                    b_ = it % 2; g0 = s * L + t0
                    if own: g0 = t0
                    load_tile(it)
                    r_Qt = [r_Qz] * 8
                    npair = nst // 2
                    items = [(h, pr_) for h in range(16) for pr_ in range(npair)]
                    def emit_qk(idx):
                        h, pr_ = items[idx]
                        qb = h // 2; base = (h % 2) * 64; hg = h // 4
                        sc_ = sc[idx % 3]; rsc_ = r_sc[idx % 3]
                        for k2 in range(2):
                            st = pr_ * 2 + k2
                            MM(sc_[:, k2 * 512:k2 * 512 + n], K2[:, hg, s, st * 128:(st + 1) * 128], Qz[:, h, 0:n],
                               True, True, [r_K2, r_Qt[qb]], [rsc_])
                    def emit_pv(idx):
                        h, pr_ = items[idx]
                        qb = h // 2; base = (h % 2) * 64; hg = h // 4
                        Vt = Ve if base == 0 else Vo
                        sc_ = sc[idx % 3]; rsc_ = r_sc[idx % 3]
                        pi = idx % 3
                        po = pob[h % 2]; rpo = r_pob[h % 2]
                        ACT(pT2[pi][:, :, 0:n], sc_[:].rearrange("p (k w) -> p k w", k=2)[:, :, 0:n], AF.Exp, [rsc_], [r_pT2[pi]])
                        for k2 in range(2):
                            st = pr_ * 2 + k2
                            MM(po[:, 0:n], Vt[:, s, st, hg, :], pT2[pi][:, k2, 0:n], st == 0, st == nst - 1, [r_V, r_pT2[pi]], [rpo])
                        if pr_ == npair - 1:
                            nb_ = slice(base, base + 64); db_ = slice(64 - base, 128 - base)
                            CP(dn[nb_, 0:n], po[db_, 0:n], [rpo], [r_dn])
                            RCP(dn[nb_, 0:n], dn[nb_, 0:n], [r_dn], [r_dn])
                            TT(att[nb_, 0:n], po[nb_, 0:n], dn[nb_, 0:n], ALU.mult, [rpo, r_dn], [r_att])
                            TT(yd[b_][nb_, qb, 0:n], att[nb_, 0:n], sdg[b_][nb_, qb, 0:n], ALU.mult, [r_att, r_sdg[b_]], [r_yd[b_]])
                    for idx in range(len(items) + 2):
                        if idx < len(items): emit_qk(idx)
                        if idx >= 2: emit_pv(idx - 2)
                    fw.dma(sp, g.YIN[12:20, :, g0:g0 + n].rearrange("j p w -> p j w"), yd[b_][:, :, 0:n], reads=[r_yd[b_]], writes=g.rYIN[12:20])
                fw.barrier()

        def sin_reduce(ph_sb, x, rx, n, tag):
            ki, kf, mk, rk = ph_sb
            TS(ki[0:64, 0:n], x, 1.0 / TWO_PI, None, ALU.mult, None, rx, [rk])
            CP(kf[0:64, 0:n], ki[0:64, 0:n], [rk], [rk])
            STT(x, kf[0:64, 0:n], -TWO_PI, x, ALU.mult, ALU.add, [rk] + rx, rx)
            TS(mk[0:64, 0:n], x, math.pi, -TWO_PI, ALU.is_gt, ALU.mult, rx, [rk])
            TT(x, x, mk[0:64, 0:n], ALU.add, rx + [rk], rx)
            TS(mk[0:64, 0:n], x, -math.pi, TWO_PI, ALU.is_lt, ALU.mult, rx, [rk])
            TT(x, x, mk[0:64, 0:n], ALU.add, rx + [rk], rx)

        def hyena_filters(g, l):
            L = g.L; N2 = 2 * L; nm = g.name
            with contextlib.ExitStack() as ph:
                def sb(name, shape, dt): return ph.enter_context(nc.sbuf_tensor(uq(name), list(shape), dt))
                psum_std(ph, nb=4, with_pst=False) if g.latent else psum_std(ph)
                w1 = sb("w1", [33, 64], F32); w2 = sb("w2", [64, 64], F32); r_w = Reg("hw")
                b1 = sb("b1", [64, 1], F32); b2 = sb("b2", [64, 1], F32); fr = sb("fr", [64, 1], F32)
                w3 = sb("w3", [64, 2048], BF16); b3 = sb("b3", [128, 16], F32); ndl = sb("ndl", [128, 4], F32)
                h2 = sb("h2", [64, N2], BF16); r_h2 = Reg("h2")
                mlp_scope = contextlib.ExitStack()
                def sbm(name, shape, dt): return mlp_scope.enter_context(nc.sbuf_tensor(uq(name), list(shape), dt))
                trow = sb("trow", [128, N2], F32); r_trow = Reg("trow")
                decs = [sb("dec%d" % i, [128, 512], F32) for i in range(2)]; r_decs = [Reg("dec%d" % i) for i in range(2)]
                fts = [sb("ft%d" % i, [128, 512], F32) for i in range(2)]; r_fts = [Reg("ft%d" % i) for i in range(2)]
                sqjf = sb("sqjf", [128, 512], F32); r_sqjf = Reg("sqjf")
                fb = sb("fb", [128, N2], BF16); r_fb = Reg("fb")
                ssq = sb("ssq", [128, 16], F32); r_ssq = Reg("ssq"); rs = sb("rs", [128, 1], F32)
                zf = sbm("zf", [33, 512], F32); r_zf = Reg("zf")
                h1 = sbm("h1", [64, 512], F32); r_h1 = Reg("h1")
                ki = sbm("ki", [64, 512], I32); kf = sbm("kf", [64, 512], F32); mk = sbm("mk", [64, 512], F32); rk = Reg("rk")
                fw.dma(sp, w1[:], W['hy_w1'][l], writes=[r_w]); fw.dma(sp, w2[:], W['hy_w2'][l], writes=[r_w])
                for (t_, nm_) in ((b1, 'hy_b1'), (b2, 'hy_b2'), (fr, 'hy_freq')):
                    fw.dma(sp, t_[:], W[nm_][l].rearrange("(d o) -> d o", o=1), writes=[r_w])
                fw.dma(pool, w3[:], W['hy_w3'][l], writes=[r_w])
                colvec(b3[:], W['hy_b3'][l], 16, [r_w])
                fw.dma(sp, ndl[:], CT["negdelta"][:, :], writes=[r_w])
                fw.dma(sp, trow[:], CT["trow_" + nm][0:1, :].partition_broadcast(128), writes=[r_trow])
                for ch in range(N2 // 512):
                    fw.dma(sp, zf[:], CT["zf_" + nm][:, ch * 512:(ch + 1) * 512], writes=[r_zf])
                    ps, rps = PS()
                    MM(ps[0:64, :], w1[:], zf[:], True, True, [r_w, r_zf], [rps])
                    TS(h1[:], ps[0:64, :], b1[:, 0:1], fr[:, 0:1], ALU.add, ALU.mult, [rps, r_w], [r_h1])
                    sin_reduce((ki, kf, mk, rk), h1[:], [r_h1], 512, "a")
                    ACT(h1[:], h1[:], AF.Sin, [r_h1], [r_h1])
                    ps, rps = PS()
                    MM(ps[0:64, :], w2[:], h1[:], True, True, [r_w, r_h1], [rps])
                    TS(h1[:], ps[0:64, :], b2[:, 0:1], fr[:, 0:1], ALU.add, ALU.mult, [rps, r_w], [r_h1])
                    sin_reduce((ki, kf, mk, rk), h1[:], [r_h1], 512, "b")
                    ACT(h2[:, ch * 512:(ch + 1) * 512], h1[:], AF.Sin, [r_h1], [r_h2])
                fw.barrier()
                mlp_scope.close()
                if g.latent:
                    Dms = [sb("Dm%d" % i, [128, 64, 128], BF16) for i in range(2)]; r_Ds = [[Reg("D%d" % i)] for i in range(2)]
                    hcnt = [0]
                    Abuf = sb("Abuf", [128, 64, 3, 64], BF16); r_A = Reg("A")
                    Xb = sb("Xb", [128, 3, 64, 64], BF16); r_X = [Reg("X%d" % i) for i in range(4)]
                    wa_t = sb("wa_t", [128, 192], BF16); r_tab = Reg("tab")
                    tbc = sb("tbc", [128, 64, 128], BF16); tbs = sb("tbs", [128, 64, 128], BF16)
                    pa = [ph.enter_context(nc.psum_tensor(uq("pa"), [128, 1024], F32)) for i in range(2)]; r_pa = [Reg("pa%d" % i) for i in range(2)]
                    fw.dma(sp, wa_t[:], CT["wa"][:, :], writes=[r_tab])
                    fw.dma(sp, tbc[:], CT["tb_c"][:, :, :], writes=[r_tab]); fw.dma(sp, tbs[:], CT["tb_s"][:, :, :], writes=[r_tab])
                else:
                    Zt = sb("Zt", [128, 4, 128], BF16); r_Zt = Reg("Zt")
                    fpc = sb("fpc", [128, 4, 512], BF16); fps = sb("fps", [128, 4, 512], BF16); r_tab = Reg("tab")
                    Xp = sb("Xp", [128, 2, 4, 128], BF16); r_X = Reg("X")
                    fw.dma(sp, fpc[:], CT["fp_c"][:, :, :], writes=[r_tab]); fw.dma(sp, fps[:], CT["fp_sn"][:, :, :], writes=[r_tab])
                for o in range(2):
                    for cb in range(4):
                        ssi = 0
                        for half in range(2):
                            col0 = half * 1024 + o * 512 + cb * 128
                            bcol = col0 // 128
                            for ch in range(L // 512 if L >= 512 else 1):
                                n = min(512, L); p0 = half * L + ch * 512
                                kq = ssi % 2
                                dec = decs[kq]; r_dec = r_decs[kq]; ft = fts[kq]; r_ft = r_fts[kq]
                                ps, rps = PS()
                                MM(ps[:, 0:n], w3[:, col0:col0 + 128], h2[:, p0:p0 + n], True, True, [r_w, r_h2], [rps])
                                ACT(dec[:, 0:n], trow[:, p0:p0 + n], AF.Exp, [r_trow, r_w], [r_dec], scale=ndl[:, cb:cb + 1])
                                STT(ft[:, 0:n], ps[:, 0:n], b3[:, bcol:bcol + 1], dec[:, 0:n], ALU.add, ALU.mult, [rps, r_w, r_dec], [r_ft])
                                CP(fb[:, p0:p0 + n], ft[:, 0:n], [r_ft], [r_fb])
                                ACT(sqjf[:, 0:n], ft[:, 0:n], AF.Square, [r_ft], [r_sqjf, r_ssq], accum_out=ssq[:, ssi:ssi + 1]); ssi += 1
                        fw.op(dve, lambda e: e.reduce_sum(out=rs[:], in_=ssq[:, 0:ssi], axis=mybir.AxisListType.X), [r_ssq], [r_ssq])
                        ACT(rs[:], rs[:], AF.Sqrt, [r_ssq], [r_ssq], bias=EPS)
                        RCP(rs[:], rs[:], [r_ssq], [r_ssq])
                        TS(fb[:], fb[:], rs[:, 0:1], None, ALU.mult, None, [r_fb, r_ssq], [r_fb])
                        if g.latent:
                            fw.dma(sp, g.ZB[:, :], fb[:], reads=[r_fb], writes=[g.rZB])
                            for half in range(2):
                                Dm = Dms[(hcnt[0] + half) % 2]; r_D = r_Ds[(hcnt[0] + half) % 2]
                                fw.dma(sp, Dm[0:64], g.ZB[half * 64:(half + 1) * 64, :].rearrange("c (j a) -> j c a", a=128), reads=[g.rZB], writes=r_D)
                            for half in range(2):
                                Dm = Dms[(hcnt[0] + half) % 2]; r_D = r_Ds[(hcnt[0] + half) % 2]
                                fft_fwd_sample(Dm, r_D, 64, Abuf, r_A, Xb, r_X, wa_t, r_tab, tbc, tbs, pa, r_pa)
                                for ri in range(2):
                                    fw.dma(sp, g.HS[o, cb, half, ri], Xb[:, 1 + ri].rearrange("p g c -> p (g c)"), reads=r_X, writes=[g.rHS])
                        else:
                            for tt in range(4):
                                TR(PSX.pst[:, tt * 128:(tt + 1) * 128], fb[:, tt * 128:(tt + 1) * 128], ident_b[:], [r_fb], [PSX.rpst])
                            ACT(Zt[:], PSX.pst[:, 0:512].rearrange("p (t c) -> p t c", t=4), AF.Copy, [PSX.rpst], [r_Zt])
                            for ri, tab in ((0, fpc), (1, fps)):
                                ps, rps = PS()
                                for kt in range(4):
                                    for tt in range(4):
                                        MM(ps[:, kt * 128:(kt + 1) * 128], tab[:, tt, kt * 128:(kt + 1) * 128], Zt[:, tt, :], tt == 0, tt == 3, [r_tab, r_Zt], [rps])
                                ACT(Xp[:, ri], ps[:].rearrange("p (k c) -> p k c", k=4), AF.Copy, [rps], [r_X])
                                fw.dma(sp, g.HS[o, cb, ri], Xp[:, ri].rearrange("p k c -> p (k c)"), reads=[r_X], writes=[g.rHS])
                fw.barrier()

        def fft_fwd_sample(Dm, r_D, J, Abuf, r_A, Xb, r_X, wa_t, r_tab, tbc, tbs, pa, r_pa):
            for i4, c0 in enumerate(range(0, 64, 4)):
                pa_ = pa[i4 % 2]; rpa_ = r_pa[i4 % 2]
                for ci in range(4):
                    o_ = (ci // 2) * 512 + (ci % 2) * 192
                    MM(pa_[:, o_:o_ + 192], Dm[0:J, c0 + ci, :], wa_t[0:J, :], True, True, r_D + [r_tab], [rpa_])
                src_ = pa_[:].rearrange("p (b x) -> p b x", b=2)[:, :, 0:384]
                dst_ = Abuf[:, c0:c0 + 4].rearrange("p (b c) r g -> p b (c r g)", b=2)
                if i4 % 2:
                    CP(dst_, src_, [rpa_], [r_A])
                else:
                    ACT(dst_, src_, AF.Copy, [rpa_], [r_A])
            for g4 in range(16):
                pq, rpq = PS()
                for gi in range(4):
                    gg = g4 * 4 + gi
                    MM(pq[:, gi * 128:(gi + 1) * 128], tbc[:, gg, :], Abuf[:, :, 0:2, gg].rearrange("p c r -> p r c"), True, False, [r_tab, r_A], [rpq])
                    MM(pq[:, gi * 128:(gi + 1) * 128], tbs[:, gg, :], Abuf[:, :, 1:3, gg].rearrange("p c r -> p r c"), False, True, [r_tab, r_A], [rpq])
                pv4 = pq[:].rearrange("p (g r c) -> p g r c", g=4, r=2)
                ACT(Xb[:, 1, g4 * 4:(g4 + 1) * 4, :], pv4[:, :, 0, :], AF.Copy, [rpq], [r_X[g4 // 4]] + (r_D if g4 == 0 else []))
                CP(Xb[:, 2, g4 * 4:(g4 + 1) * 4, :], pv4[:, :, 1, :], [rpq], [r_X[g4 // 4]] + (r_D if g4 == 0 else []))

        def hyena(g, l, own=False):
            L = g.L; T = g.T
            with contextlib.ExitStack() as ph:
                def sb(name, shape, dt): return ph.enter_context(nc.sbuf_tensor(uq(name), list(shape), dt))
                psum_std(ph, nb=4, with_pst=False) if g.latent else psum_std(ph)
                shw = sb("shw", [128, 12, 3], F32); shb = sb("shb", [128, 12], F32); skp = sb("skp", [128, 2, 4], F32); r_hv = Reg("hv")
                for k in range(3):
                    fw.dma(sp, shw[:, :, k], W['hy_short_w'][l][k, :].rearrange("(b p) -> p b", p=128), writes=[r_hv], allow_slow_non_contiguous=True)
                colvec(shb[:], W['hy_short_b'][l], 12, [r_hv])
                for o in range(2):
                    fw.dma(sp, skp[:, o, :], W['hy_skip'][l][o, :].rearrange("(b p) -> p b", p=128), writes=[r_hv], allow_slow_non_contiguous=True)
                raw = sb("raw", [128, g.nseq, L + 2], BF16); r_raw = Reg("raw")
                bufA = sb("bufA", [128, T], F32); bufB = sb("bufB", [128, T], F32); r_bA = Reg("bufA"); r_bB = Reg("bufB")
                xg = sb("xg", [128, T], F32); r_xg = Reg("xg")
                zb = sb("zbh", [128, T], BF16); r_zb = Reg("zbh")
                sbg = raw[:, 0, 0:T] if g.latent else sb("sbg", [128, T], BF16); r_sbg = r_raw if g.latent else Reg("sbg")
                yo = zb; r_yo = r_zb
                if g.latent:
                    DA = sb("DA", [128, 20480], BF16)
                    r_Dlo = Reg("Dlo"); r_Dhi = Reg("Dhi"); r_D = [r_Dlo, r_Dhi]
                    Abuf = DA[:, 8192:20480].rearrange("p (c r g) -> p c r g", r=3, g=64); r_A = Reg("A")
                    Pb = DA[:, 0:8192].rearrange("p (c n) -> p c n", n=128); r_Plo = Reg("Plo"); r_Phi = Reg("Phi"); r_P = [r_Plo, r_Phi]
                    Xb = sb("Xb", [128, 3, 64, 64], BF16); r_X = [Reg("X%d" % i) for i in range(4)]
                    Dm = Xb[:].rearrange("p r g c -> p (r g c)")[:, 0:8192].rearrange("p (c a) -> p c a", a=128)
                    Hb = [sb("Hb%d" % i, [128, 2, 64 * 64], BF16) for i in range(1)]; r_H = Reg("H")
                    QC = 1024
                    tq = [sb("tq%d" % i, [128, QC], BF16) for i in range(4)]; r_tq = [Reg("tq%d" % i) for i in range(4)]
                    wa_t = sb("wa_t", [128, 192], BF16); r_tab = Reg("tab")
                    tbc = sb("tbc", [128, 64, 128], BF16); tbs = sb("tbs", [128, 64, 128], BF16)
                    pa = [ph.enter_context(nc.psum_tensor(uq("pa"), [128, 1024], F32)) for i in range(2)]; r_pa = [Reg("pa%d" % i) for i in range(2)]
                    fw.dma(sp, tbc[:], CT["tb_c"][:, :, :], writes=[r_tab]); fw.dma(sp, tbs[:], CT["tb_s"][:, :, :], writes=[r_tab])
                    tpc = sb("tpc", [128, 128], BF16); tps = sb("tps", [128, 128], BF16)
                    tcst = sb("tcst", [128, 128, 32], BF16)
                    fw.dma(sp, wa_t[:], CT["wa"][:, :], writes=[r_tab]); fw.dma(sp, tpc[:], CT["tbp_c"][:, :], writes=[r_tab])
                    fw.dma(sp, tps[:], CT["tbp_s"][:, :], writes=[r_tab]); fw.dma(sp, tcst[:], CT["tc_st"][:, :, :], writes=[r_tab])
                else:
                    Zt = sb("Zt", [128, 2, 2, 128], BF16); r_Zt = Reg("Zt")
                    fpc = sb("fpc", [128, 4, 512], BF16); fps = sb("fps", [128, 4, 512], BF16); r_tab = Reg("tab")
                    fic = sb("fic", [128, 4, 256], BF16); fis = sb("fis", [128, 4, 256], BF16)
                    Xp = sb("Xp", [128, 2, 4, 2, 128], BF16); r_X = Reg("X")
                    Hp = sb("Hp", [128, 2, 4, 128], BF16); r_H = Reg("H")
                    tq = [sb("tq%d" % i, [128, 4, 2, 128], BF16) for i in range(4)]; r_tq = [Reg("tq%d" % i) for i in range(4)]
                    for t_, nm_ in ((fpc, "fp_c"), (fps, "fp_sn"), (fic, "fi_c"), (fis, "fi_sn")):
                        fw.dma(sp, t_[:], CT[nm_][:, :, :], writes=[r_tab])

                def short_conv(blk, dst, rdst):
                    MS(raw[:, :, 0:1], 0.0, [r_raw]); MS(raw[:, :, L + 1:L + 2], 0.0, [r_raw])
                    fw.dma(sp, raw[:, :, 1:L + 1], g.U[12 + blk].rearrange("p (s t) -> p s t", s=g.nseq), reads=[g.rU[12 + blk]], writes=[r_raw])
                    d3 = dst.rearrange("p (s t) -> p s t", s=g.nseq)
                    TS(d3, raw[:, :, 0:L], shw[:, blk, 0:1], shb[:, blk:blk + 1], ALU.mult, ALU.add, [r_raw, r_hv], rdst)
                    STT(d3, raw[:, :, 1:L + 1], shw[:, blk, 1:2], d3, ALU.mult, ALU.add, [r_raw, r_hv] + rdst, rdst)
                    STT(d3, raw[:, :, 2:L + 2], shw[:, blk, 2:3], d3, ALU.mult, ALU.add, [r_raw, r_hv] + rdst, rdst)

                def longconv_sample(src, rsrc, dst, rdst, o, cb):
                    CP(zb[:], src, rsrc, [r_zb])
                    fw.dma(sp, g.ZB[:, 0:T], zb[:], reads=[r_zb], writes=[g.rZB])
                    for half in range(2):
                        pass
                        fw.dma(sp, Dm[0:32], g.ZB[half * 64:(half + 1) * 64, 0:T].rearrange("c (j a) -> j c a", a=128), reads=[g.rZB], writes=[r_Dlo, r_Dhi, r_A] + r_X)
                        fw.dma(sp, Hb[0][:], g.HS[o, cb, half].rearrange("r p x -> p r x"), reads=[g.rHS], writes=[r_H])
                        fft_fwd_sample(Dm, r_D, 32, Abuf, r_A, Xb, r_X, wa_t, r_tab, tbc, tbs, pa, r_pa)
                        for q in range(4096 // QC):
                            sl = slice(q * QC, (q + 1) * QC)
                            xr = Xb[:, 1].rearrange("p g c -> p (g c)")[:, sl]; xi = Xb[:, 2].rearrange("p g c -> p (g c)")[:, sl]
                            xn = Xb[:, 0].rearrange("p g c -> p (g c)")[:, sl]
                            hr = Hb[0][:, 0, sl]; hi = Hb[0][:, 1, sl]
                            rxq = r_X[q * QC // 1024]
                            TT(tq[0][:], xr, hr, ALU.mult, [rxq, r_H], [r_tq[0]])
                            TT(tq[1][:], xi, hi, ALU.mult, [rxq, r_H], [r_tq[1]])
                            TT(tq[2][:], xr, hi, ALU.mult, [rxq, r_H], [r_tq[2]])
                            TT(tq[3][:], xi, hr, ALU.mult, [rxq, r_H], [r_tq[3]])
                            TT(xr, tq[0][:], tq[1][:], ALU.subtract, [r_tq[0], r_tq[1]], [rxq])
                            TT(xi, tq[2][:], tq[3][:], ALU.add, [r_tq[2], r_tq[3]], [rxq])
                            STT(xn, tq[2][:], -1.0, tq[3][:], ALU.mult, ALU.subtract, [r_tq[2], r_tq[3]], [rxq])
                        for i4, c0 in enumerate(range(0, 64, 4)):
                            ps, rps = PS()
                            for ci in range(4):
                                MM(ps[:, ci * 128:(ci + 1) * 128], Xb[:, 1:3, :, c0 + ci], tpc[:], True, False, r_X + [r_tab], [rps])
                                MM(ps[:, ci * 128:(ci + 1) * 128], Xb[:, 0:2, :, c0 + ci], tps[:], False, True, r_X + [r_tab], [rps])
                            if i4 % 2:
                                CP(Pb[:, c0:c0 + 4, :], ps[:].rearrange("p (c n) -> p c n", c=4), [rps], r_P)
                            else:
                                ACT(Pb[:, c0:c0 + 4, :], ps[:].rearrange("p (c n) -> p c n", c=4), AF.Copy, [rps], r_P)
                        for n16 in range(8):
                            ps, rps = PS()
                            for bi in range(16):
                                nb = n16 * 16 + bi
                                pv_ = ps[0:64, :].rearrange("p (j b) -> p j b", b=16)[:, :, bi]
                                MM(pv_, Pb[:, :, nb], tcst[:, nb, :], True, True, r_P + [r_tab], [rps])
                            dv = dst[half * 64:(half + 1) * 64, :].rearrange("p (j a) -> p j a", a=128)[:, :, n16 * 16:(n16 + 1) * 16]
                            if n16 % 2:
                                CP(dv, ps[0:64, :].rearrange("p (j b) -> p j b", b=16), [rps], rdst)
                            else:
                                ACT(dv, ps[0:64, :].rearrange("p (j b) -> p j b", b=16), AF.Copy, [rps], rdst)

                def longconv_prompt(src, rsrc, dst, rdst, o, cb):
                    CP(zb[:], src, rsrc, [r_zb])
                    fw.dma(sp, Hp[:], g.HS[o, cb].rearrange("r p (k c) -> p r k c", k=4), reads=[g.rHS], writes=[r_H])
                    for s in range(2):
                        for tt in range(2):
                            TR(PSX.pst[:, (s * 2 + tt) * 128:(s * 2 + tt + 1) * 128], zb[:, s * L + tt * 128:s * L + (tt + 1) * 128], ident_b[:], [r_zb], [PSX.rpst])
                    ACT(Zt[:].rearrange("p t s c -> p s t c"), PSX.pst[:, 0:512].rearrange("p (s t c) -> p s t c", s=2, t=2), AF.Copy, [PSX.rpst], [r_Zt])
                    for ri, tab in ((0, fpc), (1, fps)):
                        for k2 in range(2):
                            ps, rps = PS()
                            for kk in range(2):
                                kt = k2 * 2 + kk
                                for tt in range(2):
                                    MM(ps[:, kk * 256:(kk + 1) * 256], tab[:, tt, kt * 128:(kt + 1) * 128], Zt[:, tt].rearrange("p s c -> p (s c)"),
                                       tt == 0, tt == 1, [r_tab, r_Zt], [rps])
                            ACT(Xp[:, ri, k2 * 2:k2 * 2 + 2].rearrange("p k s c -> p (k s c)"), ps[:], AF.Copy, [rps], [r_X])
                    for s in range(2):
                        xr = Xp[:, 0, :, s, :]; xi = Xp[:, 1, :, s, :]; hr = Hp[:, 0]; hi = Hp[:, 1]
                        TT(tq[0][:, :, s, :], xr, hr, ALU.mult, [r_X, r_H], [r_tq[0]])
                        TT(tq[1][:, :, s, :], xi, hi, ALU.mult, [r_X, r_H], [r_tq[1]])
                        TT(tq[2][:, :, s, :], xr, hi, ALU.mult, [r_X, r_H], [r_tq[2]])
                        TT(tq[3][:, :, s, :], xi, hr, ALU.mult, [r_X, r_H], [r_tq[3]])
                    TT(Xp[:, 0], tq[0][:], tq[1][:], ALU.subtract, [r_tq[0], r_tq[1]], [r_X])
                    TT(Xp[:, 1], tq[2][:], tq[3][:], ALU.add, [r_tq[2], r_tq[3]], [r_X])
                    for s in range(2):
                        ps, rps = PS()
                        for kt in range(4):
                            MM(ps[:, 0:256], Xp[:, 0, kt, s, :], fic[:, kt, :], kt == 0, False, [r_X, r_tab], [rps])
                            MM(ps[:, 0:256], Xp[:, 1, kt, s, :], fis[:, kt, :], False, kt == 3, [r_X, r_tab], [rps])
                        ACT(dst[:, s * L:(s + 1) * L], ps[:, 0:256], AF.Copy, [rps], rdst)

                longconv = longconv_sample if g.latent else longconv_prompt
                for cb in range(4):
                    short_conv(cb, bufA[:], [r_bA])
                    short_conv(4 + cb, xg[:], [r_xg])
                    longconv(bufA[:], [r_bA], bufB[:], [r_bB], 0, cb)
                    STT(bufB[:], bufA[:], skp[:, 0, cb:cb + 1], bufB[:], ALU.mult, ALU.add, [r_bA, r_bB, r_hv], [r_bB])
                    TT(bufB[:], bufB[:], xg[:], ALU.mult, [r_bB, r_xg], [r_bB])
                    short_conv(8 + cb, xg[:], [r_xg])
                    longconv(bufB[:], [r_bB], bufA[:], [r_bA], 1, cb)
                    STT(bufA[:], bufB[:], skp[:, 1, cb:cb + 1], bufA[:], ALU.mult, ALU.add, [r_bA, r_bB, r_hv], [r_bA])
                    TT(bufA[:], bufA[:], xg[:], ALU.mult, [r_bA, r_xg], [r_bA])
                    fw.dma(sp, sbg[:], g.U[24 + cb], reads=[g.rU[24 + cb]], writes=[r_sbg])
                    TT(yo[:], bufA[:], sbg[:], ALU.mult, [r_bA, r_sbg], [r_yo])
                    if own:
                        fw.dma(sp, g.YB[cb].rearrange("(q p) w -> p q w", p=128), yo[:].rearrange("p (q w) -> p q w", q=4), reads=[r_yo], writes=[g.rYB])
                    else:
                        fw.dma(sp, g.YIN[4 + cb], yo[:], reads=[r_yo], writes=[g.rYIN[4 + cb]])
                fw.barrier()

        def tail(g, l, x_src, r_xsrc, x_dst, r_xdst, own=False):
            T = 1024 if own else g.T
            with contextlib.ExitStack() as ph:
                def sb(name, shape, dt): return ph.enter_context(nc.sbuf_tensor(uq(name), list(shape), dt))
                psum_std(ph)
                wo = sb("wo", [128, 20, D], BF16); r_wo = Reg("wo")
                wout = sb("wout", [128, 8, D], BF16); r_wout = Reg("wout")
                gate = sb("gate", [128, D], F32); r_gate = Reg("gate")
                badab = sb("badab", [128, D], F32); r_bada = Reg("bada")
                ccol = sb("ccol", [128, 8], F32); r_cc = Reg("cc")
                crep = sb("crep", [128, 8, 128], BF16); r_crep = Reg("crep")
                wsl = [sb("wsl%d" % i, [128, 8, 512], BF16) for i in range(2)]; r_wsl = [Reg("wsl%d" % i) for i in range(2)]
                yin = [sb("yin%d" % i, [128, 20, 512], BF16) for i in range(2)]; r_yin = [Reg("yin%d" % i) for i in range(2)]
                gmf = [sb("gmf%d" % i, [128, 4, 512], BF16) for i in range(3)]; r_gmf = [Reg("gmf%d" % i) for i in range(3)]
                mrg = sb("mrg", [128, 8, 512], F32); r_mrg = [Reg("mrg%d" % i) for i in range(8)]
                mrb = sb("mrb", [128, 8, 512], BF16); r_mrb = [Reg("mrb%d" % i) for i in range(8)]
                tmp = sb("tmpt", [128, 512], F32); r_tmp = Reg("tmpt")
                xt = [sb("xtt%d" % i, [128, D], F32) for i in range(2)]; r_xt = [Reg("xtt%d" % i) for i in range(2)]
                xo = [sb("xo%d" % i, [128, D], F32) for i in range(2)]; r_xo = [Reg("xo%d" % i) for i in range(2)]
                for bi, nm_ in ((0, 'wo_conv'), (4, 'wo_hyena'), (8, 'wo_pool')):
                    fw.dma(pool, wo[:, bi:bi + 4, :], W[nm_][l].rearrange("(kt p) n -> p kt n", p=128), writes=[r_wo])
                fw.dma(pool, wo[:, 12:20, :], W['wo_attn'][l].rearrange("(kt p) n -> p kt n", p=128), writes=[r_wo])
                fw.dma(pool, wout[:], W['w_out'][l].rearrange("(kt p) n -> p kt n", p=128), writes=[r_wout])
                fw.dma(sp, ccol[:], cvec[g.crow, :].rearrange("(b p) -> p b", p=128), writes=[r_cc], allow_slow_non_contiguous=True)
                fw.dma(sp, badab[:], W['b_ada'][l:l + 1, 2 * D:3 * D].partition_broadcast(128), writes=[r_bada])
                ACT(ccol[:], ccol[:], AF.Silu, [r_cc], [r_cc])
                CP(crep[:], ccol[:].unsqueeze(2).broadcast_to([128, 8, 128]), [r_cc], [r_crep])
                wada = W['w_ada'][l].rearrange("(kt p) n -> p kt n", p=128)
                for cb in range(2):
                    fw.dma(pool, wsl[cb][:], wada[:, :, 2 * D + cb * 512:2 * D + (cb + 1) * 512], writes=[r_wsl[cb]])
                    ps, rps = PS()
                    for kt in range(8):
                        MM(ps[:], crep[:, kt, :], wsl[cb][:, kt, :], kt == 0, kt == 7, [r_crep, r_wsl[cb]], [rps])
                    TT(gate[:, cb * 512:(cb + 1) * 512], ps[:], badab[:, cb * 512:(cb + 1) * 512], ALU.add, [rps, r_bada], [r_gate])
                kts = (4, 4, 4, 8); kb0 = (0, 4, 8, 12)
                if own:
                    ybo = sb("ybo", [128, 4, 1024], BF16); r_ybo = Reg("ybo")
                    for cb in range(4):
                        fw.idma(pool, ybo[:, cb, :], g.YB[cb][:, :], idxQ[:, 0:1], reads=[g.rYB], writes=[r_ybo])
                def load_yin(tt_):
                    fw.dma(sp, yin[tt_ % 2][:], g.YIN[:, :, tt_ * 512:tt_ * 512 + 512].rearrange("j p w -> p j w"), reads=g.rYIN, writes=[r_yin[tt_ % 2]])
                load_yin(0)
                for tt in range(T // 512):
                    b_ = tt % 2; t0 = tt * 512
                    if tt + 1 < T // 512: load_yin(tt + 1)
                    if own:
                        CP(yin[b_][:, 4:8, :], ybo[:, :, t0:t0 + 512], [r_ybo], [r_yin[b_]])
                    for f in range(8):
                        gi_ = (tt * 8 + f) % 3
                        if own:
                            fw.dma(sp, gmf[gi_][:], g.UE[56 + f:88:8, :, 128 + t0:128 + t0 + 512].rearrange("j p w -> p j w"), reads=g.rUE[56 + f:88:8], writes=[r_gmf[gi_]])
                        else:
                            fw.dma(sp, gmf[gi_][:], g.U[56 + f:88:8, :, t0:t0 + 512].rearrange("j p w -> p j w"), reads=g.rU[56 + f:88:8], writes=[r_gmf[gi_]])
                        for br in range(4):
                            ps, rps = PS()
                            for kt in range(kts[br]):
                                MM(ps[:], wo[:, kb0[br] + kt, f * 128:(f + 1) * 128], yin[b_][:, kb0[br] + kt, :], kt == 0, kt == kts[br] - 1,
                                   [r_wo, r_yin[b_]], [rps])
                            if br == 0:
                                TT(mrg[:, f, :], ps[:], gmf[gi_][:, br, :], ALU.mult, [rps, r_gmf[gi_]], [r_mrg[f]])
                            else:
                                TT(tmp[:], ps[:], gmf[gi_][:, br, :], ALU.mult, [rps, r_gmf[gi_]], [r_tmp])
                                TT(mrg[:, f, :], mrg[:, f, :], tmp[:], ALU.add, [r_mrg[f], r_tmp], [r_mrg[f]])
                        ACT(mrb[:, f, :], mrg[:, f, :], AF.Copy, [r_mrg[f]], [r_mrb[f]])
                    for sub in range(4):
                        xb_ = (tt * 4 + sub) % 2; tok0 = t0 + sub * 128
                        if own:
                            fw.idma(pool, xt[xb_][:], x_src[:, :], idxE[:, 1 + tt * 4 + sub:2 + tt * 4 + sub], reads=[r_xsrc], writes=[r_xt[xb_]])
                        else:
                            fw.dma(sp, xt[xb_][:], x_src[tok0:tok0 + 128, :], reads=[r_xsrc], writes=[r_xt[xb_]])
                        for nchk in range(2):
                            ps, rps = PS()
                            for kt in range(8):
                                MM(ps[:], mrb[:, kt, sub * 128:(sub + 1) * 128], wout[:, kt, nchk * 512:(nchk + 1) * 512], kt == 0, kt == 7,
                                   [r_mrb[kt], r_wout], [rps])
                            TT(tmp[:], ps[:], gate[:, nchk * 512:(nchk + 1) * 512], ALU.mult, [rps, r_gate], [r_tmp])
                            TT(xo[xb_][:, nchk * 512:(nchk + 1) * 512], tmp[:], xt[xb_][:, nchk * 512:(nchk + 1) * 512], ALU.add, [r_tmp, r_xt[xb_]], [r_xo[xb_]])
                        fw.dma(sp, x_dst[tok0:tok0 + 128, :], xo[xb_][:], reads=[r_xo[xb_]], writes=[r_xdst])
                fw.barrier()

        r_in = Reg("xin")
        for g in groups:
            for l in range(cfg["layers"]):
                last = (l == cfg["layers"] - 1)
                x_src = g.x_in if l == 0 else g.x_mid
                r_src = r_in if l == 0 else g.r_xmid
                x_dst = g.x_out if last else g.x_mid
                r_dst = g.r_xout if last else g.r_xmid
                own = g.latent and last
                if own:
                    dense_in(g, l, x_src, r_src, wbs=(3, 4, 5, 6, 11))
                    dense_in(g, l, x_src, r_src, wbs=tuple(w for w in range(22) if w not in (3, 4, 5, 6, 11)), ext=True)
                else:
                    dense_in(g, l, x_src, r_src)
                conformer(g, l, own)
                pooling(g, l, own)
                attention(g, l, own)
                hyena_filters(g, l)
                hyena(g, l, own)
                tail(g, l, x_src, r_src, x_dst, r_dst, own)
        fw.barrier()
        nops = fw.nops
    return nc, nops

_PROG = {}
def _cfg():
    return dict(prompt=os.environ.get("K_PROMPT", "1") == "1", sample=os.environ.get("K_SAMPLE", "1") == "1",
                layers=int(os.environ.get("K_LAYERS", "2")))

def kernel(**inputs):
    cfg = _cfg()
    key = tuple(sorted(cfg.items()))
    if key not in _PROG:
        _PROG[key] = build_program(cfg)
    nc, nops = _PROG[key]
    C = host_consts()
    f32 = lambda a: np.ascontiguousarray(np.asarray(a, dtype=np.float32))
    xp = f32(inputs['x_prompt']); xs = f32(inputs['x_sample'])
    ck = f32(inputs['cache_k']); cv = f32(inputs['cache_v']); c = f32(inputs['c']); cctx = f32(inputs['c_ctx'])
    base = {k: f32(inputs[k]) for k in WEIGHT_SHAPES}
    for k, v in C.items(): base["c_" + k] = v
    in_maps = []
    for core in range(8):
        b = core // 4
        m = dict(base)
        m["xp"] = xp[2 * core:2 * core + 2].reshape(2 * LP, D)
        m["xs"] = xs[b]
        m["ck"] = ck[b].reshape(DEPTH, PAST, 256); m["cv"] = cv[b].reshape(DEPTH, PAST, 256)
        m["cvec"] = np.stack([cctx, c[b]], axis=0)
        q = core % 4; off = q * 1024
        ie = (off - 128 + np.arange(1280)).reshape(10, 128).T
        m["idxE"] = np.ascontiguousarray(np.clip(ie, 0, LS - 1).astype(np.int32))
        m["idxQ"] = (q * 128 + np.arange(128)).astype(np.int32).reshape(128, 1)
        m["maskLR"] = np.tile(np.array([[0.0 if q == 0 else 1.0, 0.0 if q == 3 else 1.0]], np.float32), (128, 1))
        m["ropec_own"] = np.ascontiguousarray(C["ropec"][:, off:off + 1024]); m["ropes_own"] = np.ascontiguousarray(C["ropes"][:, off:off + 1024])
        m["invcnt_own"] = np.ascontiguousarray(C["invcnt_s"][:, off:off + 1024])
        in_maps.append(m)
    res = run_bass_kernel_spmd(nc, in_maps, core_ids=list(range(8)))
    R = res.results
    y_prompt = np.concatenate([R[i]["yp"].reshape(2, LP, D) for i in range(8)], axis=0).astype(np.float32)
    y_sample = np.stack([np.concatenate([R[4 * b + q]["ys"] for q in range(4)], axis=0) for b in range(2)], axis=0).astype(np.float32)
    nk = np.concatenate([R[i]["nk"].reshape(2, DEPTH, LP, 4, 64) for i in range(8)], axis=0).astype(np.float32)
    nv = np.concatenate([R[i]["nv"].reshape(2, DEPTH, LP, 4, 64) for i in range(8)], axis=0).astype(np.float32)
    return (y_prompt, y_sample, nk, nv)
```

```python
import os, math, contextlib
import numpy as np
import ml_dtypes
import concourse.bass as bass
import concourse.mybir as mybir
from concourse.bass_utils import run_bass_kernel_spmd

F32 = mybir.dt.float32; BF16 = mybir.dt.bfloat16; I32 = mybir.dt.int32
AF = mybir.ActivationFunctionType; ALU = mybir.AluOpType
NPBF = ml_dtypes.bfloat16

D = 1024; NIN = 11264; DEPTH = 2; LP = 256; LS = 4096; PAST = 256
EPS = 1e-6
TWO_PI = 2.0 * math.pi

SAME_ENGINE_SYNC = os.environ.get('K_SES', '1') == '1'
class Reg:
    __slots__ = ("name", "w", "r", "dent")
    def __init__(self, name=""):
        self.name = name; self.w = None; self.r = []; self.dent = None

class Eng:
    def __init__(self, name, h):
        self.name = name; self.h = h; self.sem = None; self.cnt = 0; self.seen = {}

class FW:
    def __init__(self, nc, es, ndma=70):
        self.nc = nc; self.es = es
        self.pe = Eng("pe", nc.tensor); self.act = Eng("act", nc.scalar)
        self.dve = Eng("dve", nc.vector); self.pool = Eng("pool", nc.gpsimd); self.sp = Eng("sp", nc.sync)
        self.nsem = 0
        for e in (self.pe, self.act, self.dve, self.pool):
            e.sem = self.alloc_sem(e.name)
        self.dpool = [[self.alloc_sem("d%d" % i), 0] for i in range(ndma)]
        self.dfree = list(range(ndma)); self.dused = []
        self.nops = 0
    def alloc_sem(self, name):
        self.nsem += 1
        return self.es.enter_context(self.nc.semaphore("s%d_%s" % (self.nsem, name)))
    def _deps(self, eng, reads, writes):
        best = {}
        def add(t):
            k = id(t[0])
            if k not in best or best[k][1] < t[1]: best[k] = t
        for r in reads:
            if r.w is not None: add(r.w)
        for w in writes:
            if w.w is not None: add(w.w)
            for t in w.r: add(t)
        for k, (sem, val) in best.items():
            if sem is eng.sem and (eng is self.pe or not SAME_ENGINE_SYNC): continue
            if eng.seen.get(k, 0) >= val: continue
            eng.h.wait_ge(sem, val); eng.seen[k] = val
    def _mark(self, tok, reads, writes):
        for w in writes: w.w = tok; w.r = []
        for r in reads:
            if not any(r is w for w in writes): r.r.append(tok)
    def op(self, eng, fn, reads=(), writes=()):
        self._deps(eng, reads, writes)
        ins = fn(eng.h)
        ins.then_inc(eng.sem, 1); eng.cnt += 1; self.nops += 1
        tok = (eng.sem, eng.cnt)
        self._mark(tok, reads, writes)
        return tok
    def dma(self, q, out, in_, reads=(), writes=(), **kw):
        self._deps(q, reads, writes)
        is_store = 'DRam' in type(out.tensor).__name__
        prim = reads[0] if (is_store and len(reads)) else (writes[0] if len(writes) else reads[0])
        if prim.dent is None:
            if not self.dfree: raise RuntimeError("out of dma sems")
            prim.dent = self.dfree.pop(); self.dused.append(prim)
        ent = self.dpool[prim.dent]
        ins = q.h.dma_start(out=out, in_=in_, **kw)
        ins.then_inc(ent[0], 16); ent[1] += 16; self.nops += 1
        tok = (ent[0], ent[1])
        self._mark(tok, reads, writes)
        return tok
    def idma(self, q, out, in_, idx_ap, reads=(), writes=()):
        self._deps(q, reads, writes)
        prim = writes[0]
        if prim.dent is None:
            if not self.dfree: raise RuntimeError("out of dma sems")
            prim.dent = self.dfree.pop(); self.dused.append(prim)
        ent = self.dpool[prim.dent]
        ins = q.h.indirect_dma_start(out=out, out_offset=None, in_=in_, in_offset=bass.IndirectOffsetOnAxis(ap=idx_ap, axis=0))
        ins.then_inc(ent[0], 16); ent[1] += 16; self.nops += 1
        tok = (ent[0], ent[1])
        self._mark(tok, reads, writes)
        return tok
    def all_tokens(self):
        toks = []
        for e in (self.pe, self.act, self.dve, self.pool):
            if e.cnt > 0: toks.append((e.sem, e.cnt))
        for i, (s, c) in enumerate(self.dpool):
            if c > 0: toks.append((s, c))
        return toks
    def barrier(self):
        toks = self.all_tokens()
        for e in (self.pe, self.act, self.dve, self.pool, self.sp):
            for (sem, val) in toks:
                k = id(sem)
                if e.seen.get(k, 0) >= val: continue
                e.h.wait_ge(sem, val); e.seen[k] = val
        for r in self.dused: r.dent = None
        self.dused = []; self.dfree = list(range(len(self.dpool)))

def _bf(a): return np.ascontiguousarray(a.astype(np.float32)).astype(NPBF)

_CONST_CACHE = {}
def host_consts():
    if _CONST_CACHE: return _CONST_CACHE
    C = {}
    C["ident_f"] = np.eye(128, dtype=np.float32)
    C["ident_b"] = _bf(np.eye(128))
    bo = np.zeros((128, 128), np.float32); bo[:64, :64] = 1 / 64.; bo[64:, 64:] = 1 / 64.
    C["bones"] = _bf(bo)
    C["ones512"] = _bf(np.full((128, 128), 1 / 512.))
    R = np.zeros((128, 128), np.float32)
    for d in range(128):
        if (d % 32) < 16: R[d + 16, d] = -1.0
        else: R[d - 16, d] = 1.0
    C["rmat"] = _bf(R)
    t = np.arange(LS); row = (t // 64).astype(np.float32); col = (t % 64).astype(np.float32)
    cosT = np.zeros((128, LS), np.float32); sinT = np.zeros((128, LS), np.float32)
    inv = (10000.0 ** (-np.arange(16, dtype=np.float32) / 16)).astype(np.float32)
    for d in range(128):
        dd = d % 64; half = dd // 32; f = (dd % 32) % 16
        pos = row if half == 0 else col
        ang = (pos * inv[f]).astype(np.float32)
        cosT[d] = np.cos(ang); sinT[d] = np.sin(ang)
    C["ropec"] = cosT; C["ropes"] = sinT
    for nm, L in (("p", LP), ("s", LS)):
        tt = np.arange(L); ic = np.zeros((4, L), np.float32)
        for g, w in enumerate((2, 4, 8, 16)):
            lo = np.clip(tt - w // 2, 0, L); hi = np.clip(tt + w // 2, 0, L)
            ic[g] = 1.0 / (hi - lo)
        C["invcnt_" + nm] = ic
        tl = np.linspace(0.0, 1.0, L, dtype=np.float32)[:, None]
        bands = np.linspace(1e-4, 15, 16, dtype=np.float32)[None, :]
        w_ = ((2.0 * math.pi / L) * np.arange(L, dtype=np.float32))[:, None]
        z = np.concatenate([tl, np.cos(bands * w_), np.sin(bands * w_)], axis=-1).astype(np.float32)
        idx = np.concatenate([np.arange(L), (L - np.arange(L)) % L])
        C["zf_" + nm] = np.ascontiguousarray(z[idx].T)
        trow = tl[:, 0][idx].astype(np.float32).copy(); trow[L] = 1e4
        C["trow_" + nm] = trow[None, :]
    max_decay = math.log(1e-2) / 0.3; min_decay = math.log(1e-2) / 1.5
    deltas = np.abs(np.linspace(min_decay, max_decay, 512, dtype=np.float32))
    C["negdelta"] = np.ascontiguousarray((-deltas).reshape(4, 128).T)
    N = 512
    tt = np.arange(512)[:, None].astype(np.float64); kk = np.arange(512)[None, :].astype(np.float64)
    ang = 2 * np.pi * tt * kk / N
    C["fp_c"] = _bf(np.cos(ang).reshape(4, 128, 512).transpose(1, 0, 2))
    C["fp_sn"] = _bf((-np.sin(ang)).reshape(4, 128, 512).transpose(1, 0, 2))
    k2 = np.arange(512)[:, None].astype(np.float64); t2 = np.arange(256)[None, :].astype(np.float64)
    ang2 = 2 * np.pi * k2 * t2 / N
    C["fi_c"] = _bf((np.cos(ang2) / N).reshape(4, 128, 256).transpose(1, 0, 2))
    C["fi_sn"] = _bf((-np.sin(ang2) / N).reshape(4, 128, 256).transpose(1, 0, 2))
    N = 8192
    j = np.arange(64)[:, None].astype(np.float64); g = np.arange(64)[None, :].astype(np.float64)
    a1 = 2 * np.pi * j * g / 64
    C["wa"] = _bf(np.concatenate([np.concatenate([np.cos(a1), -np.sin(a1), -np.cos(a1)], axis=1), np.zeros((64, 192))], axis=0))
    a = np.arange(128)[:, None, None].astype(np.float64)
    gg = np.arange(64)[None, :, None].astype(np.float64); kb = np.arange(128)[None, None, :].astype(np.float64)
    th = 2 * np.pi * a * (gg + 64 * kb) / N
    C["tb_c"] = _bf(np.cos(th)); C["tb_s"] = _bf(np.sin(th)); C["tb_sn"] = _bf(-np.sin(th))
    kb2 = np.arange(128)[:, None].astype(np.float64); nb = np.arange(128)[None, :].astype(np.float64)
    al = 2 * np.pi * kb2 * nb / 128
    C["tbp_c"] = _bf(np.cos(al)); C["tbp_s"] = _bf(np.sin(al))
    g3 = np.arange(64)[:, None, None].astype(np.float64); nb3 = np.arange(128)[None, :, None].astype(np.float64)
    na3 = np.arange(32)[None, None, :].astype(np.float64)
    ph = 2 * np.pi * g3 * (128 * na3 + nb3) / N
    C["tc_st"] = _bf(np.concatenate([np.cos(ph) / N, -np.sin(ph) / N], axis=0))
    _CONST_CACHE.update(C)
    return C

WEIGHT_SHAPES = {
    'w_ada': (DEPTH, D, 3 * D), 'b_ada': (DEPTH, 3 * D), 'norm_g': (DEPTH, D), 'w_in': (DEPTH, D, NIN),
    'conv_dw_w': (DEPTH, 31, 512), 'conv_dw_b': (DEPTH, 512), 'conv_ln_g': (DEPTH, 512), 'conv_ln_b': (DEPTH, 512),
    'conv_pw': (DEPTH, 512, 512), 'hy_short_w': (DEPTH, 3, 1536), 'hy_short_b': (DEPTH, 1536),
    'hy_w1': (DEPTH, 33, 64), 'hy_b1': (DEPTH, 64), 'hy_freq': (DEPTH, 64), 'hy_w2': (DEPTH, 64, 64), 'hy_b2': (DEPTH, 64),
    'hy_w3': (DEPTH, 64, 2048), 'hy_b3': (DEPTH, 2048), 'hy_skip': (DEPTH, 2, 512),
    'pool_w': (DEPTH, 4, 128, 128), 'pool_scale': (DEPTH, 512), 'q_norm': (DEPTH, 64), 'k_norm': (DEPTH, 64),
    'wo_conv': (DEPTH, 512, D), 'wo_hyena': (DEPTH, 512, D), 'wo_pool': (DEPTH, 512, D), 'wo_attn': (DEPTH, D, D),
    'w_out': (DEPTH, D, D),
}

BLK = dict(a_val=0, a_glu=4, a_gate=8, b_proj=12, b_gate=24, c_in=28, c_gate=32, q=36, k=44, v=46, d_gate=48, gm=56)
def blk_func(j):
    if j < 4: return AF.Identity
    if j < 8: return AF.Sigmoid
    if j < 12: return AF.Silu
    if j < 24: return AF.Identity
    if j < 28: return AF.Silu
    if j < 32: return AF.Identity
    if j < 36: return AF.Silu
    if j < 48: return AF.Identity
    if j < 56: return AF.Silu
    return AF.Sigmoid

class Grp:
    pass

def build_program(cfg):
    nc = bass.Bass("TRN2", target_bir_lowering=False)
    C = host_consts()
    es = contextlib.ExitStack()
    with es:
        fw = FW(nc, es)
        pe, act, dve, pool, sp = fw.pe, fw.act, fw.dve, fw.pool, fw.sp

        _uid = [0]
        def uq(name):
            _uid[0] += 1
            return '%s_%d' % (name, _uid[0])
        def din(name, shape, dt=F32): return nc.dram_tensor(name, list(shape), dt, kind="ExternalInput").ap()
        def dout(name, shape, dt=F32): return nc.dram_tensor(name, list(shape), dt, kind="ExternalOutput").ap()
        def dscr(name, shape, dt): return nc.dram_tensor(name, list(shape), dt).ap()

        W = {k: din(k, s) for k, s in WEIGHT_SHAPES.items()}
        CT = {}
        for k, v in C.items():
            CT[k] = din("c_" + k, v.shape, BF16 if v.dtype == NPBF else F32)
        xp_in = din("xp", (2 * LP, D)); xs_in = din("xs", (LS, D))
        ck_in = din("ck", (DEPTH, PAST, 256)); cv_in = din("cv", (DEPTH, PAST, 256))
        cvec = din("cvec", (2, D))
        idxE_in = din("idxE", (128, 10), I32); idxQ_in = din("idxQ", (128, 1), I32); maskLR_in = din("maskLR", (128, 2))
        ropec_own = din("ropec_own", (128, 1024)); ropes_own = din("ropes_own", (128, 1024)); invcnt_own = din("invcnt_own", (4, 1024))
        yp_out = dout("yp", (2 * LP, D)); ys_out = dout("ys", (1024, D))
        nk_out = dout("nk", (2, DEPTH, LP, 256)); nv_out = dout("nv", (2, DEPTH, LP, 256))
        dbg = {}

        groups = []
        if cfg["prompt"]:
            g = Grp(); g.name = "p"; g.nseq = 2; g.L = LP; g.T = 512; g.x_in = xp_in; g.x_out = yp_out; g.crow = 0; g.latent = False
            groups.append(g)
        if cfg["sample"]:
            g = Grp(); g.name = "s"; g.nseq = 1; g.L = LS; g.T = LS; g.x_in = xs_in; g.x_out = ys_out; g.crow = 1; g.latent = True
            groups.append(g)
        for g in groups:
            T = g.T
            g.x_mid = dscr("xmid_" + g.name, (T, D), F32); g.r_xmid = Reg("xmid")
            g.U = dscr("U_" + g.name, (88, 128, T), BF16); g.rU = [Reg("U%d" % j) for j in range(88)]
            g.UKV = dscr("UKV_" + g.name, (4, 128, T), F32); g.rUKV = [Reg("UKV%d" % j) for j in range(4)]
            g.YIN = dscr("YIN_" + g.name, (20, 128, T), BF16); g.rYIN = [Reg("Y%d" % j) for j in range(20)]
            g.QR = dscr("QR_" + g.name, (8, 128, T), BF16); g.rQR = Reg("QR")
            g.ZB = dscr("ZB_" + g.name, (128, 2 * g.L if g.nseq == 1 else g.T), BF16); g.rZB = Reg("ZB")
            if g.latent:
                g.HS = dscr("HS_" + g.name, (2, 4, 2, 2, 128, 64 * 64), BF16)
            else:
                g.HS = dscr("HS_" + g.name, (2, 4, 2, 128, 4 * 128), BF16)
            g.rHS = Reg("HS")
            if g.latent:
                g.UE = dscr("UE_" + g.name, (88, 128, 1280), BF16); g.rUE = [Reg("UE%d" % j) for j in range(88)]
                g.KR = dscr("KR_" + g.name, (2, 128, g.T), BF16); g.rKR = Reg("KR")
                g.VT = dscr("VT_" + g.name, (128, g.T // 128, 256), BF16); g.rVT = Reg("VT")
                g.YB = [dscr("YB%d_" % cb_ + g.name, (4 * 128, 1024), BF16) for cb_ in range(4)]; g.rYB = Reg("YB")
            g.r_xout = Reg("xout")
        r_nk = Reg("nk"); r_nv = Reg("nv")

        def sbp(name, shape, dt): return es.enter_context(nc.sbuf_tensor(name, list(shape), dt))
        ident_f = sbp("ident_f", [128, 128], F32); ident_b = sbp("ident_b", [128, 128], BF16)
        bones = sbp("bones", [128, 128], BF16); ones512 = sbp("ones512", [128, 128], BF16); rmat = sbp("rmat", [128, 128], BF16)
        r_const = Reg("const")
        for tile_, nm in ((ident_f, "ident_f"), (ident_b, "ident_b"), (bones, "bones"), (ones512, "ones512"), (rmat, "rmat")):
            fw.dma(sp, tile_[:], CT[nm][:, :], writes=[Reg("c" + nm)])
        idxE = sbp("idxE_sb", [128, 10], I32); idxQ = sbp("idxQ_sb", [128, 1], I32); maskLR = sbp("maskLR_sb", [128, 2], F32)
        fw.dma(sp, idxE[:], idxE_in[:, :], writes=[Reg("cidxE")]); fw.dma(sp, idxQ[:], idxQ_in[:, :], writes=[Reg("cidxQ")])
        fw.dma(sp, maskLR[:], maskLR_in[:, :], writes=[Reg("cmask")])
        fw.barrier()
        class PSX_: pass
        PSX = PSX_()
        psi = [0]
        def psum_std(ph, nb=7, with_pst=True):
            PSX.banks = [ph.enter_context(nc.psum_tensor(uq("ps"), [128, 512], F32)) for i in range(nb)]
            PSX.regs = [Reg("ps%d" % i) for i in range(nb)]
            if with_pst:
                PSX.pst = ph.enter_context(nc.psum_tensor(uq("pst"), [128, 1024], BF16)); PSX.rpst = Reg("pst")
        def PS():
            nb = len(PSX.banks)
            i = psi[0] % nb; psi[0] += 1
            return PSX.banks[i], PSX.regs[i]

        def ACT(out, in_, func, reads, writes, **kw):
            return fw.op(act, lambda e: e.activation(out=out, in_=in_, func=func, **kw), reads, writes)
        def MM(out, lhsT, rhs, start, stop, reads, writes):
            return fw.op(pe, lambda e: e.matmul(out, lhsT=lhsT, rhs=rhs, start=start, stop=stop), reads, writes)
        def TR(out, in_, ident, reads, writes):
            return fw.op(pe, lambda e: e.transpose(out, in_, ident), reads, writes)
        def TT(out, a, b, op, reads, writes, eng=None):
            return fw.op(eng or dve, lambda e: e.tensor_tensor(out=out, in0=a, in1=b, op=op), reads, writes)
        def TS(out, a, s1, s2, op0, op1, reads, writes, eng=None):
            if s2 is None:
                return fw.op(eng or dve, lambda e: e.tensor_scalar(out=out, in0=a, scalar1=s1, scalar2=None, op0=op0), reads, writes)
            return fw.op(eng or dve, lambda e: e.tensor_scalar(out=out, in0=a, scalar1=s1, scalar2=s2, op0=op0, op1=op1), reads, writes)
        def STT(out, a, s, b, op0, op1, reads, writes, eng=None):
            return fw.op(eng or dve, lambda e: e.scalar_tensor_tensor(out=out, in0=a, scalar=s, in1=b, op0=op0, op1=op1), reads, writes)
        def CP(out, in_, reads, writes, eng=None):
            return fw.op(eng or dve, lambda e: e.tensor_copy(out=out, in_=in_), reads, writes)
        def MS(out, val, writes, eng=None):
            return fw.op(eng or dve, lambda e: e.memset(out, val), (), writes)
        def RCP(out, in_, reads, writes):
            return fw.op(dve, lambda e: e.reciprocal(out=out, in_=in_), reads, writes)
        def colvec(dst, src_vec_ap, nb, writes):
            return fw.dma(sp, dst, src_vec_ap.rearrange("(b p) -> p b", p=128), writes=writes, allow_slow_non_contiguous=True)

        def dense_in(g, l, x_src, r_xsrc, wbs=tuple(range(22)), ext=False):
            T = g.T; TC = min(T, 2048); nchunk = T // TC
            if ext: TC = 1280; nchunk = 1
            with contextlib.ExitStack() as ph:
                def sb(name, shape, dt): return ph.enter_context(nc.sbuf_tensor(uq(name), list(shape), dt))
                psum_std(ph)
                modb = sb("modb", [128, 3 * D], F32); r_mod = Reg("mod")
                Gt = sb("Gt", [128, D], F32); r_G = Reg("G")
                ngb = sb("ngb", [128, D], F32); r_ng = Reg("ng")
                badab = sb("badab", [128, 3 * D], F32); r_bada = Reg("bada")
                ccol = sb("ccol", [128, 8], F32); r_cc = Reg("cc")
                crep = sb("crep", [128, 8, 128], BF16); r_crep = Reg("crep")
                wsl = [sb("wsl%d" % i, [128, 8, 512], BF16) for i in range(3)]; r_wsl = [Reg("wsl%d" % i) for i in range(3)]
                hT = sb("hT", [128, 8, TC], BF16); r_hT = [Reg("hT%d" % i) for i in range(TC // 128)]
                xt = [sb("xt%d" % i, [128, D], F32) for i in range(2)]; r_xt = [Reg("xt%d" % i) for i in range(2)]
                sqj = sb("sqj", [128, D], F32); r_sqj = Reg("sqj")
                ss = [sb("ss%d" % i, [128, 1], F32) for i in range(2)]; r_ss = [Reg("ss%d" % i) for i in range(2)]
                hb = [sb("hb%d" % i, [128, D], BF16) for i in range(2)]; r_hb = [Reg("hb%d" % i) for i in range(2)]
                stg = [sb("stg%d" % i, [128, TC], BF16) for i in range(3)]; r_stg = [Reg("stg%d" % i) for i in range(3)]
                stgf = [sb("stgf%d" % i, [128, TC], F32) for i in range(2)]; r_stgf = [Reg("stgf%d" % i) for i in range(2)]
                fuse_qk = g.latent
                if fuse_qk:
                    gqd = sb("gqd", [128, 1], F32); gkd = sb("gkd", [128, 1], F32); r_gvd = Reg("gqkd")
                    for h2 in range(2):
                        fw.dma(sp, gqd[h2 * 64:(h2 + 1) * 64, :], W['q_norm'][l].rearrange("(d o) -> d o", o=1), writes=[r_gvd])
                        fw.dma(sp, gkd[h2 * 64:(h2 + 1) * 64, :], W['k_norm'][l].rearrange("(d o) -> d o", o=1), writes=[r_gvd])
                    TS(gqd[:], gqd[:], 0.125, None, ALU.mult, None, [r_gvd], [r_gvd])
                    CW = 512
                    dsq = [sb("dsq%d" % i, [128, CW], BF16) for i in range(4)]; r_dsq = [Reg("dsq%d" % i) for i in range(4)]
                    drs = [sb("drs%d" % i, [128, CW], F32) for i in range(4)]; r_drs = [Reg("drs%d" % i) for i in range(4)]
                    dkf = [sb("dkf%d" % i, [128, CW], F32) for i in range(4)]; r_dkf = [Reg("dkf%d" % i) for i in range(4)]
                    dkb = [sb("dkb%d" % i, [128, CW], BF16) for i in range(4)]; r_dkb = [Reg("dkb%d" % i) for i in range(4)]
                    dt1 = [sb("dt1%d" % i, [128, CW], F32) for i in range(4)]; r_dt1 = [Reg("dt1%d" % i) for i in range(4)]
                    dqo = [sb("dqo%d" % i, [128, CW], BF16) for i in range(4)]; r_dqo = [Reg("dqo%d" % i) for i in range(4)]
                    NRT = 1024 if ext else TC
                    cosd = sb("cosd", [128, NRT], F32); sind = sb("sind", [128, NRT], F32); r_csd = Reg("csd")
                    dcnt = [0]
                    pending = []
                    def advance():
                        for gen in list(pending):
                            try: next(gen)
                            except StopIteration: pending.remove(gen)
                    def qk_chain(*a):
                        gen = qk_chain_gen(*a); next(gen); pending.append(gen)
                    vts = [sb("vts%d" % i, [128, 4, 128], BF16) for i in range(2)]; r_vts = [Reg("vts%d" % i) for i in range(2)]
                    vcnt = [0]
                    def v_chain_gen(src, rsrc, tok0, vb):
                        yield
                        k = vcnt[0] % 2; vcnt[0] += 1
                        for sub in range(4):
                            TR(PSX.pst[:, sub * 128:(sub + 1) * 128], src[:, sub * 128:(sub + 1) * 128], ident_b[:], rsrc, [PSX.rpst])
                        CP(vts[k][:], PSX.pst[:, 0:512].rearrange("p (s f) -> p s f", s=4), [PSX.rpst], [r_vts[k]])
                        fw.dma(sp, g.VT[:, tok0 // 128:tok0 // 128 + 4, vb * 128:(vb + 1) * 128], vts[k][:], reads=[r_vts[k]], writes=[g.rVT])
                    def v_chain(*a):
                        gen = v_chain_gen(*a); next(gen); pending.append(gen)
                    def qk_chain_gen(src, rsrc, gvec, tcol, n, out_dram, r_out):
                        k = dcnt[0] % 4; dcnt[0] += 1
                        TT(dsq[k][:, 0:n], src, src, ALU.mult, rsrc, [r_dsq[k]])
                        STT(dkf[k][:, 0:n], src, gvec, src, ALU.mult, ALU.bypass, rsrc + [r_gvd], [r_dkf[k]])
                        yield
                        ps, rps = PS()
                        MM(ps[:, 0:n], bones[:], dsq[k][:, 0:n], True, True, [r_dsq[k]], [rps])
                        ACT(drs[k][:, 0:n], ps[:, 0:n], AF.Ln, [rps], [r_drs[k]], bias=EPS)
                        ACT(drs[k][:, 0:n], drs[k][:, 0:n], AF.Exp, [r_drs[k]], [r_drs[k]], scale=-0.5)
                        TT(dkf[k][:, 0:n], dkf[k][:, 0:n], drs[k][:, 0:n], ALU.mult, [r_dkf[k], r_drs[k]], [r_dkf[k]])
                        CP(dkb[k][:, 0:n], dkf[k][:, 0:n], [r_dkf[k]], [r_dkb[k]])
                        yield
                        ps, rps = PS()
                        MM(ps[:, 0:n], rmat[:], dkb[k][:, 0:n], True, True, [r_dkb[k]], [rps])
                        TT(dt1[k][:, 0:n], ps[:, 0:n], sind[:, tcol:tcol + n], ALU.mult, [rps, r_csd], [r_dt1[k]])
                        TT(dkf[k][:, 0:n], dkf[k][:, 0:n], cosd[:, tcol:tcol + n], ALU.mult, [r_dkf[k], r_csd], [r_dkf[k]])
                        TT(dqo[k][:, 0:n], dkf[k][:, 0:n], dt1[k][:, 0:n], ALU.add, [r_dkf[k], r_dt1[k]], [r_dqo[k]])
                        fw.dma(sp, out_dram, dqo[k][:, 0:n], reads=[r_dqo[k]], writes=[r_out])
                fw.dma(sp, ccol[:], cvec[g.crow, :].rearrange("(b p) -> p b", p=128), writes=[r_cc], allow_slow_non_contiguous=True)
                fw.dma(sp, ngb[:], W['norm_g'][l:l + 1, :].partition_broadcast(128), writes=[r_ng])
                fw.dma(sp, badab[:], W['b_ada'][l:l + 1, :].partition_broadcast(128), writes=[r_bada])
                ACT(ccol[:], ccol[:], AF.Silu, [r_cc], [r_cc])
                CP(crep[:], ccol[:].unsqueeze(2).broadcast_to([128, 8, 128]), [r_cc], [r_crep])
                wi = 0
                wada = W['w_ada'][l].rearrange("(kt p) n -> p kt n", p=128)
                for cb in range(6):
                    s_ = wi % 3; wi += 1
                    fw.dma(pool, wsl[s_][:], wada[:, :, cb * 512:(cb + 1) * 512], writes=[r_wsl[s_]])
                    ps, rps = PS()
                    for kt in range(8):
                        MM(ps[:], crep[:, kt, :], wsl[s_][:, kt, :], kt == 0, kt == 7, [r_crep, r_wsl[s_]], [rps])
                    TT(modb[:, cb * 512:(cb + 1) * 512], ps[:], badab[:, cb * 512:(cb + 1) * 512], ALU.add, [rps, r_bada], [r_mod])
                STT(Gt[:], modb[:, D:2 * D], 1.0, ngb[:], ALU.add, ALU.mult, [r_mod, r_ng], [r_G])
                win = W['w_in'][l].rearrange("(kt p) n -> p kt n", p=128)
                si = 0; sfi = 0
                for ci in range(nchunk):
                    if fuse_qk:
                        if ext:
                            fw.dma(sp, cosd[:], ropec_own[:, :], writes=[r_csd]); fw.dma(sp, sind[:], ropes_own[:, :], writes=[r_csd])
                        else:
                            fw.dma(sp, cosd[:], CT["ropec"][:, ci * TC:(ci + 1) * TC], writes=[r_csd])
                            fw.dma(sp, sind[:], CT["ropes"][:, ci * TC:(ci + 1) * TC], writes=[r_csd])
                    for tl in range(TC // 128):
                        tok0 = ci * TC + tl * 128; b_ = tl % 2
                        if ext:
                            fw.idma(pool, xt[b_][:], x_src[:, :], idxE[:, tl:tl + 1], reads=[r_xsrc], writes=[r_xt[b_]])
                        else:
                            fw.dma(sp, xt[b_][:], x_src[tok0:tok0 + 128, :], reads=[r_xsrc], writes=[r_xt[b_]])
                        ACT(sqj[:], xt[b_][:], AF.Square, [r_xt[b_]], [r_sqj, r_ss[b_]], accum_out=ss[b_][:])
                        ACT(ss[b_][:], ss[b_][:], AF.Sqrt, [r_ss[b_]], [r_ss[b_]], scale=1.0 / D, bias=EPS)
                        RCP(ss[b_][:], ss[b_][:], [r_ss[b_]], [r_ss[b_]])
                        STT(xt[b_][:], xt[b_][:], ss[b_][:, 0:1], Gt[:], ALU.mult, ALU.mult, [r_xt[b_], r_ss[b_], r_G], [r_xt[b_]])
                        TT(hb[b_][:], xt[b_][:], modb[:, 0:D], ALU.add, [r_xt[b_], r_mod], [r_hb[b_]])
                        for kt in range(8):
                            TR(PSX.pst[:, kt * 128:(kt + 1) * 128], hb[b_][:, kt * 128:(kt + 1) * 128], ident_b[:], [r_hb[b_]], [PSX.rpst])
                        CP(hT[:, :, tl * 128:(tl + 1) * 128], PSX.pst[:].rearrange("p (k t) -> p k t", k=8), [PSX.rpst], [r_hT[tl]],
                           eng=act if tl % 2 else dve) if False else ACT(hT[:, :, tl * 128:(tl + 1) * 128], PSX.pst[:].rearrange("p (k t) -> p k t", k=8), AF.Copy, [PSX.rpst], [r_hT[tl]])
                    wb_order = list(wbs)
                    if fuse_qk:
                        hot = [w for w in (9, 10, 11) if w in wb_order]; cold = [w for w in wb_order if w not in hot]
                        wb_order = []
                        step = max(1, len(cold) // (len(hot) + 1)) if hot else 1
                        ci_ = 0
                        for hw in hot:
                            wb_order.append(hw); wb_order += cold[ci_:ci_ + step]; ci_ += step
                        wb_order += cold[ci_:]
                    for wb in wb_order:
                        s_ = wi % 3; wi += 1
                        fw.dma(pool, wsl[s_][:], win[:, :, wb * 512:(wb + 1) * 512], writes=[r_wsl[s_]])
                        for sub in range(4):
                            j = wb * 4 + sub
                            iskv = 44 <= j < 48
                            if iskv and not (fuse_qk and j >= 46):
                                so = stgf[sfi % 2]; rso = r_stgf[sfi % 2]; sfi += 1
                            else:
                                so = stg[si % 3]; rso = r_stg[si % 3]; si += 1
                            for c0 in range(0, TC, 512):
                                cw = min(512, TC - c0)
                                ps, rps = PS()
                                for kt in range(8):
                                    MM(ps[:, 0:cw], wsl[s_][:, kt, sub * 128:(sub + 1) * 128], hT[:, kt, c0:c0 + cw],
                                       kt == 0, kt == 7, [r_wsl[s_]] + r_hT[c0 // 128:(c0 + cw) // 128], [rps])
                                ACT(so[:, c0:c0 + cw], ps[:, 0:cw], blk_func(j), [rps], [rso])
                                if fuse_qk and c0 == 512: advance()
                            if fuse_qk: advance()
                            if fuse_qk and 36 <= j < 46:
                                isq = j < 44
                                if ext:
                                    for t_ in range(0, 1024, 512):
                                        qk_chain(so[:, 128 + t_:128 + t_ + 512], [rso], gqd[:, 0:1], t_, 512, g.QR[j - 36][:, t_:t_ + 512], g.rQR)
                                else:
                                    for t_ in range(0, TC, 512):
                                        if isq:
                                            qk_chain(so[:, t_:t_ + 512], [rso], gqd[:, 0:1], t_, 512, g.QR[j - 36][:, ci * TC + t_:ci * TC + t_ + 512], g.rQR)
                                        else:
                                            qk_chain(so[:, t_:t_ + 512], [rso], gkd[:, 0:1], t_, 512, g.KR[j - 44][:, ci * TC + t_:ci * TC + t_ + 512], g.rKR)
                            if fuse_qk and j in (46, 47) and not ext:
                                for t_ in range(0, TC, 512):
                                    v_chain(so[:, t_:t_ + 512], [rso], ci * TC + t_, j - 46)
                            if fuse_qk and 36 <= j < 46:
                                pass
                            elif ext:
                                fw.dma(sp, g.UE[j][:, :], so[:, 0:TC], reads=[rso], writes=[g.rUE[j]])
                            elif iskv and fuse_qk and j >= 46:
                                pass
                            elif iskv:
                                fw.dma(sp, g.UKV[j - 44][:, ci * TC:(ci + 1) * TC], so[:], reads=[rso], writes=[g.rUKV[j - 44]])
                            else:
                                fw.dma(sp, g.U[j][:, ci * TC:(ci + 1) * TC], so[:], reads=[rso], writes=[g.rU[j]])
                if fuse_qk:
                    while pending: advance()
                fw.barrier()
            return

        def seq_tiles(g, n):
            out = []
            for s in range(g.nseq):
                for t0 in range(0, g.L, n):
                    out.append((s, t0, min(n, g.L - t0)))
            return out

        def own_tiles(own):
            return [(0, 0, 512), (0, 512, 512)]
        def conformer(g, l, own=False):
            NT = 512 if g.L >= 512 else g.L
            Us = g.UE if own else g.U; rUs = g.rUE if own else g.rU
            with contextlib.ExitStack() as ph:
                def sb(name, shape, dt): return ph.enter_context(nc.sbuf_tensor(uq(name), list(shape), dt))
                psum_std(ph)
                dwc = sb("dwc", [128, 4, 31], F32); r_dwc = Reg("dwc")
                dgm = sb("dgm", [128, 4, 31, 128], BF16); r_dgm = Reg("dgm")
                dwb = sb("dwb", [128, 4], F32); lng = sb("lng", [128, 4], F32); lnb = sb("lnb", [128, 4], F32); r_vec = Reg("cvec")
                pw = sb("pw", [128, 4, 512], BF16); r_pw = Reg("pw")
                vl = [sb("vl%d" % i, [128, 4, NT + 30], BF16) for i in range(2)]; r_vl = [Reg("vl%d" % i) for i in range(2)]
                gl = [sb("gl%d" % i, [128, 4, NT + 30], BF16) for i in range(2)]; r_gl = [Reg("gl%d" % i) for i in range(2)]
                sag = [sb("sag%d" % i, [128, 4, NT], BF16) for i in range(2)]; r_sag = [Reg("sag%d" % i) for i in range(2)]
                a2b_ = [sb("a2b%d" % i, [128, 4, NT], BF16) for i in range(2)]; r_a2b_ = [Reg("a2b%d" % i) for i in range(2)]
                a2s_ = [sb("a2s%d" % i, [128, 4, NT], BF16) for i in range(2)]; r_a2s_ = [Reg("a2s%d" % i) for i in range(2)]
                mean_ = [sb("mean%d" % i, [128, NT], F32) for i in range(2)]; r_mean_ = [Reg("mean%d" % i) for i in range(2)]
                var_ = [sb("var%d" % i, [128, NT], F32) for i in range(2)]; r_var_ = [Reg("var%d" % i) for i in range(2)]
                nmr_ = [sb("nmr%d" % i, [128, NT], F32) for i in range(2)]; r_nmr_ = [Reg("nmr%d" % i) for i in range(2)]
                tmp_ = [sb("tmpc%d" % i, [128, NT], F32) for i in range(2)]; r_tmp_ = [Reg("tmpc%d" % i) for i in range(2)]
                lno_ = [sb("lno%d" % i, [128, 4, NT], BF16) for i in range(2)]; r_lno_ = [Reg("lno%d" % i) for i in range(2)]
                yo = [sb("yoc%d" % i, [128, 4, NT], BF16) for i in range(2)]; r_yo = [Reg("yoc%d" % i) for i in range(2)]
                for b in range(4):
                    fw.dma(sp, dwc[:, b, :], W['conv_dw_w'][l][:, b * 128:(b + 1) * 128].rearrange("k c -> c k"), writes=[r_dwc],
                           allow_slow_non_contiguous=True)
                colvec(dwb[:], W['conv_dw_b'][l], 4, [r_vec]); colvec(lng[:], W['conv_ln_g'][l], 4, [r_vec]); colvec(lnb[:], W['conv_ln_b'][l], 4, [r_vec])
                fw.dma(pool, pw[:], W['conv_pw'][l].rearrange("(kt p) n -> p kt n", p=128), writes=[r_pw])
                for b in range(4):
                    for k in range(31):
                        TS(dgm[:, b, k, :], ident_b[:], dwc[:, b, k:k + 1], None, ALU.mult, None, [r_dwc], [r_dgm])
                for it, (s, t0, n) in enumerate(own_tiles(own) if own else seq_tiles(g, NT)):
                    b_ = it % 2; g0 = s * g.L + t0
                    a2b = a2b_[b_]; r_a2b = r_a2b_[b_]; a2s = a2s_[b_]; r_a2s = r_a2s_[b_]; mean = mean_[b_]; r_mean = r_mean_[b_]
                    var = var_[b_]; r_var = r_var_[b_]; nmr = nmr_[b_]; r_nmr = r_nmr_[b_]; tmp = tmp_[b_]; r_tmp = r_tmp_[b_]; lno = lno_[b_]; r_lno = r_lno_[b_]
                    vmin, vmax, cb0 = (-128, 1152, 128) if own else (0, g.L, s * g.L)
                    lo = max(t0 - 15, vmin); hi = min(t0 + n + 15, vmax)
                    off = lo - (t0 - 15); w = hi - lo
                    if off > 0 or w < n + 30:
                        MS(vl[b_][:], 0.0, [r_vl[b_]]); MS(gl[b_][:], 0.0, [r_gl[b_]])
                    fw.dma(sp, vl[b_][:, :, off:off + w], Us[0:4, :, cb0 + lo:cb0 + hi].rearrange("j p w -> p j w"), reads=rUs[0:4], writes=[r_vl[b_]])
                    fw.dma(sp, gl[b_][:, :, off:off + w], Us[4:8, :, cb0 + lo:cb0 + hi].rearrange("j p w -> p j w"), reads=rUs[4:8], writes=[r_gl[b_]])
                    fw.dma(sp, sag[b_][:, :, 0:n], Us[8:12, :, cb0 + t0:cb0 + t0 + n].rearrange("j p w -> p j w"), reads=rUs[8:12], writes=[r_sag[b_]])
                    TT(vl[b_][:], vl[b_][:], gl[b_][:], ALU.mult, [r_vl[b_], r_gl[b_]], [r_vl[b_]])
                    if own:
                        g0 = t0
                        if t0 == 0:
                            TS(vl[b_][:, :, 0:15], vl[b_][:, :, 0:15], maskLR[:, 0:1], None, ALU.mult, None, [r_vl[b_]], [r_vl[b_]])
                        if t0 + n == 1024:
                            TS(vl[b_][:, :, n + 15:n + 30], vl[b_][:, :, n + 15:n + 30], maskLR[:, 1:2], None, ALU.mult, None, [r_vl[b_]], [r_vl[b_]])
                    pss = []
                    for b in range(4):
                        ps, rps = PS(); pss.append((ps, rps))
                        for k in range(31):
                            MM(ps[:, 0:n], dgm[:, b, k, :], vl[b_][:, b, k:k + n], k == 0, k == 30, [r_dgm, r_vl[b_]], [rps])
                        ACT(a2b[:, b, 0:n], ps[:, 0:n], AF.Identity, [rps], [r_a2b], bias=dwb[:, b:b + 1])
                        ACT(a2s[:, b, 0:n], ps[:, 0:n], AF.Square, [rps], [r_a2s], bias=dwb[:, b:b + 1])
                    pm, rpm = PS(); pq, rpq = PS()
                    for b in range(4):
                        MM(pm[:, 0:n], ones512[:], a2b[:, b, 0:n], b == 0, b == 3, [r_a2b], [rpm])
                    for b in range(4):
                        MM(pq[:, 0:n], ones512[:], a2s[:, b, 0:n], b == 0, b == 3, [r_a2s], [rpq])
                    ACT(mean[:, 0:n], pm[:, 0:n], AF.Copy, [rpm], [r_mean])
                    TT(var[:, 0:n], mean[:, 0:n], mean[:, 0:n], ALU.mult, [r_mean], [r_var])
                    TT(var[:, 0:n], pq[:, 0:n], var[:, 0:n], ALU.subtract, [rpq, r_var], [r_var])
                    TS(var[:, 0:n], var[:, 0:n], 0.0, None, ALU.max, None, [r_var], [r_var])
                    ACT(var[:, 0:n], var[:, 0:n], AF.Sqrt, [r_var], [r_var], bias=EPS)
                    RCP(var[:, 0:n], var[:, 0:n], [r_var], [r_var])
                    STT(nmr[:, 0:n], mean[:, 0:n], -1.0, var[:, 0:n], ALU.mult, ALU.mult, [r_mean, r_var], [r_nmr])
                    for b in range(4):
                        TT(tmp[:, 0:n], a2b[:, b, 0:n], var[:, 0:n], ALU.mult, [r_a2b, r_var], [r_tmp])
                        TT(tmp[:, 0:n], tmp[:, 0:n], nmr[:, 0:n], ALU.add, [r_tmp, r_nmr], [r_tmp])
                        ACT(lno[:, b, 0:n], tmp[:, 0:n], AF.Silu, [r_tmp, r_vec], [r_lno], scale=lng[:, b:b + 1], bias=lnb[:, b:b + 1])
                    for ob in range(4):
                        ps, rps = PS()
                        for kb in range(4):
                            MM(ps[:, 0:n], pw[:, kb, ob * 128:(ob + 1) * 128], lno[:, kb, 0:n], kb == 0, kb == 3, [r_pw, r_lno], [rps])
                        TT(yo[b_][:, ob, 0:n], ps[:, 0:n], sag[b_][:, ob, 0:n], ALU.mult, [rps, r_sag[b_]], [r_yo[b_]])
                    fw.dma(sp, g.YIN[0:4, :, g0:g0 + n].rearrange("j p w -> p j w"), yo[b_][:, :, 0:n], reads=[r_yo[b_]], writes=g.rYIN[0:4])
                fw.barrier()

        def pooling(g, l, own=False):
            NT = 512 if g.L >= 512 else g.L
            Us = g.UE if own else g.U; rUs = g.rUE if own else g.rU
            with contextlib.ExitStack() as ph:
                def sb(name, shape, dt): return ph.enter_context(nc.sbuf_tensor(uq(name), list(shape), dt))
                psum_std(ph)
                pwt = sb("pwt", [128, 4, 128], BF16); r_pwt = Reg("pwt")
                psc = sb("psc", [128, 4], F32); r_psc = Reg("psc")
                xin = [sb("xin%d" % i, [128, 4, NT + 32], BF16) for i in range(2)]; r_xin = [Reg("xin%d" % i) for i in range(2)]
                scg = [sb("scg%d" % i, [128, 4, NT], BF16) for i in range(2)]; r_scg = [Reg("scg%d" % i) for i in range(2)]
                icn = [sb("icn%d" % i, [128, 4, NT], F32) for i in range(2)]; r_icn = [Reg("icn%d" % i) for i in range(2)]
                was = [sb("wa_%d" % i, [128, NT + 32], F32) for i in range(4)]; wbs_ = [sb("wb_%d" % i, [128, NT + 32], F32) for i in range(4)]
                r_was = [Reg("wa%d" % i) for i in range(4)]; r_wbs = [Reg("wb%d" % i) for i in range(4)]
                plds = [sb("pld%d" % i, [128, NT], BF16) for i in range(4)]; r_plds = [Reg("pld%d" % i) for i in range(4)]
                yo = [sb("yop%d" % i, [128, 4, NT], BF16) for i in range(2)]; r_yo = [Reg("yop%d" % i) for i in range(2)]
                fw.dma(pool, pwt[:], W['pool_w'][l].rearrange("g c d -> c g d"), writes=[r_pwt])
                colvec(psc[:], W['pool_scale'][l], 4, [r_psc])
                ict = invcnt_own if own else CT["invcnt_" + g.name]
                for it, (s, t0, n) in enumerate(own_tiles(own) if own else seq_tiles(g, NT)):
                    b_ = it % 2; g0 = s * g.L + t0
                    vmin, vmax, cb0 = (-128, 1152, 128) if own else (0, g.L, s * g.L)
                    lo = max(t0 - 16, vmin); hi = min(t0 + n + 16, vmax); off = lo - (t0 - 16); w = hi - lo
                    if off > 0 or w < n + 32:
                        MS(xin[b_][:], 0.0, [r_xin[b_]])
                    fw.dma(sp, xin[b_][:, :, off:off + w], Us[28:32, :, cb0 + lo:cb0 + hi].rearrange("j p w -> p j w"), reads=rUs[28:32], writes=[r_xin[b_]])
                    fw.dma(sp, scg[b_][:, :, 0:n], Us[32:36, :, cb0 + t0:cb0 + t0 + n].rearrange("j p w -> p j w"), reads=rUs[32:36], writes=[r_scg[b_]])
                    if own:
                        g0 = t0
                        if t0 == 0:
                            TS(xin[b_][:, :, 0:16], xin[b_][:, :, 0:16], maskLR[:, 0:1], None, ALU.mult, None, [r_xin[b_]], [r_xin[b_]])
                        if t0 + n == 1024:
                            TS(xin[b_][:, :, n + 16:n + 32], xin[b_][:, :, n + 16:n + 32], maskLR[:, 1:2], None, ALU.mult, None, [r_xin[b_]], [r_xin[b_]])
                    for gi in range(4):
                        fw.dma(sp, icn[b_][:, gi, 0:n], ict[gi:gi + 1, t0:t0 + n].partition_broadcast(128), writes=[r_icn[b_]])
                    m = n + 32
                    for gi in range(4):
                        wa = was[gi]; wb_ = wbs_[gi]; r_wa = r_was[gi]; r_wb = r_wbs[gi]; pld = plds[gi]; r_pld = r_plds[gi]
                        x_ = xin[b_][:, gi, :]
                        TT(wa[:, 1:m], x_[:, 0:m - 1], x_[:, 1:m], ALU.add, [r_xin[b_]], [r_wa])
                        cur, rcur, oth, roth = wa, r_wa, wb_, r_wb
                        lo_i = 1; hi_i = m
                        sh = 1
                        for lev in range(gi):
                            nlo = lo_i + sh; nhi = hi_i - sh
                            TT(oth[:, nlo:nhi], cur[:, nlo - sh:nhi - sh], cur[:, nlo + sh:nhi + sh], ALU.add, [rcur], [roth])
                            cur, rcur, oth, roth = oth, roth, cur, rcur
                            lo_i, hi_i = nlo, nhi; sh *= 2
                        TT(oth[:, 16:16 + n], cur[:, 16:16 + n], icn[b_][:, gi, 0:n], ALU.mult, [rcur, r_icn[b_]], [roth])
                        TT(pld[:, 0:n], oth[:, 16:16 + n], x_[:, 16:16 + n], ALU.subtract, [roth, r_xin[b_]], [r_pld])
                        ps, rps = PS()
                        MM(ps[:, 0:n], pwt[:, gi, :], pld[:, 0:n], True, True, [r_pwt, r_pld], [rps])
                        STT(yo[b_][:, gi, 0:n], ps[:, 0:n], psc[:, gi:gi + 1], scg[b_][:, gi, 0:n], ALU.mult, ALU.mult, [rps, r_psc, r_scg[b_]], [r_yo[b_]])
                    fw.dma(sp, g.YIN[8:12, :, g0:g0 + n].rearrange("j p w -> p j w"), yo[b_][:, :, 0:n], reads=[r_yo[b_]], writes=g.rYIN[8:12])
                fw.barrier()

        def attention(g, l, own=False):
            L = g.L; NT = 512 if L >= 512 else L
            Us = g.UE if own else g.U; rUs = g.rUE if own else g.rU
            nkeys = L + (PAST if g.latent else 0); nst = nkeys // 128
            with contextlib.ExitStack() as ph:
                def sb(name, shape, dt): return ph.enter_context(nc.sbuf_tensor(uq(name), list(shape), dt))
                sc = [ph.enter_context(nc.psum_tensor(uq("sc"), [128, 1024], F32)) for i in range(3)]; r_sc = [Reg("sc%d" % i) for i in range(3)]
                PSX.banks = [sc[i][:, k * 512:(k + 1) * 512] for i in range(3) for k in range(2)]
                PSX.regs = [r_sc[i] for i in range(3) for k in range(2)]
                K2 = sb("K2", [128, 4, g.nseq, nkeys], BF16); r_K2 = Reg("K2")
                Ve = sb("Ve", [128, g.nseq, nst, 4, 128], BF16); Vo = sb("Vo", [128, g.nseq, nst, 4, 128], BF16); r_V = Reg("V")
                gq = sb("gq", [128, 1], F32); gk = sb("gk", [128, 1], F32); r_gv = Reg("gqk")
                NSET = 2
                kraws = [sb("kraw%d" % i, [128, NT], F32) for i in range(NSET)]; r_kraws = [Reg("kraw%d" % i) for i in range(NSET)]
                sqs = [sb("sqa%d" % i, [128, NT], BF16) for i in range(NSET)]; r_sqs = [Reg("sqa%d" % i) for i in range(NSET)]
                rstds = [sb("rstd%d" % i, [128, NT], F32) for i in range(NSET)]; r_rstds = [Reg("rstd%d" % i) for i in range(NSET)]
                knfs = [sb("knf%d" % i, [128, NT], F32) for i in range(NSET)]; r_knfs = [Reg("knf%d" % i) for i in range(NSET)]
                knbs = [sb("knb%d" % i, [128, NT], BF16) for i in range(NSET)]; r_knbs = [Reg("knb%d" % i) for i in range(NSET)]
                t1s = [sb("t1a%d" % i, [128, NT], F32) for i in range(NSET)]; r_t1s = [Reg("t1a%d" % i) for i in range(NSET)]
                qsts = [sb("qst%d" % i, [128, NT], BF16) for i in range(NSET)]; r_qsts = [Reg("qst%d" % i) for i in range(NSET)]
                cosl = sb("cosl", [128, NT], F32); sinl = sb("sinl", [128, NT], F32); r_cs = Reg("cs")
                cosq = sb("cosq", [128, NT], F32); sinq = sb("sinq", [128, NT], F32); r_csq = Reg("csq")
                setc = [0]
                otm = sb("otm", [128, 256], F32); r_otm = Reg("otm")
                vraw = sb("vraw", [128, 2, NT], F32); r_vraw = Reg("vraw")
                MS(Ve[:], 1.0, [r_V]); MS(Vo[:], 1.0, [r_V])
                for h2 in range(2):
                    fw.dma(sp, gq[h2 * 64:(h2 + 1) * 64, :], W['q_norm'][l].rearrange("(d o) -> d o", o=1), writes=[r_gv])
                    fw.dma(sp, gk[h2 * 64:(h2 + 1) * 64, :], W['k_norm'][l].rearrange("(d o) -> d o", o=1), writes=[r_gv])
                TS(gq[:], gq[:], 0.125, None, ALU.mult, None, [r_gv], [r_gv])
                koff = PAST if g.latent else 0
                if g.latent:
                    ckd = sb("ckd", [128, 2, 4, 128], F32); r_ckd = Reg("ckd")
                    for st in range(2):
                        for dup in range(2):
                            fw.dma(sp, ckd[:, st, :, dup * 64:(dup + 1) * 64],
                                   ck_in[l, st * 128:(st + 1) * 128, :].rearrange("s (g d) -> s g d", g=4), writes=[r_ckd])
                    for st in range(2):
                        for hg in range(4):
                            ps, rps = PS()
                            TR(ps[:, 0:128], ckd[:, st, hg, :], ident_f[:], [r_ckd], [rps])
                            CP(K2[:, hg, 0, st * 128:(st + 1) * 128], ps[:, 0:128], [rps], [r_K2])
                        fw.dma(pool, Ve[:, 0, st, :, 0:64], cv_in[l, st * 128:(st + 1) * 128, :].rearrange("s (g d) -> s g d", g=4), writes=[r_V])
                        fw.dma(pool, Vo[:, 0, st, :, 64:128], cv_in[l, st * 128:(st + 1) * 128, :].rearrange("s (g d) -> s g d", g=4), writes=[r_V])
                def qk_norm(k, src, rsrc, gvec, n):
                    TT(sqs[k][:, 0:n], src, src, ALU.mult, rsrc, [r_sqs[k]])
                    ps, rps = PS()
                    MM(ps[:, 0:n], bones[:], sqs[k][:, 0:n], True, True, [r_sqs[k]], [rps])
                    ACT(rstds[k][:, 0:n], ps[:, 0:n], AF.Ln, [rps], [r_rstds[k]], bias=EPS)
                    ACT(rstds[k][:, 0:n], rstds[k][:, 0:n], AF.Exp, [r_rstds[k]], [r_rstds[k]], scale=-0.5)
                    STT(knfs[k][:, 0:n], src, gvec, rstds[k][:, 0:n], ALU.mult, ALU.mult, rsrc + [r_gv, r_rstds[k]], [r_knfs[k]])
                def rope_load(n, tok0, ownt=False, q=False):
                    c_, s_, r_ = (cosq, sinq, r_csq) if q else (cosl, sinl, r_cs)
                    fw.dma(sp, c_[:, 0:n], (ropec_own if ownt else CT["ropec"])[:, tok0:tok0 + n], writes=[r_])
                    fw.dma(sp, s_[:, 0:n], (ropes_own if ownt else CT["ropes"])[:, tok0:tok0 + n], writes=[r_])
                def rope(k, n, outs, q=False):
                    c_, s_, r_ = (cosq, sinq, r_csq) if q else (cosl, sinl, r_cs)
                    CP(knbs[k][:, 0:n], knfs[k][:, 0:n], [r_knfs[k]], [r_knbs[k]])
                    ps, rps = PS()
                    MM(ps[:, 0:n], rmat[:], knbs[k][:, 0:n], True, True, [r_knbs[k]], [rps])
                    TT(t1s[k][:, 0:n], ps[:, 0:n], s_[:, 0:n], ALU.mult, [rps, r_], [r_t1s[k]])
                    TT(knfs[k][:, 0:n], knfs[k][:, 0:n], c_[:, 0:n], ALU.mult, [r_knfs[k], r_], [r_knfs[k]])
                    for (psl, out_ap, rout) in outs:
                        TT(out_ap, knfs[k][psl, 0:n], t1s[k][psl, 0:n], ALU.add, [r_knfs[k], r_t1s[k]], rout)
                if g.latent:
                    for dup in range(2):
                        fw.dma(sp, K2[dup * 64:(dup + 1) * 64, :, 0, koff:koff + L], g.KR.rearrange("b (h d) t -> d (b h) t", h=2), reads=[g.rKR], writes=[r_K2])
                    src_v = g.VT[:, :, :].rearrange("p st (g d) -> p (st g) d", g=4)
                    nst0 = koff // 128
                    fw.dma(sp, Ve[:, 0, nst0:nst0 + L // 128, :, 0:64].rearrange("p st g d -> p (st g) d"), src_v, reads=[g.rVT], writes=[r_V])
                    fw.dma(sp, Vo[:, 0, nst0:nst0 + L // 128, :, 64:128].rearrange("p st g d -> p (st g) d"), src_v, reads=[g.rVT], writes=[r_V])
                for (s, t0, n) in ([] if g.latent else seq_tiles(g, NT)):
                    g0 = s * L + t0
                    for hg in range(4):
                        if g.latent:
                            for dup in range(2):
                                fw.dma(sp, K2[dup * 64:(dup + 1) * 64, hg, s, koff + t0:koff + t0 + n],
                                       g.KR[hg // 2][(hg % 2) * 64:(hg % 2) * 64 + 64, g0:g0 + n], reads=[g.rKR], writes=[r_K2])
                            continue
                        k = setc[0] % NSET; setc[0] += 1
                        kraw = kraws[k]; r_kraw = r_kraws[k]; knf = knfs[k]; r_knf = r_knfs[k]
                        for dup in range(2):
                            fw.dma(sp, kraw[dup * 64:(dup + 1) * 64, 0:n], g.UKV[hg // 2][(hg % 2) * 64:(hg % 2) * 64 + 64, g0:g0 + n],
                                   reads=[g.rUKV[hg // 2]], writes=[r_kraw])
                        qk_norm(k, kraw[:, 0:n], [r_kraw], gk[:, 0:1], n)
                        if g.latent:
                            if hg == 0: rope_load(n, t0)
                            rope(k, n, [(slice(0, 128), K2[:, hg, s, koff + t0:koff + t0 + n], [r_K2])])
                        else:
                            CP(K2[:, hg, s, t0:t0 + n], knf[:, 0:n], [r_knf], [r_K2])
                            for sub in range(n // 128):
                                ps, rps = PS()
                                TR(ps[:, 0:64], knf[0:64, sub * 128:(sub + 1) * 128], ident_f[0:64, 0:64], [r_knf], [rps])
                                CP(otm[:, hg * 64:(hg + 1) * 64], ps[:, 0:64], [rps], [r_otm]) if False else None
                                ACT(otm[:, 0:64], ps[:, 0:64], AF.Copy, [rps], [r_otm])
                                fw.dma(sp, nk_out[s, l, t0 + sub * 128:t0 + (sub + 1) * 128, hg * 64:(hg + 1) * 64], otm[:, 0:64], reads=[r_otm], writes=[r_nk])
                    if g.latent:
                        st0 = (koff + t0) // 128; nsub = n // 128
                        src_v = g.VT[:, g0 // 128:g0 // 128 + nsub, :].rearrange("p st (g d) -> p (st g) d", g=4)
                        fw.dma(sp, Ve[:, s, st0:st0 + nsub, :, 0:64].rearrange("p st g d -> p (st g) d"), src_v, reads=[g.rVT], writes=[r_V])
                        fw.dma(sp, Vo[:, s, st0:st0 + nsub, :, 64:128].rearrange("p st g d -> p (st g) d"), src_v, reads=[g.rVT], writes=[r_V])
                        continue
                    fw.dma(sp, vraw[:, :, 0:n], g.UKV[2:4, :, g0:g0 + n].rearrange("j p w -> p j w"), reads=g.rUKV[2:4], writes=[r_vraw])
                    for sub in range(n // 128):
                        st = (koff + t0) // 128 + sub
                        for vb in range(2):
                            ps, rps = PS()
                            TR(ps[:, 0:128], vraw[:, vb, sub * 128:(sub + 1) * 128], ident_f[:], [r_vraw], [rps])
                            CP(Ve[:, s, st, 2 * vb:2 * vb + 2, 0:64], ps[:, 0:128].rearrange("p (g d) -> p g d", g=2), [rps], [r_V])
                            ACT(Vo[:, s, st, 2 * vb:2 * vb + 2, 64:128], ps[:, 0:128].rearrange("p (g d) -> p g d", g=2), AF.Copy, [rps], [r_V])
                            if not g.latent:
                                ACT(otm[:, 0:128], ps[:, 0:128], AF.Copy, [rps], [r_otm])
                                fw.dma(sp, nv_out[s, l, t0 + sub * 128:t0 + (sub + 1) * 128, vb * 128:(vb + 1) * 128], otm[:, 0:128], reads=[r_otm], writes=[r_nv])
                qraw = [sb("qraw%d" % i, [128, 8, NT], BF16) for i in range(2)]; r_qraw = [Reg("qraw%d" % i) for i in range(2)]
                sdg = [sb("sdg0", [128, 8, NT], BF16)] * 2; r_sdg = [Reg("sdg0")] * 2
                Qz = sb("Qz", [128, 16, NT], BF16); r_Qz = Reg("Qz")
                MS(Qz[:], 0.0, [r_Qz])
                pT2 = [sb("pT2%d" % i, [128, 2, NT], BF16) for i in range(3)]; r_pT2 = [Reg("pT2%d" % i) for i in range(3)]
                pob = [ph.enter_context(nc.psum_tensor(uq("po"), [128, 512], F32)) for i in range(2)]; r_pob = [Reg("po%d" % i) for i in range(2)]
                dn = sb("dn", [128, NT], F32); r_dn = Reg("dn")
                att = sb("att", [128, NT], F32); r_att = Reg("att")
                yd = [sb("yd0", [128, 8, NT], BF16)] * 2; r_yd = [Reg("yd0")] * 2
                qtiles = own_tiles(own) if own else seq_tiles(g, NT)
                def qcols(it):
                    s_, t0_, n_ = qtiles[it]
                    return (t0_ if own else s_ * L + t0_)
                for it, (s_, t0_, n_) in enumerate([] if g.latent else qtiles):
                    cq0 = 128 + t0_ if own else s_ * L + t0_
                    qb_ = it % 2
                    fw.dma(sp, qraw[qb_][:, :, 0:n_], Us[36:44, :, cq0:cq0 + n_].rearrange("j p w -> p j w"), reads=rUs[36:44], writes=[r_qraw[qb_]])
                    if g.latent: rope_load(n_, t0_, own, q=True)
                    for qb in range(8):
                        k = setc[0] % NSET; setc[0] += 1
                        qk_norm(k, qraw[qb_][:, qb, 0:n_], [r_qraw[qb_]], gq[:, 0:1], n_)
                        if g.latent:
                            rope(k, n_, [(slice(0, 128), qsts[k][:, 0:n_], [r_qsts[k]])], q=True)
                        else:
                            CP(qsts[k][:, 0:n_], knfs[k][:, 0:n_], [r_knfs[k]], [r_qsts[k]])
                        fw.dma(sp, g.QR[qb][:, qcols(it):qcols(it) + n_], qsts[k][:, 0:n_], reads=[r_qsts[k]], writes=[g.rQR])
                def load_tile(it):
                    s_, t0_, n_ = qtiles[it]
                    c0 = qcols(it); cq0 = 128 + t0_ if own else s_ * L + t0_
                    for hh in range(2):
                        fw.dma(sp, Qz[hh * 64:(hh + 1) * 64, hh:16:2, 0:n_], g.QR[:, hh * 64:(hh + 1) * 64, c0:c0 + n_].rearrange("q p w -> p q w"),
                               reads=[g.rQR], writes=[r_Qz])
                    fw.dma(sp, sdg[0][:, :, 0:n_], Us[48:56, :, cq0:cq0 + n_].rearrange("j p w -> p j w"), reads=rUs[48:56], writes=[r_sdg[0]])
                for it, (s, t0, n) in enumerate(qtiles):
                    b_ = it % 2; g0 = s * L + t0
                    if own: g0 = t0
                    load_tile(it)
                    r_Qt = [r_Qz] * 8
                    npair = nst // 2
                    items = [(h, pr_) for h in range(16) for pr_ in range(npair)]
                    def emit_qk(idx):
                        h, pr_ = items[idx]
                        qb = h // 2; base = (h % 2) * 64; hg = h // 4
                        sc_ = sc[idx % 3]; rsc_ = r_sc[idx % 3]
                        for k2 in range(2):
                            st = pr_ * 2 + k2
                            MM(sc_[:, k2 * 512:k2 * 512 + n], K2[:, hg, s, st * 128:(st + 1) * 128], Qz[:, h, 0:n],
                               True, True, [r_K2, r_Qt[qb]], [rsc_])
                    def emit_pv(idx):
                        h, pr_ = items[idx]
                        qb = h // 2; base = (h % 2) * 64; hg = h // 4
                        Vt = Ve if base == 0 else Vo
                        sc_ = sc[idx % 3]; rsc_ = r_sc[idx % 3]
                        pi = idx % 3
                        po = pob[h % 2]; rpo = r_pob[h % 2]
                        ACT(pT2[pi][:, :, 0:n], sc_[:].rearrange("p (k w) -> p k w", k=2)[:, :, 0:n], AF.Exp, [rsc_], [r_pT2[pi]])
                        for k2 in range(2):
                            st = pr_ * 2 + k2
                            MM(po[:, 0:n], Vt[:, s, st, hg, :], pT2[pi][:, k2, 0:n], st == 0, st == nst - 1, [r_V, r_pT2[pi]], [rpo])
                        if pr_ == npair - 1:
                            nb_ = slice(base, base + 64); db_ = slice(64 - base, 128 - base)
                            CP(dn[nb_, 0:n], po[db_, 0:n], [rpo], [r_dn])
                            RCP(dn[nb_, 0:n], dn[nb_, 0:n], [r_dn], [r_dn])
                            TT(att[nb_, 0:n], po[nb_, 0:n], dn[nb_, 0:n], ALU.mult, [rpo, r_dn], [r_att])
                            TT(yd[b_][nb_, qb, 0:n], att[nb_, 0:n], sdg[b_][nb_, qb, 0:n], ALU.mult, [r_att, r_sdg[b_]], [r_yd[b_]])
                    for idx in range(len(items) + 2):
                        if idx < len(items): emit_qk(idx)
                        if idx >= 2: emit_pv(idx - 2)
                    fw.dma(sp, g.YIN[12:20, :, g0:g0 + n].rearrange("j p w -> p j w"), yd[b_][:, :, 0:n], reads=[r_yd[b_]], writes=g.rYIN[12:20])
                fw.barrier()

        def sin_reduce(ph_sb, x, rx, n, tag):
            ki, kf, mk, rk = ph_sb
            TS(ki[0:64, 0:n], x, 1.0 / TWO_PI, None, ALU.mult, None, rx, [rk])
            CP(kf[0:64, 0:n], ki[0:64, 0:n], [rk], [rk])
            STT(x, kf[0:64, 0:n], -TWO_PI, x, ALU.mult, ALU.add, [rk] + rx, rx)
            TS(mk[0:64, 0:n], x, math.pi, -TWO_PI, ALU.is_gt, ALU.mult, rx, [rk])
            TT(x, x, mk[0:64, 0:n], ALU.add, rx + [rk], rx)
            TS(mk[0:64, 0:n], x, -math.pi, TWO_PI, ALU.is_lt, ALU.mult, rx, [rk])
            TT(x, x, mk[0:64, 0:n], ALU.add, rx + [rk], rx)

        def hyena_filters(g, l):
            L = g.L; N2 = 2 * L; nm = g.name
            with contextlib.ExitStack() as ph:
                def sb(name, shape, dt): return ph.enter_context(nc.sbuf_tensor(uq(name), list(shape), dt))
                psum_std(ph, nb=4, with_pst=False) if g.latent else psum_std(ph)
                w1 = sb("w1", [33, 64], F32); w2 = sb("w2", [64, 64], F32); r_w = Reg("hw")
                b1 = sb("b1", [64, 1], F32); b2 = sb("b2", [64, 1], F32); fr = sb("fr", [64, 1], F32)
                w3 = sb("w3", [64, 2048], BF16); b3 = sb("b3", [128, 16], F32); ndl = sb("ndl", [128, 4], F32)
                h2 = sb("h2", [64, N2], BF16); r_h2 = Reg("h2")
                mlp_scope = contextlib.ExitStack()
                def sbm(name, shape, dt): return mlp_scope.enter_context(nc.sbuf_tensor(uq(name), list(shape), dt))
                trow = sb("trow", [128, N2], F32); r_trow = Reg("trow")
                decs = [sb("dec%d" % i, [128, 512], F32) for i in range(2)]; r_decs = [Reg("dec%d" % i) for i in range(2)]
                fts = [sb("ft%d" % i, [128, 512], F32) for i in range(2)]; r_fts = [Reg("ft%d" % i) for i in range(2)]
                sqjf = sb("sqjf", [128, 512], F32); r_sqjf = Reg("sqjf")
                fb = sb("fb", [128, N2], BF16); r_fb = Reg("fb")
                ssq = sb("ssq", [128, 16], F32); r_ssq = Reg("ssq"); rs = sb("rs", [128, 1], F32)
                zf = sbm("zf", [33, 512], F32); r_zf = Reg("zf")
                h1 = sbm("h1", [64, 512], F32); r_h1 = Reg("h1")
                ki = sbm("ki", [64, 512], I32); kf = sbm("kf", [64, 512], F32); mk = sbm("mk", [64, 512], F32); rk = Reg("rk")
                fw.dma(sp, w1[:], W['hy_w1'][l], writes=[r_w]); fw.dma(sp, w2[:], W['hy_w2'][l], writes=[r_w])
                for (t_, nm_) in ((b1, 'hy_b1'), (b2, 'hy_b2'), (fr, 'hy_freq')):
                    fw.dma(sp, t_[:], W[nm_][l].rearrange("(d o) -> d o", o=1), writes=[r_w])
                fw.dma(pool, w3[:], W['hy_w3'][l], writes=[r_w])
                colvec(b3[:], W['hy_b3'][l], 16, [r_w])
                fw.dma(sp, ndl[:], CT["negdelta"][:, :], writes=[r_w])
                fw.dma(sp, trow[:], CT["trow_" + nm][0:1, :].partition_broadcast(128), writes=[r_trow])
                for ch in range(N2 // 512):
                    fw.dma(sp, zf[:], CT["zf_" + nm][:, ch * 512:(ch + 1) * 512], writes=[r_zf])
                    ps, rps = PS()
                    MM(ps[0:64, :], w1[:], zf[:], True, True, [r_w, r_zf], [rps])
                    TS(h1[:], ps[0:64, :], b1[:, 0:1], fr[:, 0:1], ALU.add, ALU.mult, [rps, r_w], [r_h1])
                    sin_reduce((ki, kf, mk, rk), h1[:], [r_h1], 512, "a")
                    ACT(h1[:], h1[:], AF.Sin, [r_h1], [r_h1])
                    ps, rps = PS()
                    MM(ps[0:64, :], w2[:], h1[:], True, True, [r_w, r_h1], [rps])
                    TS(h1[:], ps[0:64, :], b2[:, 0:1], fr[:, 0:1], ALU.add, ALU.mult, [rps, r_w], [r_h1])
                    sin_reduce((ki, kf, mk, rk), h1[:], [r_h1], 512, "b")
                    ACT(h2[:, ch * 512:(ch + 1) * 512], h1[:], AF.Sin, [r_h1], [r_h2])
                fw.barrier()
                mlp_scope.close()
                if g.latent:
                    Dms = [sb("Dm%d" % i, [128, 64, 128], BF16) for i in range(2)]; r_Ds = [[Reg("D%d" % i)] for i in range(2)]
                    hcnt = [0]
                    Abuf = sb("Abuf", [128, 64, 3, 64], BF16); r_A = Reg("A")
                    Xb = sb("Xb", [128, 3, 64, 64], BF16); r_X = [Reg("X%d" % i) for i in range(4)]
                    wa_t = sb("wa_t", [128, 192], BF16); r_tab = Reg("tab")
                    tbc = sb("tbc", [128, 64, 128], BF16); tbs = sb("tbs", [128, 64, 128], BF16)
                    pa = [ph.enter_context(nc.psum_tensor(uq("pa"), [128, 1024], F32)) for i in range(2)]; r_pa = [Reg("pa%d" % i) for i in range(2)]
                    fw.dma(sp, wa_t[:], CT["wa"][:, :], writes=[r_tab])
                    fw.dma(sp, tbc[:], CT["tb_c"][:, :, :], writes=[r_tab]); fw.dma(sp, tbs[:], CT["tb_s"][:, :, :], writes=[r_tab])
                else:
                    Zt = sb("Zt", [128, 4, 128], BF16); r_Zt = Reg("Zt")
                    fpc = sb("fpc", [128, 4, 512], BF16); fps = sb("fps", [128, 4, 512], BF16); r_tab = Reg("tab")
                    Xp = sb("Xp", [128, 2, 4, 128], BF16); r_X = Reg("X")
                    fw.dma(sp, fpc[:], CT["fp_c"][:, :, :], writes=[r_tab]); fw.dma(sp, fps[:], CT["fp_sn"][:, :, :], writes=[r_tab])
                for o in range(2):
                    for cb in range(4):
                        ssi = 0
                        for half in range(2):
                            col0 = half * 1024 + o * 512 + cb * 128
                            bcol = col0 // 128
                            for ch in range(L // 512 if L >= 512 else 1):
                                n = min(512, L); p0 = half * L + ch * 512
                                kq = ssi % 2
                                dec = decs[kq]; r_dec = r_decs[kq]; ft = fts[kq]; r_ft = r_fts[kq]
                                ps, rps = PS()
                                MM(ps[:, 0:n], w3[:, col0:col0 + 128], h2[:, p0:p0 + n], True, True, [r_w, r_h2], [rps])
                                ACT(dec[:, 0:n], trow[:, p0:p0 + n], AF.Exp, [r_trow, r_w], [r_dec], scale=ndl[:, cb:cb + 1])
                                STT(ft[:, 0:n], ps[:, 0:n], b3[:, bcol:bcol + 1], dec[:, 0:n], ALU.add, ALU.mult, [rps, r_w, r_dec], [r_ft])
                                CP(fb[:, p0:p0 + n], ft[:, 0:n], [r_ft], [r_fb])
                                ACT(sqjf[:, 0:n], ft[:, 0:n], AF.Square, [r_ft], [r_sqjf, r_ssq], accum_out=ssq[:, ssi:ssi + 1]); ssi += 1
                        fw.op(dve, lambda e: e.reduce_sum(out=rs[:], in_=ssq[:, 0:ssi], axis=mybir.AxisListType.X), [r_ssq], [r_ssq])
                        ACT(rs[:], rs[:], AF.Sqrt, [r_ssq], [r_ssq], bias=EPS)
                        RCP(rs[:], rs[:], [r_ssq], [r_ssq])
                        TS(fb[:], fb[:], rs[:, 0:1], None, ALU.mult, None, [r_fb, r_ssq], [r_fb])
                        if g.latent:
                            fw.dma(sp, g.ZB[:, :], fb[:], reads=[r_fb], writes=[g.rZB])
                            for half in range(2):
                                Dm = Dms[(hcnt[0] + half) % 2]; r_D = r_Ds[(hcnt[0] + half) % 2]
                                fw.dma(sp, Dm[0:64], g.ZB[half * 64:(half + 1) * 64, :].rearrange("c (j a) -> j c a", a=128), reads=[g.rZB], writes=r_D)
                            for half in range(2):
                                Dm = Dms[(hcnt[0] + half) % 2]; r_D = r_Ds[(hcnt[0] + half) % 2]
                                fft_fwd_sample(Dm, r_D, 64, Abuf, r_A, Xb, r_X, wa_t, r_tab, tbc, tbs, pa, r_pa)
                                for ri in range(2):
                                    fw.dma(sp, g.HS[o, cb, half, ri], Xb[:, 1 + ri].rearrange("p g c -> p (g c)"), reads=r_X, writes=[g.rHS])
                        else:
                            for tt in range(4):
                                TR(PSX.pst[:, tt * 128:(tt + 1) * 128], fb[:, tt * 128:(tt + 1) * 128], ident_b[:], [r_fb], [PSX.rpst])
                            ACT(Zt[:], PSX.pst[:, 0:512].rearrange("p (t c) -> p t c", t=4), AF.Copy, [PSX.rpst], [r_Zt])
                            for ri, tab in ((0, fpc), (1, fps)):
                                ps, rps = PS()
                                for kt in range(4):
                                    for tt in range(4):
                                        MM(ps[:, kt * 128:(kt + 1) * 128], tab[:, tt, kt * 128:(kt + 1) * 128], Zt[:, tt, :], tt == 0, tt == 3, [r_tab, r_Zt], [rps])
                                ACT(Xp[:, ri], ps[:].rearrange("p (k c) -> p k c", k=4), AF.Copy, [rps], [r_X])
                                fw.dma(sp, g.HS[o, cb, ri], Xp[:, ri].rearrange("p k c -> p (k c)"), reads=[r_X], writes=[g.rHS])
                fw.barrier()

        def fft_fwd_sample(Dm, r_D, J, Abuf, r_A, Xb, r_X, wa_t, r_tab, tbc, tbs, pa, r_pa):
            for i4, c0 in enumerate(range(0, 64, 4)):
                pa_ = pa[i4 % 2]; rpa_ = r_pa[i4 % 2]
                for ci in range(4):
                    o_ = (ci // 2) * 512 + (ci % 2) * 192
                    MM(pa_[:, o_:o_ + 192], Dm[0:J, c0 + ci, :], wa_t[0:J, :], True, True, r_D + [r_tab], [rpa_])
                src_ = pa_[:].rearrange("p (b x) -> p b x", b=2)[:, :, 0:384]
                dst_ = Abuf[:, c0:c0 + 4].rearrange("p (b c) r g -> p b (c r g)", b=2)
                if i4 % 2:
                    CP(dst_, src_, [rpa_], [r_A])
                else:
                    ACT(dst_, src_, AF.Copy, [rpa_], [r_A])
            for g4 in range(16):
                pq, rpq = PS()
                for gi in range(4):
                    gg = g4 * 4 + gi
                    MM(pq[:, gi * 128:(gi + 1) * 128], tbc[:, gg, :], Abuf[:, :, 0:2, gg].rearrange("p c r -> p r c"), True, False, [r_tab, r_A], [rpq])
                    MM(pq[:, gi * 128:(gi + 1) * 128], tbs[:, gg, :], Abuf[:, :, 1:3, gg].rearrange("p c r -> p r c"), False, True, [r_tab, r_A], [rpq])
                pv4 = pq[:].rearrange("p (g r c) -> p g r c", g=4, r=2)
                ACT(Xb[:, 1, g4 * 4:(g4 + 1) * 4, :], pv4[:, :, 0, :], AF.Copy, [rpq], [r_X[g4 // 4]] + (r_D if g4 == 0 else []))
                CP(Xb[:, 2, g4 * 4:(g4 + 1) * 4, :], pv4[:, :, 1, :], [rpq], [r_X[g4 // 4]] + (r_D if g4 == 0 else []))

        def hyena(g, l, own=False):
            L = g.L; T = g.T
            with contextlib.ExitStack() as ph:
                def sb(name, shape, dt): return ph.enter_context(nc.sbuf_tensor(uq(name), list(shape), dt))
                psum_std(ph, nb=4, with_pst=False) if g.latent else psum_std(ph)
                shw = sb("shw", [128, 12, 3], F32); shb = sb("shb", [128, 12], F32); skp = sb("skp", [128, 2, 4], F32); r_hv = Reg("hv")
                for k in range(3):
                    fw.dma(sp, shw[:, :, k], W['hy_short_w'][l][k, :].rearrange("(b p) -> p b", p=128), writes=[r_hv], allow_slow_non_contiguous=True)
                colvec(shb[:], W['hy_short_b'][l], 12, [r_hv])
                for o in range(2):
                    fw.dma(sp, skp[:, o, :], W['hy_skip'][l][o, :].rearrange("(b p) -> p b", p=128), writes=[r_hv], allow_slow_non_contiguous=True)
                raw = sb("raw", [128, g.nseq, L + 2], BF16); r_raw = Reg("raw")
                bufA = sb("bufA", [128, T], F32); bufB = sb("bufB", [128, T], F32); r_bA = Reg("bufA"); r_bB = Reg("bufB")
                xg = sb("xg", [128, T], F32); r_xg = Reg("xg")
                zb = sb("zbh", [128, T], BF16); r_zb = Reg("zbh")
                sbg = raw[:, 0, 0:T] if g.latent else sb("sbg", [128, T], BF16); r_sbg = r_raw if g.latent else Reg("sbg")
                yo = zb; r_yo = r_zb
                if g.latent:
                    DA = sb("DA", [128, 20480], BF16)
                    r_Dlo = Reg("Dlo"); r_Dhi = Reg("Dhi"); r_D = [r_Dlo, r_Dhi]
                    Abuf = DA[:, 8192:20480].rearrange("p (c r g) -> p c r g", r=3, g=64); r_A = Reg("A")
                    Pb = DA[:, 0:8192].rearrange("p (c n) -> p c n", n=128); r_Plo = Reg("Plo"); r_Phi = Reg("Phi"); r_P = [r_Plo, r_Phi]
                    Xb = sb("Xb", [128, 3, 64, 64], BF16); r_X = [Reg("X%d" % i) for i in range(4)]
                    Dm = Xb[:].rearrange("p r g c -> p (r g c)")[:, 0:8192].rearrange("p (c a) -> p c a", a=128)
                    Hb = [sb("Hb%d" % i, [128, 2, 64 * 64], BF16) for i in range(1)]; r_H = Reg("H")
                    QC = 1024
                    tq = [sb("tq%d" % i, [128, QC], BF16) for i in range(4)]; r_tq = [Reg("tq%d" % i) for i in range(4)]
                    wa_t = sb("wa_t", [128, 192], BF16); r_tab = Reg("tab")
                    tbc = sb("tbc", [128, 64, 128], BF16); tbs = sb("tbs", [128, 64, 128], BF16)
                    pa = [ph.enter_context(nc.psum_tensor(uq("pa"), [128, 1024], F32)) for i in range(2)]; r_pa = [Reg("pa%d" % i) for i in range(2)]
                    fw.dma(sp, tbc[:], CT["tb_c"][:, :, :], writes=[r_tab]); fw.dma(sp, tbs[:], CT["tb_s"][:, :, :], writes=[r_tab])
                    tpc = sb("tpc", [128, 128], BF16); tps = sb("tps", [128, 128], BF16)
                    tcst = sb("tcst", [128, 128, 32], BF16)
                    fw.dma(sp, wa_t[:], CT["wa"][:, :], writes=[r_tab]); fw.dma(sp, tpc[:], CT["tbp_c"][:, :], writes=[r_tab])
                    fw.dma(sp, tps[:], CT["tbp_s"][:, :], writes=[r_tab]); fw.dma(sp, tcst[:], CT["tc_st"][:, :, :], writes=[r_tab])
                else:
                    Zt = sb("Zt", [128, 2, 2, 128], BF16); r_Zt = Reg("Zt")
                    fpc = sb("fpc", [128, 4, 512], BF16); fps = sb("fps", [128, 4, 512], BF16); r_tab = Reg("tab")
                    fic = sb("fic", [128, 4, 256], BF16); fis = sb("fis", [128, 4, 256], BF16)
                    Xp = sb("Xp", [128, 2, 4, 2, 128], BF16); r_X = Reg("X")
                    Hp = sb("Hp", [128, 2, 4, 128], BF16); r_H = Reg("H")
                    tq = [sb("tq%d" % i, [128, 4, 2, 128], BF16) for i in range(4)]; r_tq = [Reg("tq%d" % i) for i in range(4)]
                    for t_, nm_ in ((fpc, "fp_c"), (fps, "fp_sn"), (fic, "fi_c"), (fis, "fi_sn")):
                        fw.dma(sp, t_[:], CT[nm_][:, :, :], writes=[r_tab])

                def short_conv(blk, dst, rdst):
                    MS(raw[:, :, 0:1], 0.0, [r_raw]); MS(raw[:, :, L + 1:L + 2], 0.0, [r_raw])
                    fw.dma(sp, raw[:, :, 1:L + 1], g.U[12 + blk].rearrange("p (s t) -> p s t", s=g.nseq), reads=[g.rU[12 + blk]], writes=[r_raw])
                    d3 = dst.rearrange("p (s t) -> p s t", s=g.nseq)
                    TS(d3, raw[:, :, 0:L], shw[:, blk, 0:1], shb[:, blk:blk + 1], ALU.mult, ALU.add, [r_raw, r_hv], rdst)
                    STT(d3, raw[:, :, 1:L + 1], shw[:, blk, 1:2], d3, ALU.mult, ALU.add, [r_raw, r_hv] + rdst, rdst)
                    STT(d3, raw[:, :, 2:L + 2], shw[:, blk, 2:3], d3, ALU.mult, ALU.add, [r_raw, r_hv] + rdst, rdst)

                def longconv_sample(src, rsrc, dst, rdst, o, cb):
                    CP(zb[:], src, rsrc, [r_zb])
                    fw.dma(sp, g.ZB[:, 0:T], zb[:], reads=[r_zb], writes=[g.rZB])
                    for half in range(2):
                        pass
                        fw.dma(sp, Dm[0:32], g.ZB[half * 64:(half + 1) * 64, 0:T].rearrange("c (j a) -> j c a", a=128), reads=[g.rZB], writes=[r_Dlo, r_Dhi, r_A] + r_X)
                        fw.dma(sp, Hb[0][:], g.HS[o, cb, half].rearrange("r p x -> p r x"), reads=[g.rHS], writes=[r_H])
                        fft_fwd_sample(Dm, r_D, 32, Abuf, r_A, Xb, r_X, wa_t, r_tab, tbc, tbs, pa, r_pa)
                        for q in range(4096 // QC):
                            sl = slice(q * QC, (q + 1) * QC)
                            xr = Xb[:, 1].rearrange("p g c -> p (g c)")[:, sl]; xi = Xb[:, 2].rearrange("p g c -> p (g c)")[:, sl]
                            xn = Xb[:, 0].rearrange("p g c -> p (g c)")[:, sl]
                            hr = Hb[0][:, 0, sl]; hi = Hb[0][:, 1, sl]
                            rxq = r_X[q * QC // 1024]
                            TT(tq[0][:], xr, hr, ALU.mult, [rxq, r_H], [r_tq[0]])
                            TT(tq[1][:], xi, hi, ALU.mult, [rxq, r_H], [r_tq[1]])
                            TT(tq[2][:], xr, hi, ALU.mult, [rxq, r_H], [r_tq[2]])
                            TT(tq[3][:], xi, hr, ALU.mult, [rxq, r_H], [r_tq[3]])
                            TT(xr, tq[0][:], tq[1][:], ALU.subtract, [r_tq[0], r_tq[1]], [rxq])
                            TT(xi, tq[2][:], tq[3][:], ALU.add, [r_tq[2], r_tq[3]], [rxq])
                            STT(xn, tq[2][:], -1.0, tq[3][:], ALU.mult, ALU.subtract, [r_tq[2], r_tq[3]], [rxq])
                        for i4, c0 in enumerate(range(0, 64, 4)):
                            ps, rps = PS()
                            for ci in range(4):
                                MM(ps[:, ci * 128:(ci + 1) * 128], Xb[:, 1:3, :, c0 + ci], tpc[:], True, False, r_X + [r_tab], [rps])
                                MM(ps[:, ci * 128:(ci + 1) * 128], Xb[:, 0:2, :, c0 + ci], tps[:], False, True, r_X + [r_tab], [rps])
                            if i4 % 2:
                                CP(Pb[:, c0:c0 + 4, :], ps[:].rearrange("p (c n) -> p c n", c=4), [rps], r_P)
                            else:
                                ACT(Pb[:, c0:c0 + 4, :], ps[:].rearrange("p (c n) -> p c n", c=4), AF.Copy, [rps], r_P)
                        for n16 in range(8):
                            ps, rps = PS()
                            for bi in range(16):
                                nb = n16 * 16 + bi
                                pv_ = ps[0:64, :].rearrange("p (j b) -> p j b", b=16)[:, :, bi]
                                MM(pv_, Pb[:, :, nb], tcst[:, nb, :], True, True, r_P + [r_tab], [rps])
                            dv = dst[half * 64:(half + 1) * 64, :].rearrange("p (j a) -> p j a", a=128)[:, :, n16 * 16:(n16 + 1) * 16]
                            if n16 % 2:
                                CP(dv, ps[0:64, :].rearrange("p (j b) -> p j b", b=16), [rps], rdst)
                            else:
                                ACT(dv, ps[0:64, :].rearrange("p (j b) -> p j b", b=16), AF.Copy, [rps], rdst)

                def longconv_prompt(src, rsrc, dst, rdst, o, cb):
                    CP(zb[:], src, rsrc, [r_zb])
                    fw.dma(sp, Hp[:], g.HS[o, cb].rearrange("r p (k c) -> p r k c", k=4), reads=[g.rHS], writes=[r_H])
                    for s in range(2):
                        for tt in range(2):
                            TR(PSX.pst[:, (s * 2 + tt) * 128:(s * 2 + tt + 1) * 128], zb[:, s * L + tt * 128:s * L + (tt + 1) * 128], ident_b[:], [r_zb], [PSX.rpst])
                    ACT(Zt[:].rearrange("p t s c -> p s t c"), PSX.pst[:, 0:512].rearrange("p (s t c) -> p s t c", s=2, t=2), AF.Copy, [PSX.rpst], [r_Zt])
                    for ri, tab in ((0, fpc), (1, fps)):
                        for k2 in range(2):
                            ps, rps = PS()
                            for kk in range(2):
                                kt = k2 * 2 + kk
                                for tt in range(2):
                                    MM(ps[:, kk * 256:(kk + 1) * 256], tab[:, tt, kt * 128:(kt + 1) * 128], Zt[:, tt].rearrange("p s c -> p (s c)"),
                                       tt == 0, tt == 1, [r_tab, r_Zt], [rps])
                            ACT(Xp[:, ri, k2 * 2:k2 * 2 + 2].rearrange("p k s c -> p (k s c)"), ps[:], AF.Copy, [rps], [r_X])
                    for s in range(2):
                        xr = Xp[:, 0, :, s, :]; xi = Xp[:, 1, :, s, :]; hr = Hp[:, 0]; hi = Hp[:, 1]
                        TT(tq[0][:, :, s, :], xr, hr, ALU.mult, [r_X, r_H], [r_tq[0]])
                        TT(tq[1][:, :, s, :], xi, hi, ALU.mult, [r_X, r_H], [r_tq[1]])
                        TT(tq[2][:, :, s, :], xr, hi, ALU.mult, [r_X, r_H], [r_tq[2]])
                        TT(tq[3][:, :, s, :], xi, hr, ALU.mult, [r_X, r_H], [r_tq[3]])
                    TT(Xp[:, 0], tq[0][:], tq[1][:], ALU.subtract, [r_tq[0], r_tq[1]], [r_X])
                    TT(Xp[:, 1], tq[2][:], tq[3][:], ALU.add, [r_tq[2], r_tq[3]], [r_X])
                    for s in range(2):
                        ps, rps = PS()
                        for kt in range(4):
                            MM(ps[:, 0:256], Xp[:, 0, kt, s, :], fic[:, kt, :], kt == 0, False, [r_X, r_tab], [rps])
                            MM(ps[:, 0:256], Xp[:, 1, kt, s, :], fis[:, kt, :], False, kt == 3, [r_X, r_tab], [rps])
                        ACT(dst[:, s * L:(s + 1) * L], ps[:, 0:256], AF.Copy, [rps], rdst)

                longconv = longconv_sample if g.latent else longconv_prompt
                for cb in range(4):
                    short_conv(cb, bufA[:], [r_bA])
                    short_conv(4 + cb, xg[:], [r_xg])
                    longconv(bufA[:], [r_bA], bufB[:], [r_bB], 0, cb)
                    STT(bufB[:], bufA[:], skp[:, 0, cb:cb + 1], bufB[:], ALU.mult, ALU.add, [r_bA, r_bB, r_hv], [r_bB])
                    TT(bufB[:], bufB[:], xg[:], ALU.mult, [r_bB, r_xg], [r_bB])
                    short_conv(8 + cb, xg[:], [r_xg])
                    longconv(bufB[:], [r_bB], bufA[:], [r_bA], 1, cb)
                    STT(bufA[:], bufB[:], skp[:, 1, cb:cb + 1], bufA[:], ALU.mult, ALU.add, [r_bA, r_bB, r_hv], [r_bA])
                    TT(bufA[:], bufA[:], xg[:], ALU.mult, [r_bA, r_xg], [r_bA])
                    fw.dma(sp, sbg[:], g.U[24 + cb], reads=[g.rU[24 + cb]], writes=[r_sbg])
                    TT(yo[:], bufA[:], sbg[:], ALU.mult, [r_bA, r_sbg], [r_yo])
                    if own:
                        fw.dma(sp, g.YB[cb].rearrange("(q p) w -> p q w", p=128), yo[:].rearrange("p (q w) -> p q w", q=4), reads=[r_yo], writes=[g.rYB])
                    else:
                        fw.dma(sp, g.YIN[4 + cb], yo[:], reads=[r_yo], writes=[g.rYIN[4 + cb]])
                fw.barrier()

        def tail(g, l, x_src, r_xsrc, x_dst, r_xdst, own=False):
            T = 1024 if own else g.T
            with contextlib.ExitStack() as ph:
                def sb(name, shape, dt): return ph.enter_context(nc.sbuf_tensor(uq(name), list(shape), dt))
                psum_std(ph)
                wo = sb("wo", [128, 20, D], BF16); r_wo = Reg("wo")
                wout = sb("wout", [128, 8, D], BF16); r_wout = Reg("wout")
                gate = sb("gate", [128, D], F32); r_gate = Reg("gate")
                badab = sb("badab", [128, D], F32); r_bada = Reg("bada")
                ccol = sb("ccol", [128, 8], F32); r_cc = Reg("cc")
                crep = sb("crep", [128, 8, 128], BF16); r_crep = Reg("crep")
                wsl = [sb("wsl%d" % i, [128, 8, 512], BF16) for i in range(2)]; r_wsl = [Reg("wsl%d" % i) for i in range(2)]
                yin = [sb("yin%d" % i, [128, 20, 512], BF16) for i in range(2)]; r_yin = [Reg("yin%d" % i) for i in range(2)]
                gmf = [sb("gmf%d" % i, [128, 4, 512], BF16) for i in range(3)]; r_gmf = [Reg("gmf%d" % i) for i in range(3)]
                mrg = sb("mrg", [128, 8, 512], F32); r_mrg = [Reg("mrg%d" % i) for i in range(8)]
                mrb = sb("mrb", [128, 8, 512], BF16); r_mrb = [Reg("mrb%d" % i) for i in range(8)]
                tmp = sb("tmpt", [128, 512], F32); r_tmp = Reg("tmpt")
                xt = [sb("xtt%d" % i, [128, D], F32) for i in range(2)]; r_xt = [Reg("xtt%d" % i) for i in range(2)]
                xo = [sb("xo%d" % i, [128, D], F32) for i in range(2)]; r_xo = [Reg("xo%d" % i) for i in range(2)]
                for bi, nm_ in ((0, 'wo_conv'), (4, 'wo_hyena'), (8, 'wo_pool')):
                    fw.dma(pool, wo[:, bi:bi + 4, :], W[nm_][l].rearrange("(kt p) n -> p kt n", p=128), writes=[r_wo])
                fw.dma(pool, wo[:, 12:20, :], W['wo_attn'][l].rearrange("(kt p) n -> p kt n", p=128), writes=[r_wo])
                fw.dma(pool, wout[:], W['w_out'][l].rearrange("(kt p) n -> p kt n", p=128), writes=[r_wout])
                fw.dma(sp, ccol[:], cvec[g.crow, :].rearrange("(b p) -> p b", p=128), writes=[r_cc], allow_slow_non_contiguous=True)
                fw.dma(sp, badab[:], W['b_ada'][l:l + 1, 2 * D:3 * D].partition_broadcast(128), writes=[r_bada])
                ACT(ccol[:], ccol[:], AF.Silu, [r_cc], [r_cc])
                CP(crep[:], ccol[:].unsqueeze(2).broadcast_to([128, 8, 128]), [r_cc], [r_crep])
                wada = W['w_ada'][l].rearrange("(kt p) n -> p kt n", p=128)
                for cb in range(2):
                    fw.dma(pool, wsl[cb][:], wada[:, :, 2 * D + cb * 512:2 * D + (cb + 1) * 512], writes=[r_wsl[cb]])
                    ps, rps = PS()
                    for kt in range(8):
                        MM(ps[:], crep[:, kt, :], wsl[cb][:, kt, :], kt == 0, kt == 7, [r_crep, r_wsl[cb]], [rps])
                    TT(gate[:, cb * 512:(cb + 1) * 512], ps[:], badab[:, cb * 512:(cb + 1) * 512], ALU.add, [rps, r_bada], [r_gate])
                kts = (4, 4, 4, 8); kb0 = (0, 4, 8, 12)
                if own:
                    ybo = sb("ybo", [128, 4, 1024], BF16); r_ybo = Reg("ybo")
                    for cb in range(4):
                        fw.idma(pool, ybo[:, cb, :], g.YB[cb][:, :], idxQ[:, 0:1], reads=[g.rYB], writes=[r_ybo])
                for tt in range(T // 512):
                    b_ = tt % 2; t0 = tt * 512
                    fw.dma(sp, yin[b_][:], g.YIN[:, :, t0:t0 + 512].rearrange("j p w -> p j w"), reads=g.rYIN, writes=[r_yin[b_]])
                    if own:
                        CP(yin[b_][:, 4:8, :], ybo[:, :, t0:t0 + 512], [r_ybo], [r_yin[b_]])
                    for f in range(8):
                        gi_ = (tt * 8 + f) % 3
                        if own:
                            fw.dma(sp, gmf[gi_][:], g.UE[56 + f:88:8, :, 128 + t0:128 + t0 + 512].rearrange("j p w -> p j w"), reads=g.rUE[56 + f:88:8], writes=[r_gmf[gi_]])
                        else:
                            fw.dma(sp, gmf[gi_][:], g.U[56 + f:88:8, :, t0:t0 + 512].rearrange("j p w -> p j w"), reads=g.rU[56 + f:88:8], writes=[r_gmf[gi_]])
                        for br in range(4):
                            ps, rps = PS()
                            for kt in range(kts[br]):
                                MM(ps[:], wo[:, kb0[br] + kt, f * 128:(f + 1) * 128], yin[b_][:, kb0[br] + kt, :], kt == 0, kt == kts[br] - 1,
                                   [r_wo, r_yin[b_]], [rps])
                            if br == 0:
                                TT(mrg[:, f, :], ps[:], gmf[gi_][:, br, :], ALU.mult, [rps, r_gmf[gi_]], [r_mrg[f]])
                            else:
                                TT(tmp[:], ps[:], gmf[gi_][:, br, :], ALU.mult, [rps, r_gmf[gi_]], [r_tmp])
                                TT(mrg[:, f, :], mrg[:, f, :], tmp[:], ALU.add, [r_mrg[f], r_tmp], [r_mrg[f]])
                        ACT(mrb[:, f, :], mrg[:, f, :], AF.Copy, [r_mrg[f]], [r_mrb[f]])
                    for sub in range(4):
                        xb_ = (tt * 4 + sub) % 2; tok0 = t0 + sub * 128
                        if own:
                            fw.idma(pool, xt[xb_][:], x_src[:, :], idxE[:, 1 + tt * 4 + sub:2 + tt * 4 + sub], reads=[r_xsrc], writes=[r_xt[xb_]])
                        else:
                            fw.dma(sp, xt[xb_][:], x_src[tok0:tok0 + 128, :], reads=[r_xsrc], writes=[r_xt[xb_]])
                        for nchk in range(2):
                            ps, rps = PS()
                            for kt in range(8):
                                MM(ps[:], mrb[:, kt, sub * 128:(sub + 1) * 128], wout[:, kt, nchk * 512:(nchk + 1) * 512], kt == 0, kt == 7,
                                   [r_mrb[kt], r_wout], [rps])
                            TT(tmp[:], ps[:], gate[:, nchk * 512:(nchk + 1) * 512], ALU.mult, [rps, r_gate], [r_tmp])
                            TT(xo[xb_][:, nchk * 512:(nchk + 1) * 512], tmp[:], xt[xb_][:, nchk * 512:(nchk + 1) * 512], ALU.add, [r_tmp, r_xt[xb_]], [r_xo[xb_]])
                        fw.dma(sp, x_dst[tok0:tok0 + 128, :], xo[xb_][:], reads=[r_xo[xb_]], writes=[r_xdst])
                fw.barrier()

        r_in = Reg("xin")
        for g in groups:
            for l in range(cfg["layers"]):
                last = (l == cfg["layers"] - 1)
                x_src = g.x_in if l == 0 else g.x_mid
                r_src = r_in if l == 0 else g.r_xmid
                x_dst = g.x_out if last else g.x_mid
                r_dst = g.r_xout if last else g.r_xmid
                own = g.latent and last
                if own:
                    dense_in(g, l, x_src, r_src, wbs=(3, 4, 5, 6, 11))
                    dense_in(g, l, x_src, r_src, wbs=tuple(w for w in range(22) if w not in (3, 4, 5, 6, 11)), ext=True)
                else:
                    dense_in(g, l, x_src, r_src)
                conformer(g, l, own)
                pooling(g, l, own)
                attention(g, l, own)
                hyena_filters(g, l)
                hyena(g, l, own)
                tail(g, l, x_src, r_src, x_dst, r_dst, own)
        fw.barrier()
        nops = fw.nops
    return nc, nops

_PROG = {}
def _cfg():
    return dict(prompt=os.environ.get("K_PROMPT", "1") == "1", sample=os.environ.get("K_SAMPLE", "1") == "1",
                layers=int(os.environ.get("K_LAYERS", "2")))

def kernel(**inputs):
    cfg = _cfg()
    key = tuple(sorted(cfg.items()))
    if key not in _PROG:
        _PROG[key] = build_program(cfg)
    nc, nops = _PROG[key]
    C = host_consts()
    f32 = lambda a: np.ascontiguousarray(np.asarray(a, dtype=np.float32))
    xp = f32(inputs['x_prompt']); xs = f32(inputs['x_sample'])
    ck = f32(inputs['cache_k']); cv = f32(inputs['cache_v']); c = f32(inputs['c']); cctx = f32(inputs['c_ctx'])
    base = {k: f32(inputs[k]) for k in WEIGHT_SHAPES}
    for k, v in C.items(): base["c_" + k] = v
    in_maps = []
    for core in range(8):
        b = core // 4
        m = dict(base)
        m["xp"] = xp[2 * core:2 * core + 2].reshape(2 * LP, D)
        m["xs"] = xs[b]
        m["ck"] = ck[b].reshape(DEPTH, PAST, 256); m["cv"] = cv[b].reshape(DEPTH, PAST, 256)
        m["cvec"] = np.stack([cctx, c[b]], axis=0)
        q = core % 4; off = q * 1024
        ie = (off - 128 + np.arange(1280)).reshape(10, 128).T
        m["idxE"] = np.ascontiguousarray(np.clip(ie, 0, LS - 1).astype(np.int32))
        m["idxQ"] = (q * 128 + np.arange(128)).astype(np.int32).reshape(128, 1)
        m["maskLR"] = np.tile(np.array([[0.0 if q == 0 else 1.0, 0.0 if q == 3 else 1.0]], np.float32), (128, 1))
        m["ropec_own"] = np.ascontiguousarray(C["ropec"][:, off:off + 1024]); m["ropes_own"] = np.ascontiguousarray(C["ropes"][:, off:off + 1024])
        m["invcnt_own"] = np.ascontiguousarray(C["invcnt_s"][:, off:off + 1024])
        in_maps.append(m)
    res = run_bass_kernel_spmd(nc, in_maps, core_ids=list(range(8)))
    R = res.results
    y_prompt = np.concatenate([R[i]["yp"].reshape(2, LP, D) for i in range(8)], axis=0).astype(np.float32)
    y_sample = np.stack([np.concatenate([R[4 * b + q]["ys"] for q in range(4)], axis=0) for b in range(2)], axis=0).astype(np.float32)
    nk = np.concatenate([R[i]["nk"].reshape(2, DEPTH, LP, 4, 64) for i in range(8)], axis=0).astype(np.float32)
    nv = np.concatenate([R[i]["nv"].reshape(2, DEPTH, LP, 4, 64) for i in range(8)], axis=0).astype(np.float32)
    return (y_prompt, y_sample, nk, nv)
```
